# Optimizing a Trainium2 kernel written in Bass

```python
import math
import jax, jax.numpy as jnp
from jax import lax
import numpy as np

D_MODEL = 2048
BATCH = 8
SEQ = 2048
DEPTH = 2

N_SUBLAYERS = 3
FFN_DIM = 5632
FFN_RES_WEIGHT = 0.5
MIXER_RES_WEIGHT = 1.0
NORM_EPS = 1e-6

SSD_HEADS = 32
SSD_HEAD_DIM = 64
SSD_WIDTH = SSD_HEADS * SSD_HEAD_DIM
SSD_GROUPS = 4
SSD_STATE = 128
SSD_CONV = 4
SSD_CHUNK = 128
SSD_BC_WIDTH = SSD_GROUPS * SSD_STATE
SSD_CONV_CH = SSD_WIDTH + 2 * SSD_BC_WIDTH

ATT_HEADS = 16
ATT_HEAD_DIM = 128
ATT_WIDTH = ATT_HEADS * ATT_HEAD_DIM
DILATED_PATTERNS = ((128, 1), (512, 4), (2048, 16))
ATT_BLOCK = 128
ROPE_THETA = 10000.0

HYB_SPLIT_SIZES = (SSD_WIDTH, SSD_CONV_CH, SSD_HEADS, ATT_WIDTH, ATT_WIDTH, ATT_WIDTH)
HYB_IN_WIDTH = SSD_WIDTH + SSD_CONV_CH + SSD_HEADS + 3 * ATT_WIDTH
HYB_OUT_WIDTH = SSD_WIDTH + ATT_WIDTH

SGU_WIDTH = 4096
SGU_GROUPS = 8
SGU_CHUNK = 128

N_HYB_LAYERS = (DEPTH + 1) // 2
N_SGU_LAYERS = DEPTH // 2

kernel_name = "hybrid_ssd_dilated_attn_sgu_macaron_trunk"


def rms_norm(x, g):
    x32 = x.astype(jnp.float32)
    y = x32 * lax.rsqrt(jnp.mean(x32 * x32, axis=-1, keepdims=True) + NORM_EPS)
    return (y * g.astype(jnp.float32)).astype(x.dtype)


def layer_norm(x, g, b):
    x32 = x.astype(jnp.float32)
    mu = jnp.mean(x32, axis=-1, keepdims=True)
    var = jnp.mean(jnp.square(x32 - mu), axis=-1, keepdims=True)
    y = (x32 - mu) * lax.rsqrt(var + NORM_EPS)
    return (y * g.astype(jnp.float32) + b.astype(jnp.float32)).astype(x.dtype)


def swiglu_ffn(h, w_gate, w_up, w_down):
    return (jax.nn.silu(h @ w_gate) * (h @ w_up)) @ w_down


def modulated_sublayer(x, mod, g_pre, g_post, res_weight, fn):
    shift, scale, gate = mod[:, 0, None, :], mod[:, 1, None, :], mod[:, 2, None, :]
    h = rms_norm(x, g_pre) * (1 + scale) + shift
    return x + res_weight * (1 + gate) * rms_norm(fn(h), g_post)


def apply_rope(t, positions):
    half = t.shape[-1] // 2
    inv_freq = ROPE_THETA ** (-jnp.arange(half, dtype=jnp.float32) / half)
    ang = positions.astype(jnp.float32)[..., None] * inv_freq
    cos = jnp.cos(ang)[:, :, None, :]
    sin = jnp.sin(ang)[:, :, None, :]
    t32 = t.astype(jnp.float32)
    t1, t2 = t32[..., :half], t32[..., half:]
    return jnp.concatenate([t1 * cos - t2 * sin, t2 * cos + t1 * sin], axis=-1).astype(t.dtype)


def causal_depthwise_conv(x, w, b):
    k = w.shape[0]
    out = lax.conv_general_dilated(
        x, w[:, None, :], window_strides=(1,), padding=((k - 1, 0),),
        dimension_numbers=("NWC", "WIO", "NWC"), feature_group_count=x.shape[-1])
    return out + b


def ssd_chunked_scan(x, dt, a, bmat, cmat):
    b, s, h, p = x.shape
    g, n = bmat.shape[2], bmat.shape[3]
    k = h // g
    cl = min(SSD_CHUNK, s)
    nc = s // cl
    xc = (x * dt[..., None]).reshape(b, nc, cl, g, k, p)
    adt = (a * dt).reshape(b, nc, cl, g, k).transpose(0, 3, 4, 1, 2)
    bc = bmat.reshape(b, nc, cl, g, n)
    cc = cmat.reshape(b, nc, cl, g, n)
    acs = jnp.cumsum(adt, axis=-1)
    causal = jnp.tril(jnp.ones((cl, cl), dtype=bool))
    seg = acs[..., :, None] - acs[..., None, :]
    decay = jnp.exp(jnp.where(causal, seg, -jnp.inf))
    cb = jnp.einsum("bclgn,bcsgn->bcgls", cc, bc)
    y_diag = jnp.einsum("bcgls,bgkcls,bcsgkp->bclgkp", cb, decay, xc)
    state_decay = jnp.exp(acs[..., -1:] - acs)
    states = jnp.einsum("bclgn,bgkcl,bclgkp->bcgkpn", bc, state_decay, xc)
    chunk_decay = jnp.exp(acs[..., -1])

    def step(carry, inp):
        st, dec = inp
        return carry * dec[..., None, None] + st, carry

    h0 = jnp.zeros((b, g, k, p, n), dtype=jnp.float32)
    _, prev = lax.scan(step, h0, (states.transpose(1, 0, 2, 3, 4, 5), chunk_decay.transpose(3, 0, 1, 2)))
    prev = prev.transpose(1, 0, 2, 3, 4, 5)
    y_off = jnp.einsum("bclgn,bcgkpn,bgkcl->bclgkp", cc, prev, jnp.exp(acs))
    return (y_diag + y_off).reshape(b, s, h, p)


def dilated_window_attention(q, k, v, window, dilation):
    b, s, h, dh = q.shape
    sub_len = s // dilation
    span = window // dilation
    blk = min(ATT_BLOCK, sub_len)
    nb = sub_len // blk

    def to_blocks(t):
        t = t.reshape(b, sub_len, dilation, h, dh).transpose(0, 2, 3, 1, 4)
        return t.reshape(b, dilation, h, nb, blk, dh)

    def with_prev(t):
        tp = jnp.concatenate([jnp.zeros_like(t[:, :, :, :1]), t], axis=3)
        return jnp.concatenate([tp[:, :, :, :-1], tp[:, :, :, 1:]], axis=4)

    qb = to_blocks(q)
    kw = with_prev(to_blocks(k))
    vw = with_prev(to_blocks(v))
    scores = jnp.einsum("brhnqd,brhnkd->brhnqk", qb, kw,
                        preferred_element_type=jnp.float32) * (dh ** -0.5)
    q_idx = jnp.arange(blk)[:, None] + blk
    k_idx = jnp.arange(2 * blk)[None, :]
    dist = q_idx - k_idx
    k_pos = jnp.arange(nb)[:, None, None] * blk - blk + k_idx[None]
    mask = (dist >= 0)[None] & (dist <= span)[None] & (k_pos >= 0)
    scores = jnp.where(mask, scores, -jnp.inf)
    lse = jax.nn.logsumexp(scores, axis=-1)
    probs = jnp.exp(scores - lse[..., None])
    out = jnp.einsum("brhnqk,brhnkd->brhnqd", probs, vw.astype(jnp.float32))
    out = out.reshape(b, dilation, h, sub_len, dh).transpose(0, 3, 1, 2, 4).reshape(b, s, h, dh)
    lse = lse.reshape(b, dilation, h, sub_len).transpose(0, 3, 1, 2).reshape(b, s, h)
    return out, lse


def hybrid_ssd_attention(h, positions, w_in, conv_w, conv_b, dt_bias, a_log, d_skip, ssd_norm_g, w_out):
    b, s, _ = h.shape
    proj = h @ w_in
    split_at = [int(i) for i in np.cumsum(HYB_SPLIT_SIZES)[:-1]]
    z, xbc, dt_raw, q, k, v = jnp.split(proj, split_at, axis=-1)
    xbc = jax.nn.silu(causal_depthwise_conv(xbc, conv_w, conv_b)).astype(jnp.float32)
    xs = xbc[..., :SSD_WIDTH].reshape(b, s, SSD_HEADS, SSD_HEAD_DIM)
    bm = xbc[..., SSD_WIDTH:SSD_WIDTH + SSD_BC_WIDTH].reshape(b, s, SSD_GROUPS, SSD_STATE)
    cm = xbc[..., SSD_WIDTH + SSD_BC_WIDTH:].reshape(b, s, SSD_GROUPS, SSD_STATE)
    dt = jax.nn.softplus(dt_raw.astype(jnp.float32) + dt_bias.astype(jnp.float32))
    a = -jnp.exp(a_log.astype(jnp.float32))
    y = ssd_chunked_scan(xs, dt, a, bm, cm) + xs * d_skip.astype(jnp.float32)[:, None]
    y = y.reshape(b, s, SSD_WIDTH) * jax.nn.silu(z.astype(jnp.float32))
    y_a = rms_norm(y, ssd_norm_g).astype(h.dtype)
    q = apply_rope(q.reshape(b, s, ATT_HEADS, ATT_HEAD_DIM), positions)
    k = apply_rope(k.reshape(b, s, ATT_HEADS, ATT_HEAD_DIM), positions)
    v = v.reshape(b, s, ATT_HEADS, ATT_HEAD_DIM)
    outs, lses = [], []
    for window, dilation in DILATED_PATTERNS:
        o, l = dilated_window_attention(q, k, v, window, dilation)
        outs.append(o)
        lses.append(l)
    weights = jax.nn.softmax(jnp.stack(lses, axis=0), axis=0)
    y_b = jnp.sum(weights[..., None] * jnp.stack(outs, axis=0), axis=0)
    y_b = y_b.reshape(b, s, ATT_WIDTH).astype(h.dtype)
    return jnp.concatenate([y_a, y_b], axis=-1) @ w_out


def chunked_sgu_mixer(h, w_in, b_in, ln_g, ln_b, w_spatial, b_spatial, w_out):
    b, s, _ = h.shape
    zz = jax.nn.gelu(h @ w_in + b_in)
    u, v = zz[..., :SGU_WIDTH], zz[..., SGU_WIDTH:]
    v = layer_norm(v, ln_g, ln_b)
    nc = s // SGU_CHUNK
    vc = v.reshape(b, nc, SGU_CHUNK, SGU_GROUPS, SGU_WIDTH // SGU_GROUPS)
    causal = jnp.tril(jnp.ones((SGU_CHUNK, SGU_CHUNK), dtype=bool))
    ws = jnp.where(causal, w_spatial, 0)
    mixed = jnp.einsum("gts,bcsgd->bctgd", ws, vc) + b_spatial.T[None, None, :, :, None]
    return (u * mixed.reshape(b, s, SGU_WIDTH)) @ w_out


def setup_inputs(seed: int = 0) -> dict:
    key = jax.random.key(seed)
    ks = iter(jax.random.split(key, 32))
    f32 = jnp.float32
    nrm = lambda shape, scale: jax.random.normal(next(ks), shape, f32) * scale
    x = nrm((BATCH, SEQ, D_MODEL), 1.0)
    c = nrm((BATCH, D_MODEL), 1.0)
    offsets = jax.random.randint(next(ks), (BATCH, 1), 0, 4096, dtype=jnp.int32)
    positions = offsets + jnp.arange(SEQ, dtype=jnp.int32)[None, :]
    w_mod = nrm((DEPTH, D_MODEL, N_SUBLAYERS * 3 * D_MODEL), 0.1 * D_MODEL ** -0.5)
    b_mod = nrm((DEPTH, N_SUBLAYERS * 3 * D_MODEL), 0.02)
    norm_pre = 1.0 + nrm((DEPTH, N_SUBLAYERS, D_MODEL), 0.02)
    norm_post = 1.0 + nrm((DEPTH, N_SUBLAYERS, D_MODEL), 0.02)
    ffn_w_gate = nrm((DEPTH, 2, D_MODEL, FFN_DIM), D_MODEL ** -0.5)
    ffn_w_up = nrm((DEPTH, 2, D_MODEL, FFN_DIM), D_MODEL ** -0.5)
    ffn_w_down = nrm((DEPTH, 2, FFN_DIM, D_MODEL), FFN_DIM ** -0.5)
    hyb_w_in = nrm((N_HYB_LAYERS, D_MODEL, HYB_IN_WIDTH), D_MODEL ** -0.5)
    hyb_conv_w = nrm((N_HYB_LAYERS, SSD_CONV, SSD_CONV_CH), SSD_CONV ** -0.5)
    hyb_conv_b = nrm((N_HYB_LAYERS, SSD_CONV_CH), 0.02)
    dt0 = jnp.exp(jax.random.uniform(next(ks), (N_HYB_LAYERS, SSD_HEADS), f32,
                                     minval=math.log(1e-3), maxval=math.log(1e-1)))
    hyb_dt_bias = dt0 + jnp.log(-jnp.expm1(-dt0))
    hyb_a_log = jnp.log(jax.random.uniform(next(ks), (N_HYB_LAYERS, SSD_HEADS), f32, minval=1.0, maxval=16.0))
    hyb_d_skip = 1.0 + nrm((N_HYB_LAYERS, SSD_HEADS), 0.1)
    hyb_norm_g = 1.0 + nrm((N_HYB_LAYERS, SSD_WIDTH), 0.02)
    hyb_w_out = nrm((N_HYB_LAYERS, HYB_OUT_WIDTH, D_MODEL), HYB_OUT_WIDTH ** -0.5)
    sgu_w_in = nrm((N_SGU_LAYERS, D_MODEL, 2 * SGU_WIDTH), D_MODEL ** -0.5)
    sgu_b_in = nrm((N_SGU_LAYERS, 2 * SGU_WIDTH), 0.02)
    sgu_ln_g = 1.0 + nrm((N_SGU_LAYERS, SGU_WIDTH), 0.02)
    sgu_ln_b = nrm((N_SGU_LAYERS, SGU_WIDTH), 0.02)
    sgu_w_spatial = nrm((N_SGU_LAYERS, SGU_GROUPS, SGU_CHUNK, SGU_CHUNK), SGU_CHUNK ** -0.5)
    sgu_b_spatial = 1.0 + nrm((N_SGU_LAYERS, SGU_GROUPS, SGU_CHUNK), 0.02)
    sgu_w_out = nrm((N_SGU_LAYERS, SGU_WIDTH, D_MODEL), SGU_WIDTH ** -0.5)
    return {"x": x, "c": c, "positions": positions, "w_mod": w_mod, "b_mod": b_mod,
            "norm_pre": norm_pre, "norm_post": norm_post,
            "ffn_w_gate": ffn_w_gate, "ffn_w_up": ffn_w_up, "ffn_w_down": ffn_w_down,
            "hyb_w_in": hyb_w_in, "hyb_conv_w": hyb_conv_w, "hyb_conv_b": hyb_conv_b,
            "hyb_dt_bias": hyb_dt_bias, "hyb_a_log": hyb_a_log, "hyb_d_skip": hyb_d_skip,
            "hyb_norm_g": hyb_norm_g, "hyb_w_out": hyb_w_out,
            "sgu_w_in": sgu_w_in, "sgu_b_in": sgu_b_in, "sgu_ln_g": sgu_ln_g, "sgu_ln_b": sgu_ln_b,
            "sgu_w_spatial": sgu_w_spatial, "sgu_b_spatial": sgu_b_spatial, "sgu_w_out": sgu_w_out}


def reference(x, c, positions, w_mod, b_mod, norm_pre, norm_post, ffn_w_gate, ffn_w_up, ffn_w_down,
              hyb_w_in, hyb_conv_w, hyb_conv_b, hyb_dt_bias, hyb_a_log, hyb_d_skip, hyb_norm_g, hyb_w_out,
              sgu_w_in, sgu_b_in, sgu_ln_g, sgu_ln_b, sgu_w_spatial, sgu_b_spatial, sgu_w_out):
    c_act = jax.nn.silu(c)
    for layer in range(DEPTH):
        mod = (c_act @ w_mod[layer] + b_mod[layer]).reshape(-1, N_SUBLAYERS, 3, D_MODEL)
        x = modulated_sublayer(
            x, mod[:, 0], norm_pre[layer, 0], norm_post[layer, 0], FFN_RES_WEIGHT,
            lambda h: swiglu_ffn(h, ffn_w_gate[layer, 0], ffn_w_up[layer, 0], ffn_w_down[layer, 0]))
        i = layer // 2
        if layer % 2 == 0:
            mixer = lambda h: hybrid_ssd_attention(
                h, positions, hyb_w_in[i], hyb_conv_w[i], hyb_conv_b[i], hyb_dt_bias[i],
                hyb_a_log[i], hyb_d_skip[i], hyb_norm_g[i], hyb_w_out[i])
        else:
            mixer = lambda h: chunked_sgu_mixer(
                h, sgu_w_in[i], sgu_b_in[i], sgu_ln_g[i], sgu_ln_b[i],
                sgu_w_spatial[i], sgu_b_spatial[i], sgu_w_out[i])
        x = modulated_sublayer(x, mod[:, 1], norm_pre[layer, 1], norm_post[layer, 1], MIXER_RES_WEIGHT, mixer)
        x = modulated_sublayer(
            x, mod[:, 2], norm_pre[layer, 2], norm_post[layer, 2], FFN_RES_WEIGHT,
            lambda h: swiglu_ffn(h, ffn_w_gate[layer, 1], ffn_w_up[layer, 1], ffn_w_down[layer, 1]))
    return x
```

```python
import numpy as np
import concourse.bass as bass
import concourse.mybir as mybir

F32 = mybir.dt.float32
BF16 = mybir.dt.bfloat16
I32 = mybir.dt.int32
AF = mybir.ActivationFunctionType
ALU = mybir.AluOpType
AX = mybir.AxisListType
CE = ("pe", "act", "dve", "pool")
ALLE = ("pe", "act", "dve", "pool", "sp")


class _Op:
    __slots__ = ("fn", "deps", "inc", "semval", "dma")

    def __init__(self, fn, deps, dma=None):
        self.fn = fn
        self.deps = deps
        self.inc = False
        self.semval = 0
        self.dma = dma


class Prog:
    def __init__(self, nc, n_dma_sems=80):
        self.nc = nc
        self.ops = {e: [] for e in ALLE}
        self.esem = {e: nc.alloc_semaphore("es_" + e) for e in CE}
        self.dpool = [nc.alloc_semaphore("ds%d" % i) for i in range(n_dma_sems)]
        self.dcount = {id(s): 0 for s in self.dpool}
        self.dfree = list(self.dpool)
        self.dmap = {}
        self.lastw = {}
        self.readers = {}
        self.sb_off = 16640
        self.sb_marks = []
        self.uid = 0

    def sb(self, shape, dtype, name=None):
        self.uid += 1
        t = self.nc.alloc_sbuf_tensor_at("%s_%d" % (name or "t", self.uid), list(shape), dtype, offset=self.sb_off)
        nb = int(np.prod(shape[1:])) * mybir.dt.size(dtype)
        nb = (nb + 63) // 64 * 64
        self.sb_off += nb
        assert self.sb_off <= 196608, ("SBUF overflow", self.sb_off, name)
        return t

    def mark(self):
        self.sb_marks.append(self.sb_off)

    def release(self):
        self.sb_off = self.sb_marks.pop()

    def _collect(self, R, W):
        deps = []
        for k in R:
            t = self.lastw.get(k)
            if t is not None:
                deps.append(t)
        for k in W:
            t = self.lastw.get(k)
            if t is not None:
                deps.append(t)
            r = self.readers.get(k)
            if r:
                deps.extend(r.values())
        return deps

    def _update(self, R, W, tok, who):
        for k in R:
            self.readers.setdefault(k, {})[who] = tok
        for k in W:
            self.lastw[k] = tok
            self.readers[k] = {}

    def op(self, eng, fn, R=(), W=()):
        if eng != "pe":
            pk = [k for k in R if isinstance(k, tuple) and k[0] in ("ps", "po")]
            if pk:
                R = [k for k in R if k not in pk]
                W = list(W) + pk
        deps = self._collect(R, W)
        o = _Op(fn, deps)
        self.ops[eng].append(o)
        tok = ("e", eng, len(self.ops[eng]) - 1)
        self._update(R, W, tok, eng)
        return tok

    def _dsem(self, key):
        s = self.dmap.get(key)
        if s is None:
            assert self.dfree, "out of DMA semaphores"
            s = self.dfree.pop(0)
            self.dmap[key] = s
        return s

    def dma(self, q, out, in_, R=(), W=(), sk=None, **kw):
        assert sk is not None
        s = self._dsem(sk)
        self.dcount[id(s)] += 16
        v = self.dcount[id(s)]
        deps = self._collect(R, W)
        o = _Op(lambda e: e.dma_start(out=out, in_=in_, **kw), deps, dma=(s, v))
        self.ops[q].append(o)
        tok = ("d", s, v)
        self._update(R, W, tok, ("d", id(s)))
        return tok

    def barrier(self):
        deps = []
        for e in CE:
            if self.ops[e]:
                for i in range(len(self.ops[e]) - 1, -1, -1):
                    if self.ops[e][i].dma is None and self.ops[e][i].fn is not None:
                        deps.append(("e", e, i))
                        break
        for s in self.dpool:
            if self.dcount[id(s)] > 0:
                deps.append(("d", s, self.dcount[id(s)]))
        for e in ALLE:
            self.ops[e].append(_Op(None, list(deps)))
        self.lastw = {}
        self.readers = {}
        self.dmap = {}
        self.dfree = list(self.dpool)

    def finalize(self):
        nc = self.nc
        for e in ALLE:
            for o in self.ops[e]:
                for d in o.deps:
                    if d[0] == "e":
                        self.ops[d[1]][d[2]].inc = True
        for e in CE:
            c = 0
            for o in self.ops[e]:
                if o.inc:
                    c += 1
                    o.semval = c
        ops = self.ops
        esem = self.esem
        stats = {}

        def replay(ename):
            def run(eng):
                seen = {}
                nw = 0
                for o in ops[ename]:
                    for d in o.deps:
                        if d[0] == "e":
                            if d[1] == "pe" and ename == "pe":
                                continue
                            s = esem[d[1]]
                            v = ops[d[1]][d[2]].semval
                            k = d[1]
                        else:
                            s = d[1]
                            v = d[2]
                            k = id(s)
                        if seen.get(k, 0) >= v:
                            continue
                        seen[k] = v
                        eng.wait_ge(s, v)
                        nw += 1
                    if o.fn is None:
                        continue
                    ins = o.fn(eng)
                    if o.dma is not None:
                        ins.then_inc(o.dma[0], 16)
                    elif o.inc:
                        ins.then_inc(esem[ename], 1)
                stats[ename] = (len(ops[ename]), nw)
            return run

        with nc.Block() as block:
            block.tensor(replay("pe"))
            block.scalar(replay("act"))
            block.vector(replay("dve"))
            block.gpsimd(replay("pool"))
            block.sync(replay("sp"))
        return stats

S = 2048
D = 2048
KC = 16
FF = 5632
FCN = 44
TB = 1024
EPS = 1e-6


class Ctx:
    pass


def setup_common(P, nc, cx):
    cx.psall = nc.alloc_psum_tensor("psall", [128, 4096], F32)
    cx.ps = [cx.psall[:, i * 512:(i + 1) * 512] for i in range(8)]
    cx.ident_d = nc.dram_tensor("ident", [128, 128], F32, kind="ExternalInput").ap()
    cx.ident = P.sb([128, 128], F32, "ident")
    cx.identb = P.sb([128, 128], BF16, "identb")
    cx.onesb = P.sb([128, 128], BF16, "onesb")
    cx.epsc = P.sb([128, 1], F32, "epsc")
    P.dma("sp", cx.ident[:], cx.ident_d, W=["ident"], sk="ident")
    P.op("dve", lambda e: e.tensor_copy(out=cx.identb[:], in_=cx.ident[:]), R=["ident"], W=["identb"])
    P.op("dve", lambda e: e.memset(cx.onesb[:], 1.0), W=["onesb"])
    P.op("dve", lambda e: e.memset(cx.epsc[:], EPS), W=["epsc"])
    cx.rr = 0


def psb(cx, i, n=512):
    return cx.ps[i][:, 0:n]


def phase_x_in(P, cx, x_d, xT):
    P.mark()
    xin = [P.sb([128, 2048], F32, "xin") for _ in range(2)]
    xo = [P.sb([128, 16, 128], F32, "xo") for _ in range(2)]
    xTv = xT.rearrange("c p t -> p c t")
    for tt in range(16):
        sl = tt % 2
        P.dma("sp", xin[sl][:], x_d[tt * 128:(tt + 1) * 128, :], W=[("xin", sl)], sk=("xin", sl))
        for cb in range(4):
            bank = (tt * 4 + cb) % 4
            for j in range(4):
                c = cb * 4 + j
                P.op("pe", lambda e, bank=bank, j=j, c=c, sl=sl: e.transpose(
                    out=cx.ps[bank][:, j * 128:(j + 1) * 128], in_=xin[sl][:, c * 128:(c + 1) * 128], identity=cx.ident[:]),
                    R=[("xin", sl), "ident"], W=[("ps", bank)])
            eng = "act" if cb % 2 else "dve"
            if eng == "dve":
                P.op("dve", lambda e, bank=bank, cb=cb, sl=sl: e.tensor_copy(
                    out=xo[sl][:, cb * 4:(cb + 1) * 4, :], in_=cx.ps[bank][:].rearrange("p (a b) -> p a b", a=4)),
                    R=[("ps", bank)], W=[("xo", sl, cb)])
            else:
                P.op("act", lambda e, bank=bank, cb=cb, sl=sl: e.copy(
                    out=xo[sl][:, cb * 4:(cb + 1) * 4, :], in_=cx.ps[bank][:].rearrange("p (a b) -> p a b", a=4)),
                    R=[("ps", bank)], W=[("xo", sl, cb)])
        P.dma("sp", xTv[:, :, tt * 128:(tt + 1) * 128], xo[sl][:],
              R=[("xo", sl, cb) for cb in range(4)], W=[("xT", c, tt // 8) for c in range(16)], sk=("xo", sl))
    P.barrier()
    P.release()


def phase_x_out(P, cx, xT, out_d):
    P.mark()
    xin = [P.sb([128, 16, 128], F32, "xin") for _ in range(2)]
    xo = [P.sb([128, 2048], F32, "xo") for _ in range(2)]
    xTv = xT.rearrange("c p t -> p c t")
    for tt in range(16):
        sl = tt % 2
        P.dma("sp", xin[sl][:], xTv[:, :, tt * 128:(tt + 1) * 128], R=[("xT", c, tt // 8) for c in range(16)],
              W=[("xin", sl)], sk=("xin", sl))
        for cb in range(4):
            bank = (tt * 4 + cb) % 4
            for j in range(4):
                c = cb * 4 + j
                P.op("pe", lambda e, bank=bank, j=j, c=c, sl=sl: e.transpose(
                    out=cx.ps[bank][:, j * 128:(j + 1) * 128], in_=xin[sl][:, c, :], identity=cx.ident[:]),
                    R=[("xin", sl), "ident"], W=[("ps", bank)])
            if cb % 2 == 0:
                P.op("dve", lambda e, bank=bank, cb=cb, sl=sl: e.tensor_copy(
                    out=xo[sl][:, cb * 512:(cb + 1) * 512], in_=cx.ps[bank][:]),
                    R=[("ps", bank)], W=[("xo", sl, cb)])
            else:
                P.op("act", lambda e, bank=bank, cb=cb, sl=sl: e.copy(
                    out=xo[sl][:, cb * 512:(cb + 1) * 512], in_=cx.ps[bank][:]),
                    R=[("ps", bank)], W=[("xo", sl, cb)])
        P.dma("sp", out_d[tt * 128:(tt + 1) * 128, :], xo[sl][:],
              R=[("xo", sl, cb) for cb in range(4)], W=[("out", tt)], sk=("xo", sl))
    P.barrier()
    P.release()


def rstd_from_ss(P, cx, ps_ss_keys, rstd, key, n=TB, width=D):
    nb = n // 512
    for b in range(nb):
        P.op("act", lambda e, b=b: e.activation(out=rstd[:, b * 512:(b + 1) * 512], in_=cx.ps[4 + b][:], func=AF.Ln,
                                                bias=cx.epsc[:], scale=1.0 / width),
             R=[("ps", 4 + b), "epsc"], W=[(key, b)])
        P.op("act", lambda e, b=b: e.activation(out=rstd[:, b * 512:(b + 1) * 512], in_=rstd[:, b * 512:(b + 1) * 512],
                                                func=AF.Exp, scale=-0.5),
             R=[(key, b)], W=[(key, b)])


def phase_prenorm(P, cx, xT, t0, n, hT, Acol, Bcol, hkey="hT", sdt=F32, xkey="xT"):
    P.mark()
    xs = [P.sb([128, n], sdt, "xs") for _ in range(3)]
    sq = [P.sb([128, n], BF16, "sq") for _ in range(2)]
    tmp = [P.sb([128, n], F32, "tmp") for _ in range(2)]
    rstd = P.sb([128, n], F32, "rstd")
    nb = n // 512
    half = t0 // 1024
    for c in range(16):
        sl = c % 3
        P.dma("sp", xs[sl][:], xT[c, :, t0:t0 + n], R=[(xkey, c, half)], W=[("xs", sl)], sk=("xs", sl))
        P.op("act", lambda e, sl=sl, c=c: e.activation(out=sq[c % 2][:], in_=xs[sl][:], func=AF.Square),
             R=[("xs", sl)], W=[("sq", c % 2)])
        for b in range(nb):
            P.op("pe", lambda e, b=b, c=c: e.matmul(cx.ps[4 + b][:], lhsT=cx.onesb[:], rhs=sq[c % 2][:, b * 512:(b + 1) * 512],
                                                    start=(c == 0), stop=(c == 15)),
                 R=[("sq", c % 2), "onesb"], W=[("ps", 4 + b)])
    rstd_from_ss(P, cx, None, rstd, "rstd", n)
    for c in range(16):
        sl = c % 3
        P.dma("sp", xs[sl][:], xT[c, :, t0:t0 + n], R=[(xkey, c, half)], W=[("xs", sl)], sk=("xs", sl))
        P.op("dve", lambda e, sl=sl, c=c: e.scalar_tensor_tensor(out=tmp[c % 2][:], in0=xs[sl][:], scalar=Acol[:, c:c + 1],
                                                                 in1=rstd[:], op0=ALU.mult, op1=ALU.mult),
             R=[("xs", sl), ("rstd", 0), ("rstd", 1), "vecs"], W=[("tmp", c % 2)])
        P.op("act", lambda e, c=c: e.activation(out=hT[:, c, 0:n], in_=tmp[c % 2][:], func=AF.Identity,
                                                bias=Bcol[:, c:c + 1], scale=1.0),
             R=[("tmp", c % 2), "vecs"], W=[(hkey, c)])
    P.barrier()
    P.release()


def phase_post(P, cx, xT, yT, t0, n, Ccol):
    P.mark()
    xs = [P.sb([128, n], F32, "xs") for _ in range(2)]
    ys = [P.sb([128, n], F32, "ys") for _ in range(2)]
    tm = [P.sb([128, n], F32, "tm") for _ in range(2)]
    xn = [P.sb([128, n], F32, "xn") for _ in range(2)]
    rstd = P.sb([128, n], F32, "rstd")
    half = t0 // 1024
    rstd_from_ss(P, cx, None, rstd, "rstd", n)
    for c in range(16):
        sl = c % 2
        P.dma("sp", ys[sl][:], yT[c, :, t0:t0 + n], R=[("yT", c)], W=[("ys", sl)], sk=("ys", sl))
        P.dma("sp", xs[sl][:], xT[c, :, t0:t0 + n], R=[("xT", c, half)], W=[("xs", sl)], sk=("xs", sl))
        P.op("dve", lambda e, sl=sl, c=c: e.scalar_tensor_tensor(out=tm[sl][:], in0=ys[sl][:], scalar=Ccol[:, c:c + 1],
                                                                 in1=rstd[:], op0=ALU.mult, op1=ALU.mult),
             R=[("ys", sl), ("rstd", 0), ("rstd", 1), "vecs"], W=[("tm", sl)])
        P.op("dve", lambda e, sl=sl: e.tensor_tensor(out=xn[sl][:], in0=tm[sl][:], in1=xs[sl][:], op=ALU.add),
             R=[("tm", sl), ("xs", sl)], W=[("xn", sl)])
        P.dma("act", xT[c, :, t0:t0 + n], xn[sl][:], R=[("xn", sl)], W=[("xT", c, half)], sk=("xn", sl))
    P.barrier()
    P.release()


def down_proj(P, cx, w_d, nfc, actT, yT, t0, n, akey):
    P.mark()
    nq = 4
    qs = [(nfc * q) // nq for q in range(nq + 1)]
    qmax = max(qs[q + 1] - qs[q] for q in range(nq))
    wst = [P.sb([128, qmax, 128], F32, "wdst") for _ in range(nq)]
    wbf = [P.sb([128, nfc, 128], BF16, "wdbf") for _ in range(2)]
    ysb = [P.sb([128, 512], F32, "ysb") for _ in range(2)]
    sq = [P.sb([128, 512], BF16, "sq") for _ in range(2)]
    wv = w_d.rearrange("(fc p) d -> p fc d", p=128)
    nb = n // 512
    it = 0
    for dc in range(16):
        sl = dc % 2
        for q in range(nq):
            a, bb = qs[q], qs[q + 1]
            P.dma("sp", wst[q][:, 0:bb - a, :], wv[:, a:bb, dc * 128:(dc + 1) * 128], W=[("wdst", q)], sk=("wdst", q))
            if q % 2 == 0:
                P.op("dve", lambda e, sl=sl, a=a, bb=bb, q=q: e.tensor_copy(out=wbf[sl][:, a:bb, :], in_=wst[q][:, 0:bb - a, :]),
                     R=[("wdst", q)], W=[("wdbf", sl, q)])
            else:
                P.op("act", lambda e, sl=sl, a=a, bb=bb, q=q: e.copy(out=wbf[sl][:, a:bb, :], in_=wst[q][:, 0:bb - a, :]),
                     R=[("wdst", q)], W=[("wdbf", sl, q)])
        for b in range(nb):
            bank = it % 4
            s2 = it % 2
            it += 1
            for fc in range(nfc):
                P.op("pe", lambda e, bank=bank, fc=fc, sl=sl, b=b: e.matmul(
                    cx.ps[bank][:], lhsT=wbf[sl][:, fc, :], rhs=actT[:, fc, b * 512:(b + 1) * 512],
                    start=(fc == 0), stop=(fc == nfc - 1)),
                    R=[("wdbf", sl, min(nq - 1, (fc * nq) // nfc)), (akey, fc)], W=[("ps", bank)])
            P.op("dve", lambda e, bank=bank, s2=s2: e.tensor_copy(out=ysb[s2][:], in_=cx.ps[bank][:]),
                 R=[("ps", bank)], W=[("ysb", s2)])

            P.op("act", lambda e, bank=bank, s2=s2: e.activation(out=sq[s2][:], in_=ysb[s2][:], func=AF.Square),
                 R=[("ysb", s2)], W=[("sq", s2)])
            P.op("pe", lambda e, s2=s2, b=b, dc=dc: e.matmul(cx.ps[4 + b][:], lhsT=cx.onesb[:], rhs=sq[s2][:],
                                                             start=(dc == 0), stop=(dc == 15)),
                 R=[("sq", s2), "onesb"], W=[("ps", 4 + b)])
            P.dma("sp", yT[dc, :, t0 + b * 512:t0 + (b + 1) * 512], ysb[s2][:], R=[("ysb", s2)], W=[("yT", dc)], sk=("ysb", s2))
    P.barrier()
    P.release()


def ffn_phaseA(P, cx, hT, actT, wg_d, wu_d):
    P.mark()
    wst = [[P.sb([128, 16, 128], F32, "wst") for _ in range(2)] for _ in range(2)]
    wbf = [[P.sb([128, 16, 128], BF16, "wbf") for _ in range(2)] for _ in range(2)]
    sg = [P.sb([128, 512], BF16, "sg") for _ in range(2)]
    wviews = [wg_d.rearrange("(kc p) f -> p kc f", p=128), wu_d.rearrange("(kc p) f -> p kc f", p=128)]
    it = 0
    for fc in range(FCN):
        sl = fc % 2
        for m in range(2):
            P.dma("sp", wst[m][sl][:], wviews[m][:, :, fc * 128:(fc + 1) * 128],
                  W=[("wst", m, sl)], sk=("wst", m, sl))
            if m == 0:
                P.op("dve", lambda e, m=m, sl=sl: e.tensor_copy(out=wbf[m][sl][:], in_=wst[m][sl][:]),
                     R=[("wst", m, sl)], W=[("wbf", m, sl)])
            else:
                P.op("act", lambda e, m=m, sl=sl: e.copy(out=wbf[m][sl][:], in_=wst[m][sl][:]),
                     R=[("wst", m, sl)], W=[("wbf", m, sl)])
        for b in range(2):
            bg = (it % 2) * 2
            bu = bg + 1
            it += 1
            for m, bank in ((0, bg), (1, bu)):
                for kc in range(16):
                    P.op("pe", lambda e, m=m, bank=bank, kc=kc, sl=sl, b=b: e.matmul(
                        cx.ps[bank][:], lhsT=wbf[m][sl][:, kc, :], rhs=hT[:, kc, b * 512:(b + 1) * 512],
                        start=(kc == 0), stop=(kc == 15)),
                        R=[("wbf", m, sl), ("hT", kc)], W=[("ps", bank)])
            s2 = it % 2
            P.op("act", lambda e, bg=bg, s2=s2: e.activation(out=sg[s2][:], in_=cx.ps[bg][:], func=AF.Silu),
                 R=[("ps", bg)], W=[("sg", s2)])
            P.op("dve", lambda e, bu=bu, s2=s2, fc=fc, b=b: e.tensor_tensor(
                out=actT[:, fc, b * 512:(b + 1) * 512], in0=sg[s2][:], in1=cx.ps[bu][:], op=ALU.mult),
                R=[("sg", s2), ("ps", bu)], W=[("actT", fc)])
    P.barrier()
    P.release()


def ffn_sublayer(P, cx, xT, yT, wg_d, wu_d, wd_d, Acol, Bcol, Ccol):
    for half in range(2):
        t0 = half * TB
        P.mark()
        actT = P.sb([128, FCN, TB], BF16, "actT")
        P.mark()
        hT = P.sb([128, 16, TB], BF16, "hT")
        phase_prenorm(P, cx, xT, t0, TB, hT, Acol, Bcol)
        ffn_phaseA(P, cx, hT, actT, wg_d, wu_d)
        P.release()
        down_proj(P, cx, wd_d, FCN, actT, yT, t0, TB, "actT")
        phase_post(P, cx, xT, yT, t0, TB, Ccol)
        P.release()

OZ, OX, ODT, OQ, OKK, OV = 0, 2048, 5120, 5152, 7200, 9248
VEC_LAYOUT = [("c", 16), ("b_mod", 288), ("norm_pre", 96), ("norm_post", 96), ("conv_w", 96), ("conv_b", 24),
              ("hyb_norm_g", 16), ("sgu_b_in", 64), ("invf", 1), ("zero", 16)]
VOFF = {}
_o = 0
for _n, _w in VEC_LAYOUT:
    VOFF[_n] = _o
    _o += _w
NVEC = _o
ROW_LAYOUT = [("dt_bias", 32), ("a_log", 32), ("d_skip", 32), ("ln_g", 4096), ("ln_b", 4096), ("b_sp", 1024)]
ROFF = {}
_o = 0
for _n, _w in ROW_LAYOUT:
    ROFF[_n] = _o
    _o += _w
NROW = _o


def bc_mid(ap, reps):
    a = ap.ap
    return bass.AP(ap.tensor, ap.offset, [list(a[0]), [0, reps], list(a[1])])


def bc_last(ap, reps):
    a = ap.ap
    return bass.AP(ap.tensor, ap.offset, [list(a[0]), list(a[1]), [0, reps]])


class WStream:
    def __init__(self, P, n=2, name="ws", width=128, engines=("dve", "act"), nst=None):
        self.engines = engines
        self.P = P
        self.n = n
        self.nst = nst or n
        self.name = name
        self.st = [P.sb([128, 16, width], F32, name + "st") for _ in range(self.nst)]
        self.bf = [P.sb([128, 16, width], BF16, name + "bf") for _ in range(n)]
        self.i = 0

    def get(self, wview, c0, ncols):
        P = self.P
        sl = self.i % self.n
        ss = self.i % self.nst
        self.i += 1
        st, bf = self.st[ss], self.bf[sl]
        P.dma("sp", st[:, :, 0:ncols], wview[:, :, c0:c0 + ncols], W=[(self.name, "st", ss)], sk=(self.name, ss))
        if self.engines[self.i % len(self.engines)] == "dve":
            P.op("dve", lambda e: e.tensor_copy(out=bf[:, :, 0:ncols], in_=st[:, :, 0:ncols]),
                 R=[(self.name, "st", ss)], W=[(self.name, "bf", sl)])
        else:
            P.op("act", lambda e: e.copy(out=bf[:, :, 0:ncols], in_=st[:, :, 0:ncols]),
                 R=[(self.name, "st", ss)], W=[(self.name, "bf", sl)])
        return bf[:, :, 0:ncols], (self.name, "bf", sl)


def proj_fm(P, cx, wbf, wkey, hT, hkey, ntb, banks, evac):
    for tb in range(ntb):
        bank = banks[cx.rr % len(banks)]
        cx.rr += 1
        for kc in range(16):
            P.op("pe", lambda e, bank=bank, kc=kc, tb=tb: e.matmul(cx.ps[bank], lhsT=wbf[:, kc, :], rhs=hT[:, kc, tb * 512:(tb + 1) * 512],
                                                                  start=(kc == 0), stop=(kc == 15)),
                 R=[wkey, hkey], W=[("ps", bank)])
        evac(bank, tb)


def phase_mod(P, cx, vecs, wmod_d, mods):
    P.mark()
    ca = P.sb([128, 16, 2], F32, "ca")
    sgm = P.sb([128, 16], F32, "sgm")
    cc = vecs[:, VOFF["c"]:VOFF["c"] + 16]
    P.op("act", lambda e: e.activation(out=sgm[:], in_=cc, func=AF.Sigmoid), R=["vecs"], W=["sgm"])
    for j in range(2):
        P.op("dve", lambda e, j=j: e.tensor_tensor(out=ca[:, :, j], in0=sgm[:], in1=cc, op=ALU.mult), R=["sgm", "vecs"], W=["ca"])
    cab = P.sb([128, 16, 2], BF16, "cab")
    P.op("dve", lambda e: e.tensor_copy(out=cab[:], in_=ca[:]), R=["ca"], W=["cab"])
    wst = [P.sb([128, 16, 128], F32, "wm") for _ in range(3)]
    wbf = [P.sb([128, 16, 128], BF16, "wmb") for _ in range(3)]
    for l in range(2):
        wv = wmod_d[l].rearrange("(kc p) n -> p kc n", p=128)
        for j in range(144):
            it = l * 144 + j
            sl = it % 3
            P.dma("sp", wst[sl][:], wv[:, :, j * 128:(j + 1) * 128], W=[("wm", sl)], sk=("wm", sl))
            if it % 2 == 0:
                P.op("dve", lambda e, sl=sl: e.tensor_copy(out=wbf[sl][:], in_=wst[sl][:]), R=[("wm", sl)], W=[("wmb", sl)])
            else:
                P.op("act", lambda e, sl=sl: e.copy(out=wbf[sl][:], in_=wst[sl][:]), R=[("wm", sl)], W=[("wmb", sl)])
            for kc in range(16):
                P.op("pe", lambda e, l=l, j=j, kc=kc, sl=sl: e.matmul(cx.ps[l][:, 2 * j:2 * j + 2], lhsT=wbf[sl][:, kc, :], rhs=cab[:, kc, :],
                                                                     start=(kc == 0), stop=(kc == 15)),
                     R=[("wmb", sl), "cab"], W=[("ps", l)])
        P.op("dve", lambda e, l=l: e.tensor_tensor(out=mods[:, l * 144:(l + 1) * 144],
                                                   in0=cx.ps[l][:, 0:288].rearrange("p (j t) -> p j t", t=2)[:, :, 0],
                                                   in1=vecs[:, VOFF["b_mod"] + l * 144:VOFF["b_mod"] + (l + 1) * 144], op=ALU.add),
             R=[("ps", l), "vecs"], W=["mods"])
    P.barrier()
    P.release()


def sub_vectors(P, cx, vecs, mods, abc, l, s, rw):
    base = l * 144 + s * 48
    gpre = vecs[:, VOFF["norm_pre"] + (l * 3 + s) * 16:VOFF["norm_pre"] + (l * 3 + s + 1) * 16]
    gpost = vecs[:, VOFF["norm_post"] + (l * 3 + s) * 16:VOFF["norm_post"] + (l * 3 + s + 1) * 16]
    P.op("dve", lambda e: e.scalar_tensor_tensor(out=abc[:, 0:16], in0=mods[:, base + 16:base + 32], scalar=1.0, in1=gpre,
                                                 op0=ALU.add, op1=ALU.mult), R=["mods", "vecs"], W=["abc"])
    P.op("dve", lambda e: e.tensor_copy(out=abc[:, 16:32], in_=mods[:, base:base + 16]), R=["mods"], W=["abc"])
    P.op("dve", lambda e: e.scalar_tensor_tensor(out=abc[:, 32:48], in0=mods[:, base + 32:base + 48], scalar=1.0, in1=gpost,
                                                 op0=ALU.add, op1=ALU.mult), R=["mods", "vecs"], W=["abc"])
    if rw != 1.0:
        P.op("dve", lambda e: e.tensor_scalar(out=abc[:, 32:48], in0=abc[:, 32:48], scalar1=float(rw), scalar2=None, op0=ALU.mult),
             R=["abc"], W=["abc"])
    P.barrier()


def gelu_a(P, cx, bank, bias_col, tmps, idx, n=512):
    xb, t1, sg = tmps[idx % len(tmps)]
    k = ("gl", idx % len(tmps))
    P.op("act", lambda e: e.activation(out=xb[:, 0:n], in_=cx.ps[bank][:, 0:n], func=AF.Identity, bias=bias_col, scale=1.0),
         R=[("ps", bank), "vecs"], W=[k + ("xb",)])
    P.op("act", lambda e: e.activation(out=t1[:, 0:n], in_=xb[:, 0:n], func=AF.Square), R=[k + ("xb",)], W=[k + ("t1",)])
    P.op("dve", lambda e: e.tensor_scalar(out=t1[:, 0:n], in0=t1[:, 0:n], scalar1=0.044715, scalar2=1.0, op0=ALU.mult, op1=ALU.add),
         R=[k + ("t1",)], W=[k + ("t1",)])
    P.op("dve", lambda e: e.tensor_tensor(out=t1[:, 0:n], in0=t1[:, 0:n], in1=xb[:, 0:n], op=ALU.mult), R=[k + ("t1",), k + ("xb",)], W=[k + ("t1",)])


def gelu_b(P, cx, out_ap, okey, tmps, idx, n=512):
    xb, t1, sg = tmps[idx % len(tmps)]
    k = ("gl", idx % len(tmps))
    P.op("act", lambda e: e.activation(out=sg[:, 0:n], in_=t1[:, 0:n], func=AF.Sigmoid, scale=1.5957691216057308),
         R=[k + ("t1",)], W=[k + ("sg",)])
    P.op("dve", lambda e: e.tensor_tensor(out=out_ap, in0=xb[:, 0:n], in1=sg[:, 0:n], op=ALU.mult), R=[k + ("xb",), k + ("sg",)], W=[okey])


def sgu_sublayer(P, cx, xT, yT, vecs, rows_d, win_d, wsT_d, wout_d, triu, abc):
    NQ = 512
    Acol, Bcol, Ccol = abc[:, 0:16], abc[:, 16:32], abc[:, 32:48]
    wv = win_d.rearrange("(kc p) n -> p kc n", p=128)
    P.mark()
    wsT = P.sb([128, 8, 128], BF16, "wsT")
    bsp = P.sb([128, 8, 128], F32, "bsp")
    lng = P.sb([128, 4096], BF16, "lng")
    lnb = P.sb([128, 4096], BF16, "lnb")
    P.mark()
    lnf = P.sb([128, 4096], F32, "lnf")
    wsf = P.sb([128, 8, 128], F32, "wsf")
    P.dma("sp", wsf[:], wsT_d, W=["wsf"], sk="wsf")
    P.op("dve", lambda e: e.tensor_tensor(out=wsT[:], in0=wsf[:], in1=bc_mid(triu[:], 8), op=ALU.mult), R=["wsf", "triu"], W=["wsT"])
    P.dma("sp", bsp[:].rearrange("p a b -> p (a b)"), rows_d[0, ROFF["b_sp"]:ROFF["b_sp"] + 1024].partition_broadcast(128), W=["bsp"], sk="bsp")
    P.dma("sp", lnf[:], rows_d[0, ROFF["ln_g"]:ROFF["ln_g"] + 4096].partition_broadcast(128), W=["lnf"], sk="lnf")
    P.op("dve", lambda e: e.tensor_copy(out=lng[:], in_=lnf[:]), R=["lnf"], W=["lng"])
    P.dma("sp", lnf[:], rows_d[0, ROFF["ln_b"]:ROFF["ln_b"] + 4096].partition_broadcast(128), R=[], W=["lnf"], sk="lnf")
    P.op("dve", lambda e: e.tensor_copy(out=lnb[:], in_=lnf[:]), R=["lnf"], W=["lnb"])
    P.barrier()
    P.release()
    bin_off = VOFF["sgu_b_in"]
    def do_quarter(qt):
        t0 = qt * NQ
        P.mark()
        hT = P.sb([128, 16, NQ], BF16, "hT")
        gT = P.sb([128, 32, NQ], BF16, "gT")
        phase_prenorm(P, cx, xT, t0, NQ, hT, Acol, Bcol)
        P.mark()
        vtm = P.sb([128, 4, 4096], BF16, "vtm")
        ws = WStream(P, 2, "sgw", engines=("act",), nst=3)
        tmps = [(P.sb([128, 512], F32, "xb"), P.sb([128, 512], F32, "t1"), P.sb([128, 512], BF16, "sg")) for _ in range(2)]
        vT = [P.sb([128, 512], BF16, "vT") for _ in range(2)]
        ps7b = cx.ps[7].bitcast(BF16)
        def v_front(f):
            wbf, wkey = ws.get(wv, 4096 + f * 128, 128)
            proj_fm(P, cx, wbf, wkey, hT, "hT", 1, [0, 1],
                    lambda bank, tb: gelu_a(P, cx, bank, vecs[:, bin_off + 32 + f:bin_off + 33 + f], tmps, f))

        def v_mid(f):
            gelu_b(P, cx, vT[f % 2][:], ("vT", f % 2), tmps, f)

        def v_back(f):
            for tt in range(4):
                P.op("pe", lambda e, tt=tt: e.transpose(out=ps7b[:, tt * 128:(tt + 1) * 128], in_=vT[f % 2][:, tt * 128:(tt + 1) * 128],
                                                        identity=cx.identb[:]), R=[("vT", f % 2), "identb"], W=[("ps", 7)])
            P.op("dve", lambda e: e.tensor_copy(out=vtm[:, :, f * 128:(f + 1) * 128],
                                                in_=ps7b[:, 0:512].rearrange("p (a b) -> p a b", a=4)),
                 R=[("ps", 7)], W=[("vtm", tt2) for tt2 in range(4)])

        for f in range(34):
            if f < 32:
                v_front(f)
            if 1 <= f <= 32:
                v_mid(f - 1)
            if f >= 2:
                v_back(f - 2)
        P.mark()
        junk = P.sb([128, 4096], BF16, "junk")
        st = P.sb([128, 16], F32, "lnst")
        for tt in range(4):
            k = ("vtm", tt)
            P.op("act", lambda e, tt=tt: e.activation(out=junk[:], in_=vtm[:, tt, :], func=AF.Identity, accum_out=st[:, 0:1]),
                 R=[k], W=["junk", "lnst"])
            P.op("act", lambda e, tt=tt: e.activation(out=junk[:], in_=vtm[:, tt, :], func=AF.Square, accum_out=st[:, 1:2]),
                 R=[k], W=["junk", "lnst"])
            P.op("dve", lambda e: e.tensor_scalar(out=st[:, 2:3], in0=st[:, 0:1], scalar1=1.0 / 4096, scalar2=None, op0=ALU.mult), R=["lnst"], W=["lnst"])
            P.op("dve", lambda e: e.tensor_tensor(out=st[:, 3:4], in0=st[:, 2:3], in1=st[:, 2:3], op=ALU.mult), R=["lnst"], W=["lnst"])
            P.op("dve", lambda e: e.scalar_tensor_tensor(out=st[:, 4:5], in0=st[:, 1:2], scalar=1.0 / 4096, in1=st[:, 3:4],
                                                         op0=ALU.mult, op1=ALU.subtract), R=["lnst"], W=["lnst"])
            P.op("act", lambda e: e.activation(out=st[:, 5:6], in_=st[:, 4:5], func=AF.Ln, bias=cx.epsc[:], scale=1.0), R=["lnst", "epsc"], W=["lnst"])
            P.op("act", lambda e: e.activation(out=st[:, 6:7], in_=st[:, 5:6], func=AF.Exp, scale=-0.5), R=["lnst"], W=["lnst"])
            P.op("dve", lambda e: e.scalar_tensor_tensor(out=st[:, 7:8], in0=st[:, 2:3], scalar=-1.0, in1=st[:, 6:7], op0=ALU.mult, op1=ALU.mult),
                 R=["lnst"], W=["lnst"])
            P.op("act", lambda e, tt=tt: e.activation(out=junk[:], in_=vtm[:, tt, :], func=AF.Identity, bias=st[:, 7:8], scale=st[:, 6:7]),
                 R=[k, "lnst", "junk"], W=["junk"])
            P.op("dve", lambda e: e.tensor_tensor(out=junk[:], in0=junk[:], in1=lng[:], op=ALU.mult), R=["junk", "lng"], W=["junk"])
            P.op("dve", lambda e, tt=tt: e.tensor_tensor(out=vtm[:, tt, :], in0=junk[:], in1=lnb[:], op=ALU.add), R=["junk", "lnb"], W=[k])
        P.release()
        uT = [P.sb([128, 512], F32, "uT") for _ in range(2)]
        mt = [P.sb([128, 512], F32, "mt") for _ in range(2)]
        def u_front(f):
            wbf, wkey = ws.get(wv, f * 128, 128)
            g = f // 4
            proj_fm(P, cx, wbf, wkey, hT, "hT", 1, [0, 1],
                    lambda bank, tb: gelu_a(P, cx, bank, vecs[:, bin_off + f:bin_off + f + 1], tmps, f))
            mb = 2 + (f % 2)
            for tt in range(4):
                P.op("pe", lambda e, tt=tt: e.matmul(cx.ps[mb][:, tt * 128:(tt + 1) * 128], lhsT=vtm[:, tt, f * 128:(f + 1) * 128],
                                                     rhs=wsT[:, g, :], start=True, stop=True),
                     R=[("vtm", tt), "wsT"], W=[("ps", mb)])
            P.op("dve", lambda e: e.tensor_tensor(out=mt[f % 2][:].rearrange("p (a b) -> p a b", a=4),
                                                  in0=cx.ps[mb].rearrange("p (a b) -> p a b", a=4),
                                                  in1=bc_mid(bsp[:, g, :], 4), op=ALU.add), R=[("ps", mb), "bsp"], W=[("mt", f % 2)])

        def u_back(f):
            gelu_b(P, cx, uT[f % 2][:], ("uT", f % 2), tmps, f)
            P.op("dve", lambda e: e.tensor_tensor(out=gT[:, f, :], in0=mt[f % 2][:], in1=uT[f % 2][:], op=ALU.mult),
                 R=[("mt", f % 2), ("uT", f % 2)], W=[("gT", f)])

        for f in range(33):
            if f < 32:
                u_front(f)
            if f >= 1:
                u_back(f - 1)
        P.barrier()
        P.release()
        down_proj(P, cx, wout_d, 32, gT, yT, t0, NQ, "gT")
        phase_post(P, cx, xT, yT, t0, NQ, Ccol)
        P.release()

    for qt in range(4):
        do_quarter(qt)
    P.release()

PI = 3.141592653589793


def rope_tables(P, cx, pos_d, invf_col, cos_t, sin_t):
    P.mark()
    pi_ = P.sb([128, 2048], I32, "posi")
    ang = P.sb([128, 2048], F32, "ang")
    ki = pi_
    kf = P.sb([128, 2048], F32, "kf")
    r = P.sb([128, 2048], F32, "r")
    m = kf
    P.dma("sp", pi_[:], pos_d[0, :].partition_broadcast(128), W=["posi"], sk="posi")
    P.op("dve", lambda e: e.tensor_copy(out=ang[:], in_=pi_[:]), R=["posi"], W=["ang"])
    P.op("dve", lambda e: e.tensor_scalar(out=ang[:], in0=ang[:], scalar1=invf_col, scalar2=None, op0=ALU.mult), R=["ang", "vecs"], W=["ang"])
    for which, dst in ((0, sin_t), (1, cos_t)):
        sh = 0.0 if which == 0 else PI / 2
        P.op("dve", lambda e, sh=sh: e.tensor_scalar(out=ki[:], in0=ang[:], scalar1=sh, scalar2=1.0 / (2 * PI), op0=ALU.add, op1=ALU.mult),
             R=["ang", "posi"], W=["ki", "posi"])
        P.op("dve", lambda e: e.tensor_copy(out=kf[:], in_=ki[:]), R=["ki", "m"], W=["kf", "m"])
        P.op("dve", lambda e: e.scalar_tensor_tensor(out=r[:], in0=kf[:], scalar=-2 * PI, in1=ang[:], op0=ALU.mult, op1=ALU.add),
             R=["kf", "ang"], W=["r"])
        if sh != 0.0:
            P.op("dve", lambda e, sh=sh: e.tensor_scalar(out=r[:], in0=r[:], scalar1=sh, scalar2=None, op0=ALU.add), R=["r"], W=["r"])
        P.op("dve", lambda e: e.tensor_scalar(out=m[:], in0=r[:], scalar1=PI, scalar2=None, op0=ALU.is_gt), R=["r", "kf"], W=["m", "kf"])
        P.op("dve", lambda e: e.scalar_tensor_tensor(out=r[:], in0=m[:], scalar=-2 * PI, in1=r[:], op0=ALU.mult, op1=ALU.add), R=["m", "r"], W=["r"])
        P.op("dve", lambda e: e.tensor_scalar(out=m[:], in0=r[:], scalar1=-PI, scalar2=None, op0=ALU.is_lt), R=["r", "kf"], W=["m", "kf"])
        P.op("dve", lambda e: e.scalar_tensor_tensor(out=r[:], in0=m[:], scalar=2 * PI, in1=r[:], op0=ALU.mult, op1=ALU.add), R=["m", "r"], W=["r"])
        P.op("dve", lambda e: e.tensor_scalar(out=r[:], in0=r[:], scalar1=PI, scalar2=-PI, op0=ALU.min, op1=ALU.max), R=["r"], W=["r"])
        P.op("act", lambda e, dst=dst: e.activation(out=dst[:], in_=r[:], func=AF.Sin), R=["r"], W=["trig"])
    P.barrier()
    P.release()


def hybrid_attention(P, cx, hT, vecs, pos_d, win_d, yab, psw_d, ms_d, HS=9):
    wv = win_d.rearrange("(kc p) n -> p kc n", p=128)
    P.mark()
    cos_t = P.sb([128, 2048], F32, "cos")
    sin_t = P.sb([128, 2048], F32, "sin")
    rope_tables(P, cx, pos_d, vecs[:, VOFF["invf"]:VOFF["invf"] + 1], cos_t, sin_t)
    if HS == 2:
        P.release()
        return
    psw = P.sb([128, 128], F32, "psw")
    ms = [P.sb([128, 16, 128], BF16, "ms") for _ in range(2)]
    P.mark()
    msf = P.sb([128, 2048], F32, "msf")
    P.dma("sp", psw[:], psw_d, W=["psw"], sk="psw")
    for w in range(2):
        P.dma("sp", msf[:], ms_d[w], W=["msf"], sk="msf")
        P.op("dve", lambda e, w=w: e.tensor_copy(out=ms[w][:].rearrange("p a b -> p (a b)"), in_=msf[:]), R=["msf"], W=["ms"])
    P.barrier()
    P.release()
    ws = WStream(P, 2, "aw")
    q32 = P.sb([128, 2048], F32, "q32")
    t1 = P.sb([128, 2048], F32, "t1")
    t2 = P.sb([128, 2048], F32, "t2")
    qk = [P.sb([128, 16, 128], BF16, "qr"), P.sb([128, 16, 128], BF16, "kr")]
    vaug = P.sb([128, 16, 132], BF16, "vaug")
    ybh = [P.sb([128, 2048], BF16, "ybh") for _ in range(2)]
    E = [P.sb([128, 512], BF16, "E") for _ in range(2)]
    PT = [P.sb([128, 4, 128], BF16, "PT") for _ in range(4)]
    rden = P.sb([128, 8], F32, "rden")
    obf = [P.sb([128, 128], BF16, "obf") for _ in range(2)]
    ps0b = cx.ps[0].bitcast(BF16)
    P.op("dve", lambda e: e.memset(vaug[:], 1.0), W=["vaug"])
    scale = 128 ** -0.5
    it_s = 0
    for h in range(16 if HS >= 4 else 1):
        for which, off in ((0, OQ), (1, OKK)):
            wbf, wkey = ws.get(wv, off + h * 128, 128)

            def evac_q(bank, tb):
                P.op("act", lambda e: e.copy(out=q32[:, tb * 512:(tb + 1) * 512], in_=cx.ps[bank]), R=[("ps", bank)], W=[("q32", tb)])
                rbk = 2 + tb % 2
                P.op("pe", lambda e: e.matmul(cx.ps[rbk], lhsT=psw[:], rhs=q32[:, tb * 512:(tb + 1) * 512], start=True, stop=True),
                     R=[("q32", tb), "psw"], W=[("ps", rbk)])
                P.op("dve", lambda e: e.tensor_tensor(out=t2[:, tb * 512:(tb + 1) * 512], in0=cx.ps[rbk], in1=sin_t[:, tb * 512:(tb + 1) * 512],
                                                      op=ALU.mult), R=[("ps", rbk), "trig"], W=[("t2", tb)])
                P.op("dve", lambda e: e.tensor_tensor(out=t1[:, tb * 512:(tb + 1) * 512], in0=q32[:, tb * 512:(tb + 1) * 512],
                                                       in1=cos_t[:, tb * 512:(tb + 1) * 512], op=ALU.mult), R=[("q32", tb), "trig"], W=[("t1", tb)])
            proj_fm(P, cx, wbf, wkey, hT, "hT", 4, [0, 1], evac_q)
            dst = qk[which]
            P.op("dve", lambda e, dst=dst: e.tensor_tensor(out=dst[:], in0=t1[:].rearrange("p (i r) -> p r i", r=16),
                                                           in1=t2[:].rearrange("p (i r) -> p r i", r=16), op=ALU.add),
                 R=[("t1", tb) for tb in range(4)] + [("t2", tb) for tb in range(4)], W=[("qk", which)])
        wbf, wkey = ws.get(wv, OV + h * 128, 128)
        hTc = hT[:].rearrange("p k (i r) -> p k r i", r=16)
        for cb in range(4):
            bank = cb % 2
            for j in range(4):
                c = cb * 4 + j
                for kc in range(16):
                    P.op("pe", lambda e, c=c, j=j, kc=kc, bank=bank, wbf=wbf: e.matmul(cx.ps[bank][:, j * 128:(j + 1) * 128], lhsT=hTc[:, kc, c, :],
                                                                             rhs=wbf[:, kc, :], start=(kc == 0), stop=(kc == 15)),
                         R=[wkey, "hT"], W=[("ps", bank)])
            P.op("act", lambda e, cb=cb, bank=bank: e.copy(out=vaug[:, cb * 4:(cb + 1) * 4, 0:128],
                                                          in_=cx.ps[bank].rearrange("p (a b) -> p a b", a=4)),
                 R=[("ps", bank)], W=["vaug"])
        yb = ybh[h % 2]
        ybv = yb[:].rearrange("p (i r) -> p r i", r=16)
        def att_front(rg, c, sb_, es, pt):
            P.op("pe", lambda e: e.matmul(cx.ps[sb_], lhsT=qk[1][:, c, :],
                                          rhs=qk[0][:, rg * 4:(rg + 1) * 4, :].rearrange("p a b -> p (a b)"),
                                          start=True, stop=True),
                 R=[("qk", 0), ("qk", 1)], W=[("ps", sb_)])
            P.op("act", lambda e: e.activation(out=E[es][:], in_=cx.ps[sb_], func=AF.Exp, scale=scale),
                 R=[("ps", sb_)], W=[("E", es)])
            for wsel in (1, 0):
                js = [j for j in range(4) if (1 if c > rg * 4 + j else 0) == wsel]
                if not js:
                    continue
                j0, nj = js[0], len(js)
                e0 = (rg * 4 + j0 - c) % 16
                P.op("dve", lambda e, j0=j0, nj=nj, e0=e0, wsel=wsel: e.tensor_tensor(
                    out=PT[pt][:, j0:j0 + nj, :], in0=E[es][:].rearrange("p (a b) -> p a b", a=4)[:, j0:j0 + nj, :],
                    in1=ms[wsel][:, e0:e0 + nj, :], op=ALU.mult), R=[("E", es), "ms"], W=[("PT", pt)])

        def att_back(rg, c, pt, ybv=ybv, h=h):
            for j in range(4):
                ob = 4 + j
                P.op("pe", lambda e, j=j, ob=ob: e.matmul(cx.ps[ob][:, 0:129], lhsT=PT[pt][:, j, :],
                                                          rhs=vaug[:, c, 0:129], start=(c == 0), stop=(c == 15)),
                     R=[("PT", pt), "vaug"], W=[("po", j)])
            if c != 15:
                return
            for j in range(4):
                r = rg * 4 + j
                ob = 4 + j
                o2 = j % 2
                P.op("dve", lambda e, j=j, ob=ob: e.reciprocal(out=rden[:, j:j + 1], in_=cx.ps[ob][:, 128:129]),
                     R=[("po", j)], W=[("rden", j)])
                P.op("dve", lambda e, j=j, ob=ob, o2=o2: e.tensor_scalar(out=obf[o2][:], in0=cx.ps[ob][:, 0:128],
                                                                         scalar1=rden[:, j:j + 1], scalar2=None, op0=ALU.mult),
                     R=[("po", j), ("rden", j)], W=[("obf", o2)])
                P.op("pe", lambda e, o2=o2: e.transpose(out=ps0b[:, o2 * 128:(o2 + 1) * 128], in_=obf[o2][:], identity=cx.identb[:]),
                     R=[("obf", o2), "identb"], W=[("ps", 0)])
                P.op("act", lambda e, o2=o2, r=r: e.copy(out=ybv[:, r, :], in_=ps0b[:, o2 * 128:(o2 + 1) * 128]),
                     R=[("ps", 0)], W=[("ybh", h % 2)])

        pend = []
        for rg in range(4):
            for c in range(16):
                sb_ = 2 + it_s % 2
                es = it_s % 2
                pt = it_s % 4
                it_s += 1
                att_front(rg, c, sb_, es, pt)
                pend.append((rg, c, pt))
                if len(pend) > 1:
                    att_back(*pend.pop(0))
        while pend:
            att_back(*pend.pop(0))
        P.dma("sp", yab[16 + h], yb[:], R=[("ybh", h % 2)], W=[("yab", 16 + h)], sk=("ybh", h % 2))
    P.barrier()
    P.release()


def hybrid_ssd(P, cx, hT, vecs, rows_d, win_d, yab, triu, acs_d, HS=9):
    wv = win_d.rearrange("(kc p) n -> p kc n", p=128)
    P.mark()
    rb = P.sb([128, 96], F32, "rb")
    P.dma("sp", rb[:], rows_d[0, 0:96].partition_broadcast(128), W=["rb"], sk="rb")
    negm = P.sb([128, 128], F32, "negm")
    P.op("dve", lambda e: e.tensor_scalar(out=negm[:], in0=triu[:], scalar1=1.0, scalar2=30000.0, op0=ALU.subtract, op1=ALU.mult),
         R=["triu"], W=["negm"])
    onesf = P.sb([128, 128], F32, "onesf")
    P.op("dve", lambda e: e.memset(onesf[:], 1.0), W=["onesf"])
    onec = P.sb([128, 1], F32, "onec")
    P.op("dve", lambda e: e.memset(onec[:], 1.0), W=["onec"])
    dt = P.sb([128, 16, 32], F32, "dt")
    acs = P.sb([128, 16, 32], F32, "acs")
    eacs = P.sb([128, 16, 32], F32, "eacs")
    cdb = P.sb([128, 16, 32], F32, "cdb")
    wsd = P.sb([128, 16, 32], F32, "wsd")
    f3 = lambda t: t[:].rearrange("p a b -> p (a b)")
    P.mark()
    adt = P.sb([128, 16, 32], F32, "adt")
    acsT = P.sb([32, 16, 128], F32, "acsT")
    tA = P.sb([128, 16, 32], F32, "tA")
    tB = P.sb([128, 16, 32], F32, "tB")
    ea = P.sb([128, 32], F32, "ea")
    wdt_st = P.sb([128, 16, 32], F32, "wdtst")
    wdt = P.sb([128, 16, 32], BF16, "wdt")
    P.dma("sp", wdt_st[:], wv[:, :, ODT:ODT + 32], W=["wdtst"], sk="wdtst")
    P.op("dve", lambda e: e.tensor_copy(out=wdt[:], in_=wdt_st[:]), R=["wdtst"], W=["wdt"])
    for tt in range(16):
        for kc in range(16):
            P.op("pe", lambda e, tt=tt, kc=kc: e.matmul(cx.ps[0][:, tt * 32:(tt + 1) * 32], lhsT=hT[:, kc, tt * 128:(tt + 1) * 128],
                                                        rhs=wdt[:, kc, :], start=(kc == 0), stop=(kc == 15)), R=["wdt", "hT"], W=[("ps", 0)])
    P.op("dve", lambda e: e.tensor_tensor(out=tA[:], in0=cx.ps[0].rearrange("p (a b) -> p a b", a=16), in1=bc_mid(rb[:, 0:32], 16), op=ALU.add),
         R=[("ps", 0), "rb"], W=["tA"])
    P.op("dve", lambda e: e.scalar_tensor_tensor(out=f3(tB), in0=f3(tA), scalar=-1.0, in1=f3(tA), op0=ALU.mult, op1=ALU.max), R=["tA"], W=["tB"])
    P.op("act", lambda e: e.activation(out=f3(tB), in_=f3(tB), func=AF.Exp, scale=-1.0), R=["tB"], W=["tB"])
    P.op("act", lambda e: e.activation(out=f3(tB), in_=f3(tB), func=AF.Ln, bias=onec[:], scale=1.0), R=["tB", "onec"], W=["tB"])
    P.op("dve", lambda e: e.scalar_tensor_tensor(out=f3(dt), in0=f3(tA), scalar=0.0, in1=f3(tB), op0=ALU.max, op1=ALU.add), R=["tA", "tB"], W=["dt"])
    import os
    HP = int(os.environ.get("HP", "9"))
    if HP <= 1:
        P.barrier(); P.release(); P.release(); return
    P.op("act", lambda e: e.activation(out=ea[:], in_=rb[:, 32:64], func=AF.Exp), R=["rb"], W=["ea"])
    P.op("dve", lambda e: e.scalar_tensor_tensor(out=adt[:], in0=dt[:], scalar=-1.0, in1=bc_mid(ea[:], 16), op0=ALU.mult, op1=ALU.mult),
         R=["dt", "ea"], W=["adt"])
    HQ = int(os.environ.get("HQ", "9"))
    if HQ <= 1:
        P.barrier(); P.release(); P.release(); return
    for tt in range(16):
        P.op("pe", lambda e, tt=tt: e.matmul(cx.ps[1][:, tt * 32:(tt + 1) * 32], lhsT=triu[:], rhs=adt[:, tt, :], start=True, stop=True),
             R=["triu", "adt"], W=[("ps", 1)])
    if HQ <= 2:
        P.op("dve", lambda e: e.tensor_copy(out=f3(acs), in_=cx.ps[1]), R=[("ps", 1)], W=["acs"])
        P.barrier(); P.release(); P.release(); return
    P.op("pe", lambda e: e.matmul(cx.ps[2], lhsT=onesf[:], rhs=f3(adt), start=True, stop=True), R=["onesf", "adt"], W=[("ps", 2)])
    P.op("dve", lambda e: e.tensor_copy(out=f3(acs), in_=cx.ps[1]), R=[("ps", 1)], W=["acs"])
    if HQ <= 3:
        P.barrier(); P.release(); P.release(); return
    P.op("act", lambda e: e.activation(out=f3(eacs), in_=cx.ps[1], func=AF.Exp), R=[("ps", 1)], W=["eacs"])
    if HQ <= 4:
        P.barrier(); P.release(); P.release(); return
    P.op("act", lambda e: e.activation(out=f3(cdb), in_=cx.ps[2], func=AF.Exp), R=[("ps", 2)], W=["cdb"])
    if HQ <= 5:
        P.barrier(); P.release(); P.release(); return
    P.op("dve", lambda e: e.tensor_tensor(out=f3(tA), in0=cx.ps[2], in1=f3(acs), op=ALU.subtract), R=[("ps", 2), "acs", "dt"], W=["tA"])
    P.op("act", lambda e: e.activation(out=f3(tA), in_=f3(tA), func=AF.Exp), R=["tA"], W=["tA"])
    P.op("dve", lambda e: e.tensor_tensor(out=f3(wsd), in0=f3(tA), in1=f3(dt), op=ALU.mult), R=["tA", "dt"], W=["wsd"])
    if HP <= 2:
        P.barrier(); P.release(); P.release(); return
    for tt in range(16):
        bank = 3 + tt // 4
        P.op("pe", lambda e, tt=tt, bank=bank: e.transpose(out=cx.ps[bank][0:32, (tt % 4) * 128:(tt % 4 + 1) * 128], in_=acs[:, tt, :],
                                                           identity=cx.ident[:]), R=["acs", "ident"], W=[("ps", bank)])
    for b4 in range(4):
        P.op("dve", lambda e, b4=b4: e.tensor_copy(out=acsT[:, b4 * 4:(b4 + 1) * 4, :],
                                                   in_=cx.ps[3 + b4][0:32, :].rearrange("p (a b) -> p a b", a=4)),
             R=[("ps", 3 + b4)], W=["acsT"])
    if HP <= 3:
        P.barrier(); P.release(); P.release(); return
    P.dma("sp", acs_d, acsT[:], R=["acsT"], W=["acs_d"], sk="acsT")
    P.barrier()
    P.release()
    if HS == 5:
        P.release()
        return
    ws = WStream(P, 1, "sw")
    xpad = [P.sb([128, 516], F32, "xpad") for _ in range(2)]
    ca = [P.sb([128, 512], F32, "ca") for _ in range(2)]
    cvb = P.sb([128, 2048], BF16, "cvb")
    BT = P.sb([128, 2048], BF16, "BT")
    CT = P.sb([128, 2048], BF16, "CT")
    xs_tm = P.sb([128, 16, 512], BF16, "xs_tm")
    z_tm = P.sb([128, 16, 512], BF16, "z_tm")
    B_tm = P.sb([128, 16, 128], BF16, "B_tm")
    yaT = [P.sb([128, 4, 128], BF16, "yaT") for _ in range(2)]
    rowsb = P.sb([128, 8, 128], F32, "rowsb")
    dec = P.sb([128, 8, 128], BF16, "dec")
    LT = [P.sb([128, 8, 128], BF16, "LT") for _ in range(2)]
    cbs = P.sb([128, 128], F32, "cbs")
    xdt = [P.sb([128, 512], BF16, "xdt") for _ in range(2)]
    xw = [P.sb([128, 512], BF16, "xw") for _ in range(2)]
    carry = P.sb([128, 512], F32, "carry")
    prevb = P.sb([128, 512], BF16, "prevb")
    y1 = P.sb([128, 512], F32, "y1")
    y2 = P.sb([128, 512], F32, "y2")
    yg = P.sb([128, 512], BF16, "yg")
    ps7b = cx.ps[7].bitcast(BF16)
    cw = VOFF["conv_w"]
    cbo = VOFF["conv_b"]
    v3 = lambda ap, a: ap.rearrange("p (a b) -> p a b", a=a)
    for g in range(4 if HS >= 7 else 1):
        chunks = [(OX + g * 512 + j * 128, g * 4 + j, "xs", j) for j in range(4)] + \
                 [(OX + 2048 + g * 128, 16 + g, "B", 0), (OX + 2560 + g * 128, 20 + g, "C", 0)]
        for (col, ch, kind, j) in chunks:
            wbf, wkey = ws.get(wv, col, 128)
            dstT = {"xs": cvb, "B": BT, "C": CT}[kind]
            P.op("dve", lambda e: e.memset(xpad[0][:, 0:3], 0.0), R=[("xpad", 0)], W=[("xpad", 0)])

            def evac_c(bank, tb, ch=ch, dstT=dstT, kind=kind):
                xp = xpad[tb % 2]
                xn_ = xpad[(tb + 1) % 2]
                k0 = ("xpad", tb % 2)
                k1 = ("xpad", (tb + 1) % 2)
                P.op("act", lambda e: e.copy(out=xp[:, 3:515], in_=cx.ps[bank]), R=[("ps", bank)], W=[k0])
                if tb < 3:
                    P.op("dve", lambda e: e.tensor_copy(out=xn_[:, 0:3], in_=xp[:, 512:515]), R=[k0], W=[k1])
                P.op("dve", lambda e: e.tensor_scalar(out=ca[0][:], in0=xp[:, 0:512], scalar1=vecs[:, cw + ch:cw + ch + 1],
                                                      scalar2=vecs[:, cbo + ch:cbo + ch + 1], op0=ALU.mult, op1=ALU.add),
                     R=[k0, "vecs"], W=["ca0"])
                P.op("dve", lambda e: e.scalar_tensor_tensor(out=ca[1][:], in0=xp[:, 1:513], scalar=vecs[:, cw + 24 + ch:cw + 24 + ch + 1],
                                                              in1=ca[0][:], op0=ALU.mult, op1=ALU.add), R=[k0, "ca0", "vecs"], W=["ca1"])
                P.op("dve", lambda e: e.scalar_tensor_tensor(out=ca[0][:], in0=xp[:, 2:514], scalar=vecs[:, cw + 48 + ch:cw + 48 + ch + 1],
                                                             in1=ca[1][:], op0=ALU.mult, op1=ALU.add), R=[k0, "ca1", "vecs"], W=["ca0"])
                P.op("dve", lambda e: e.scalar_tensor_tensor(out=ca[1][:], in0=xp[:, 3:515], scalar=vecs[:, cw + 72 + ch:cw + 72 + ch + 1],
                                                              in1=ca[0][:], op0=ALU.mult, op1=ALU.add), R=[k0, "ca0", "vecs"], W=["ca1"])
                P.op("act", lambda e: e.activation(out=dstT[:, tb * 512:(tb + 1) * 512], in_=ca[1][:], func=AF.Silu),
                     R=["ca1", "xs_tm"], W=[(kind + "T", tb)])
            proj_fm(P, cx, wbf, wkey, hT, "hT", 4, [0, 1], evac_c)
            if kind in ("xs", "B"):
                for t4 in range(4):
                    for q in range(4):
                        tt = t4 * 4 + q
                        P.op("pe", lambda e, tt=tt, q=q, dstT=dstT: e.transpose(out=ps7b[:, q * 128:(q + 1) * 128], in_=dstT[:, tt * 128:(tt + 1) * 128],
                                                                               identity=cx.identb[:]), R=[(kind + "T", t4), "identb"], W=[("ps", 7)])
                    if kind == "xs":
                        P.op("dve", lambda e, t4=t4, j=j: e.tensor_copy(out=xs_tm[:, t4 * 4:(t4 + 1) * 4, j * 128:(j + 1) * 128],
                                                                        in_=v3(ps7b[:, 0:512], 4)), R=[("ps", 7)], W=["xs_tm"])
                    else:
                        P.op("dve", lambda e, t4=t4: e.tensor_copy(out=B_tm[:, t4 * 4:(t4 + 1) * 4, :], in_=v3(ps7b[:, 0:512], 4)),
                             R=[("ps", 7)], W=["B_tm"])
        for j in range(4):
            wbf, wkey = ws.get(wv, OZ + g * 512 + j * 128, 128)

            def evac_z(bank, tb, j=j):
                P.op("act", lambda e: e.activation(out=cvb[:, tb * 512:(tb + 1) * 512], in_=cx.ps[bank], func=AF.Silu),
                     R=[("ps", bank), "xs_tm"], W=[("zT", tb), ("xsT", tb)])
                for q in range(4):
                    P.op("pe", lambda e, q=q: e.transpose(out=ps7b[:, q * 128:(q + 1) * 128], in_=cvb[:, tb * 512 + q * 128:tb * 512 + (q + 1) * 128],
                                                          identity=cx.identb[:]), R=[("zT", tb), "identb"], W=[("ps", 7)])
                P.op("dve", lambda e: e.tensor_copy(out=z_tm[:, tb * 4:(tb + 1) * 4, j * 128:(j + 1) * 128], in_=v3(ps7b[:, 0:512], 4)),
                     R=[("ps", 7)], W=["z_tm"])
            proj_fm(P, cx, wbf, wkey, hT, "hT", 4, [0, 1], evac_z)
        g8 = slice(g * 8, (g + 1) * 8)
        BTk = [("BT", t) for t in range(4)]
        CTk = [("CT", t) for t in range(4)]
        def ssd_front(tt, g=g, g8=g8):
            tsl = slice(tt * 128, (tt + 1) * 128)
            sl = tt % 2
            P.dma("sp", rowsb[:], acs_d[g * 8:(g + 1) * 8, tt, :].partition_broadcast(128), R=["acs_d"], W=["rowsb"], sk="rowsb")
            P.op("pe", lambda e: e.matmul(cx.ps[0][:, 0:128], lhsT=BT[:, tsl], rhs=CT[:, tsl], start=True, stop=True),
                 R=BTk + CTk, W=[("ps", 0)])
            P.op("act", lambda e: e.copy(out=cbs[:], in_=cx.ps[0][:, 0:128]), R=[("ps", 0)], W=["cbs"])
            P.op("dve", lambda e: e.tensor_tensor(out=rowsb[:], in0=rowsb[:], in1=bc_last(acs[:, tt, g8], 128), op=ALU.subtract),
                 R=["rowsb", "acs"], W=["rowsb"])
            P.op("dve", lambda e: e.tensor_tensor(out=rowsb[:], in0=rowsb[:], in1=bc_mid(negm[:], 8), op=ALU.add), R=["rowsb", "negm"], W=["rowsb"])
            P.op("act", lambda e: e.activation(out=dec[:], in_=rowsb[:], func=AF.Exp), R=["rowsb"], W=["dec"])
            P.op("dve", lambda e: e.tensor_tensor(out=LT[sl][:], in0=dec[:], in1=bc_mid(cbs[:], 8), op=ALU.mult), R=["dec", "cbs"], W=[("LT", sl)])
            P.op("dve", lambda e: e.tensor_tensor(out=v3(xdt[sl][:], 8), in0=v3(xs_tm[:, tt, :], 8),
                                                  in1=bc_last(dt[:, tt, g8], 64), op=ALU.mult), R=["xs_tm", "dt"], W=[("xdt", sl)])
            P.op("dve", lambda e: e.tensor_tensor(out=v3(xw[sl][:], 8), in0=v3(xs_tm[:, tt, :], 8),
                                                  in1=bc_last(wsd[:, tt, g8], 64), op=ALU.mult), R=["xs_tm", "wsd"], W=[("xw", sl)])

        def ssd_back(tt, g=g, g8=g8):
            tsl = slice(tt * 128, (tt + 1) * 128)
            sl = tt % 2
            for hh in range(8):
                P.op("pe", lambda e, hh=hh: e.matmul(cx.ps[3][:, hh * 64:(hh + 1) * 64], lhsT=LT[sl][:, hh, :], rhs=xdt[sl][:, hh * 64:(hh + 1) * 64],
                                                     start=True, stop=True), R=[("LT", sl), ("xdt", sl)], W=[("ps", 3)])
            if tt > 0:
                P.op("pe", lambda e: e.matmul(cx.ps[4], lhsT=CT[:, tsl], rhs=prevb[:], start=True, stop=True),
                     R=CTk + ["prevb"], W=[("ps", 4)])
            if tt < 15:
                P.op("pe", lambda e: e.matmul(cx.ps[5], lhsT=B_tm[:, tt, :], rhs=xw[sl][:], start=True, stop=True),
                     R=["B_tm", ("xw", sl)], W=[("ps", 5)])
            if tt > 0:
                P.op("dve", lambda e: e.tensor_tensor(out=v3(y1[:], 8), in0=v3(cx.ps[4], 8),
                                                      in1=bc_last(eacs[:, tt, g8], 64), op=ALU.mult), R=[("ps", 4), "eacs"], W=["y1"])
                P.op("dve", lambda e: e.tensor_tensor(out=y1[:], in0=y1[:], in1=cx.ps[3], op=ALU.add), R=["y1", ("ps", 3)], W=["y1"])
            else:
                P.op("dve", lambda e: e.tensor_copy(out=y1[:], in_=cx.ps[3]), R=[("ps", 3)], W=["y1"])
            P.op("dve", lambda e: e.tensor_tensor(out=v3(y2[:], 8), in0=v3(xs_tm[:, tt, :], 8),
                                                  in1=bc_last(rb[:, 64 + g * 8:64 + (g + 1) * 8], 64), op=ALU.mult), R=["xs_tm", "rb"], W=["y2"])
            P.op("dve", lambda e: e.tensor_tensor(out=y2[:], in0=y2[:], in1=y1[:], op=ALU.add), R=["y2", "y1"], W=["y2"])
            P.op("dve", lambda e: e.tensor_tensor(out=yg[:], in0=y2[:], in1=z_tm[:, tt, :], op=ALU.mult), R=["y2", "z_tm"], W=["yg"])
            for q in range(4):
                P.op("pe", lambda e, q=q: e.transpose(out=ps7b[:, q * 128:(q + 1) * 128], in_=yg[:, q * 128:(q + 1) * 128], identity=cx.identb[:]),
                     R=["yg", "identb"], W=[("ps", 7)])
            P.op("act", lambda e: e.copy(out=yaT[sl][:], in_=v3(ps7b[:, 0:512], 4)), R=[("ps", 7)], W=[("yaT", sl)])
            P.dma("act", yab[g * 4:(g + 1) * 4, :, tsl].rearrange("j p t -> p j t"), yaT[sl][:], R=[("yaT", sl)], W=["yab"], sk=("yaT", sl))
            if tt < 15:
                if tt == 0:
                    P.op("dve", lambda e: e.tensor_copy(out=carry[:], in_=cx.ps[5]), R=[("ps", 5)], W=["carry"])
                else:
                    P.op("dve", lambda e: e.tensor_tensor(out=v3(carry[:], 8), in0=v3(carry[:], 8),
                                                          in1=bc_last(cdb[:, tt, g8], 64), op=ALU.mult), R=["carry", "cdb"], W=["carry"])
                    P.op("dve", lambda e: e.tensor_tensor(out=carry[:], in0=carry[:], in1=cx.ps[5], op=ALU.add), R=["carry", ("ps", 5)], W=["carry"])
                P.op("act", lambda e: e.copy(out=prevb[:], in_=carry[:]), R=["carry"], W=["prevb"])

        ssd_front(0)
        for tt in range(16):
            if tt + 1 < 16:
                ssd_front(tt + 1)
            ssd_back(tt)
        P.barrier()
    P.release()


def hybrid_sublayer(P, cx, xT, yT, vecs, rows_d, pos_d, win_d, wout_d, yab, triu, acs_d, psw_d, ms_d, abc):
    Acol, Bcol, Ccol = abc[:, 0:16], abc[:, 16:32], abc[:, 32:48]
    P.mark()
    hT = P.sb([128, 16, 2048], BF16, "hT")
    for half in range(2):
        phase_prenorm(P, cx, xT, half * 1024, 1024, hT[:, :, half * 1024:(half + 1) * 1024], Acol, Bcol)
    import os
    HS = int(os.environ.get("HS", "9"))
    if HS >= 2:
        hybrid_attention(P, cx, hT, vecs, pos_d, win_d, yab, psw_d, ms_d, HS)
    if HS >= 5:
        hybrid_ssd(P, cx, hT, vecs, rows_d, win_d, yab, triu, acs_d, HS)
    P.release()
    if HS < 9:
        return
    for half in range(2):
        t0 = half * 1024
        P.mark()
        cat = P.sb([128, 32, 1024], BF16, "cat")
        phase_prenorm(P, cx, yab, t0, 1024, cat[:, 0:16, :], vecs[:, VOFF["hyb_norm_g"]:VOFF["hyb_norm_g"] + 16],
                      vecs[:, VOFF["zero"]:VOFF["zero"] + 16], hkey="cat", sdt=BF16, xkey="yab")
        for hh in range(16):
            P.dma("sp", cat[:, 16 + hh, :], yab[16 + hh, :, t0:t0 + 1024], W=[("cat", 16 + hh)], sk=("cat", hh % 4))
        P.barrier()
        down_proj(P, cx, wout_d, 32, cat, yT, t0, 1024, "cat")
        phase_post(P, cx, xT, yT, t0, 1024, Ccol)
        P.release()

from concourse.bass_utils import run_bass_kernel_spmd

ROPE_THETA = 10000.0


def build_program(dbg=False, stop=99):
    nc = bass.Bass("TRN2", target_bir_lowering=False)
    P = Prog(nc)
    cx = Ctx()
    dt_in = lambda name, shape, dtype=F32: nc.dram_tensor(name, list(shape), dtype, kind="ExternalInput").ap()
    x_d = dt_in("x", [S, D])
    vec_d = dt_in("vecs", [128, NVEC])
    rows_d = dt_in("rows", [1, NROW])
    pos_d = dt_in("pos", [1, S], I32)
    wmod_d = dt_in("w_mod", [2, D, 18432])
    wg_d = dt_in("ffn_w_gate", [2, 2, D, FF])
    wu_d = dt_in("ffn_w_up", [2, 2, D, FF])
    wd_d = dt_in("ffn_w_down", [2, 2, FF, D])
    hin_d = dt_in("hyb_w_in", [D, 11296])
    hout_d = dt_in("hyb_w_out", [4096, D])
    sin_d = dt_in("sgu_w_in", [D, 8192])
    sout_d = dt_in("sgu_w_out", [4096, D])
    wsT_d = dt_in("wsT", [128, 8, 128])
    ms_d = dt_in("ms", [2, 128, 2048])
    psw_d = dt_in("psw", [128, 128])
    triu_d = dt_in("triu", [128, 128])
    out_d = nc.dram_tensor("out", [S, D], F32, kind="ExternalOutput").ap()
    xT = nc.dram_tensor("xT", [16, 128, S], F32).ap()
    yT = nc.dram_tensor("yT", [16, 128, S], F32).ap()
    yab = nc.dram_tensor("yab", [32, 128, S], BF16).ap()
    acs_d = nc.dram_tensor("acs_d", [32, 16, 128], F32).ap()
    dbgs = []
    setup_common(P, nc, cx)
    vecs = P.sb([128, NVEC], F32, "vecs")
    triu = P.sb([128, 128], F32, "triu")
    mods = P.sb([128, 288], F32, "mods")
    abc = P.sb([128, 48], F32, "abc")
    P.dma("sp", vecs[:], vec_d, W=["vecs"], sk="vecs")
    P.dma("sp", triu[:], triu_d, W=["triu"], sk="triu")
    P.barrier()
    phase_x_in(P, cx, x_d, xT)
    phase_mod(P, cx, vecs, wmod_d, mods)
    step = 0

    def snap():
        nonlocal step
        if dbg:
            d = nc.dram_tensor("dbg%d" % step, [16, 128, S], F32, kind="ExternalOutput").ap()
            P.dma("sp", d, xT, sk="dbg")
            P.barrier()
        step += 1
        return step >= stop

    done = False
    for l in range(2):
        if done:
            break
        sub_vectors(P, cx, vecs, mods, abc, l, 0, 0.5)
        ffn_sublayer(P, cx, xT, yT, wg_d[l, 0], wu_d[l, 0], wd_d[l, 0], abc[:, 0:16], abc[:, 16:32], abc[:, 32:48])
        if snap():
            break
        sub_vectors(P, cx, vecs, mods, abc, l, 1, 1.0)
        if l == 0:
            hybrid_sublayer(P, cx, xT, yT, vecs, rows_d, pos_d, hin_d, hout_d, yab, triu, acs_d, psw_d, ms_d, abc)
        else:
            sgu_sublayer(P, cx, xT, yT, vecs, rows_d, sin_d, wsT_d, sout_d, triu, abc)
        if snap():
            break
        sub_vectors(P, cx, vecs, mods, abc, l, 2, 0.5)
        ffn_sublayer(P, cx, xT, yT, wg_d[l, 1], wu_d[l, 1], wd_d[l, 1], abc[:, 0:16], abc[:, 16:32], abc[:, 32:48])
        if snap():
            break
    phase_x_out(P, cx, xT, out_d)
    P.finalize()
    return nc


def _lay(v):
    return np.ascontiguousarray(np.asarray(v, np.float32).reshape(-1, 128).T)


def _attn_masks():
    k = np.arange(128)[:, None]
    i = np.arange(128)[None, :]
    diff = i - k
    ms = np.zeros((2, 128, 16, 128), np.float32)
    for w in range(2):
        for e in range(16):
            if e == 0:
                if w == 1:
                    continue
                m = (diff >= 0).astype(np.float32) + ((diff >= 0) & (diff <= 32)) + ((diff >= 0) & (diff <= 8))
            elif e % 4 == 0:
                m = ((diff >= w) & (diff <= w + 31)).astype(np.float32) + ((diff >= w) & (diff <= w + 7))
            else:
                m = ((diff >= w) & (diff <= w + 7)).astype(np.float32)
            ms[w, :, e, :] = m
    return np.ascontiguousarray(ms.reshape(2, 128, 2048))


def make_in_maps(inp, cores):
    f = lambda k: np.asarray(inp[k], np.float32)
    invf = (ROPE_THETA ** (-(np.arange(128) % 64).astype(np.float64) / 64.0)).astype(np.float32)[:, None]
    shared_vec = [np.concatenate([_lay(f("b_mod")[l]) for l in range(2)], axis=1),
                  np.concatenate([_lay(f("norm_pre")[l, s]) for l in range(2) for s in range(3)], axis=1),
                  np.concatenate([_lay(f("norm_post")[l, s]) for l in range(2) for s in range(3)], axis=1),
                  np.concatenate([_lay(f("hyb_conv_w")[0, j]) for j in range(4)], axis=1),
                  _lay(f("hyb_conv_b")[0]), _lay(f("hyb_norm_g")[0]), _lay(f("sgu_b_in")[0]), invf, np.zeros((128, 16), np.float32)]
    rows = np.concatenate([f("hyb_dt_bias")[0], f("hyb_a_log")[0], f("hyb_d_skip")[0], f("sgu_ln_g")[0], f("sgu_ln_b")[0],
                           f("sgu_b_spatial")[0].reshape(-1)])[None, :].astype(np.float32)
    psw = np.zeros((128, 128), np.float32)
    for m in range(64):
        psw[m + 64, m] = -1.0
        psw[m, m + 64] = 1.0
    shared = dict(rows=np.ascontiguousarray(rows), w_mod=f("w_mod"), ffn_w_gate=f("ffn_w_gate"), ffn_w_up=f("ffn_w_up"),
                  ffn_w_down=f("ffn_w_down"), hyb_w_in=f("hyb_w_in")[0], hyb_w_out=f("hyb_w_out")[0], sgu_w_in=f("sgu_w_in")[0],
                  sgu_w_out=f("sgu_w_out")[0], wsT=np.ascontiguousarray(f("sgu_w_spatial")[0].transpose(2, 0, 1)),
                  ms=_attn_masks(), psw=psw, triu=np.triu(np.ones((128, 128), np.float32)), ident=np.eye(128, dtype=np.float32))
    maps = []
    for b in cores:
        vecs = np.concatenate([_lay(f("c")[b])] + shared_vec, axis=1)
        assert vecs.shape == (128, NVEC), vecs.shape
        m = dict(shared)
        m["x"] = np.ascontiguousarray(f("x")[b])
        m["vecs"] = np.ascontiguousarray(vecs)
        m["pos"] = np.ascontiguousarray(np.asarray(inp["positions"])[b:b + 1].astype(np.int32))
        maps.append(m)
    return maps


_NC = None


def kernel(**inp):
    global _NC
    if _NC is None:
        _NC = build_program()
    maps = make_in_maps(inp, list(range(8)))
    res = run_bass_kernel_spmd(_NC, maps, core_ids=list(range(8)))
    return np.stack([np.asarray(r["out"], np.float32) for r in res.results], axis=0)
```

```python
import numpy as np
import concourse.bass as bass
import concourse.mybir as mybir

F32 = mybir.dt.float32
BF16 = mybir.dt.bfloat16
I32 = mybir.dt.int32
AF = mybir.ActivationFunctionType
ALU = mybir.AluOpType
AX = mybir.AxisListType
CE = ("pe", "act", "dve", "pool")
ALLE = ("pe", "act", "dve", "pool", "sp")


class _Op:
    __slots__ = ("fn", "deps", "inc", "semval", "dma")

    def __init__(self, fn, deps, dma=None):
        self.fn = fn
        self.deps = deps
        self.inc = False
        self.semval = 0
        self.dma = dma


class Prog:
    def __init__(self, nc, n_dma_sems=80):
        self.nc = nc
        self.ops = {e: [] for e in ALLE}
        self.esem = {e: nc.alloc_semaphore("es_" + e) for e in CE}
        self.dpool = [nc.alloc_semaphore("ds%d" % i) for i in range(n_dma_sems)]
        self.dcount = {id(s): 0 for s in self.dpool}
        self.dfree = list(self.dpool)
        self.dmap = {}
        self.lastw = {}
        self.readers = {}
        self.sb_off = 16640
        self.sb_marks = []
        self.uid = 0

    def sb(self, shape, dtype, name=None):
        self.uid += 1
        t = self.nc.alloc_sbuf_tensor_at("%s_%d" % (name or "t", self.uid), list(shape), dtype, offset=self.sb_off)
        nb = int(np.prod(shape[1:])) * mybir.dt.size(dtype)
        nb = (nb + 63) // 64 * 64
        self.sb_off += nb
        assert self.sb_off <= 196608, ("SBUF overflow", self.sb_off, name)
        return t

    def mark(self):
        self.sb_marks.append(self.sb_off)

    def release(self):
        self.sb_off = self.sb_marks.pop()

    def _collect(self, R, W):
        deps = []
        for k in R:
            t = self.lastw.get(k)
            if t is not None:
                deps.append(t)
        for k in W:
            t = self.lastw.get(k)
            if t is not None:
                deps.append(t)
            r = self.readers.get(k)
            if r:
                deps.extend(r.values())
        return deps

    def _update(self, R, W, tok, who):
        for k in R:
            self.readers.setdefault(k, {})[who] = tok
        for k in W:
            self.lastw[k] = tok
            self.readers[k] = {}

    def op(self, eng, fn, R=(), W=()):
        if eng != "pe":
            pk = [k for k in R if isinstance(k, tuple) and k[0] in ("ps", "po")]
            if pk:
                R = [k for k in R if k not in pk]
                W = list(W) + pk
        deps = self._collect(R, W)
        o = _Op(fn, deps)
        self.ops[eng].append(o)
        tok = ("e", eng, len(self.ops[eng]) - 1)
        self._update(R, W, tok, eng)
        return tok

    def _dsem(self, key):
        s = self.dmap.get(key)
        if s is None:
            assert self.dfree, "out of DMA semaphores"
            s = self.dfree.pop(0)
            self.dmap[key] = s
        return s

    def dma(self, q, out, in_, R=(), W=(), sk=None, **kw):
        assert sk is not None
        s = self._dsem(sk)
        self.dcount[id(s)] += 16
        v = self.dcount[id(s)]
        deps = self._collect(R, W)
        o = _Op(lambda e: e.dma_start(out=out, in_=in_, **kw), deps, dma=(s, v))
        self.ops[q].append(o)
        tok = ("d", s, v)
        self._update(R, W, tok, ("d", id(s)))
        return tok

    def barrier(self):
        deps = []
        for e in CE:
            if self.ops[e]:
                for i in range(len(self.ops[e]) - 1, -1, -1):
                    if self.ops[e][i].dma is None and self.ops[e][i].fn is not None:
                        deps.append(("e", e, i))
                        break
        for s in self.dpool:
            if self.dcount[id(s)] > 0:
                deps.append(("d", s, self.dcount[id(s)]))
        for e in ALLE:
            self.ops[e].append(_Op(None, list(deps)))
        self.lastw = {}
        self.readers = {}
        self.dmap = {}
        self.dfree = list(self.dpool)

    def finalize(self):
        nc = self.nc
        for e in ALLE:
            for o in self.ops[e]:
                for d in o.deps:
                    if d[0] == "e":
                        self.ops[d[1]][d[2]].inc = True
        for e in CE:
            c = 0
            for o in self.ops[e]:
                if o.inc:
                    c += 1
                    o.semval = c
        ops = self.ops
        esem = self.esem
        stats = {}

        def replay(ename):
            def run(eng):
                seen = {}
                nw = 0
                for o in ops[ename]:
                    for d in o.deps:
                        if d[0] == "e":
                            if d[1] == "pe" and ename == "pe":
                                continue
                            s = esem[d[1]]
                            v = ops[d[1]][d[2]].semval
                            k = d[1]
                        else:
                            s = d[1]
                            v = d[2]
                            k = id(s)
                        if seen.get(k, 0) >= v:
                            continue
                        seen[k] = v
                        eng.wait_ge(s, v)
                        nw += 1
                    if o.fn is None:
                        continue
                    ins = o.fn(eng)
                    if o.dma is not None:
                        ins.then_inc(o.dma[0], 16)
                    elif o.inc:
                        ins.then_inc(esem[ename], 1)
                stats[ename] = (len(ops[ename]), nw)
            return run

        with nc.Block() as block:
            block.tensor(replay("pe"))
            block.scalar(replay("act"))
            block.vector(replay("dve"))
            block.gpsimd(replay("pool"))
            block.sync(replay("sp"))
        return stats

S = 2048
D = 2048
KC = 16
FF = 5632
FCN = 44
TB = 1024
EPS = 1e-6


class Ctx:
    pass


def setup_common(P, nc, cx):
    cx.psall = nc.alloc_psum_tensor("psall", [128, 4096], F32)
    cx.ps = [cx.psall[:, i * 512:(i + 1) * 512] for i in range(8)]
    cx.ident_d = nc.dram_tensor("ident", [128, 128], F32, kind="ExternalInput").ap()
    cx.ident = P.sb([128, 128], F32, "ident")
    cx.identb = P.sb([128, 128], BF16, "identb")
    cx.onesb = P.sb([128, 128], BF16, "onesb")
    cx.epsc = P.sb([128, 1], F32, "epsc")
    P.dma("sp", cx.ident[:], cx.ident_d, W=["ident"], sk="ident")
    P.op("dve", lambda e: e.tensor_copy(out=cx.identb[:], in_=cx.ident[:]), R=["ident"], W=["identb"])
    P.op("dve", lambda e: e.memset(cx.onesb[:], 1.0), W=["onesb"])
    P.op("dve", lambda e: e.memset(cx.epsc[:], EPS), W=["epsc"])
    cx.rr = 0


def psb(cx, i, n=512):
    return cx.ps[i][:, 0:n]


def phase_x_in(P, cx, x_d, xT):
    P.mark()
    xin = [P.sb([128, 2048], F32, "xin") for _ in range(2)]
    xo = [P.sb([128, 16, 128], F32, "xo") for _ in range(2)]
    xTv = xT.rearrange("c p t -> p c t")
    for tt in range(16):
        sl = tt % 2
        P.dma("sp", xin[sl][:], x_d[tt * 128:(tt + 1) * 128, :], W=[("xin", sl)], sk=("xin", sl))
        for cb in range(4):
            bank = (tt * 4 + cb) % 4
            for j in range(4):
                c = cb * 4 + j
                P.op("pe", lambda e, bank=bank, j=j, c=c, sl=sl: e.transpose(
                    out=cx.ps[bank][:, j * 128:(j + 1) * 128], in_=xin[sl][:, c * 128:(c + 1) * 128], identity=cx.ident[:]),
                    R=[("xin", sl), "ident"], W=[("ps", bank)])
            eng = "act" if cb % 2 else "dve"
            if eng == "dve":
                P.op("dve", lambda e, bank=bank, cb=cb, sl=sl: e.tensor_copy(
                    out=xo[sl][:, cb * 4:(cb + 1) * 4, :], in_=cx.ps[bank][:].rearrange("p (a b) -> p a b", a=4)),
                    R=[("ps", bank)], W=[("xo", sl, cb)])
            else:
                P.op("act", lambda e, bank=bank, cb=cb, sl=sl: e.copy(
                    out=xo[sl][:, cb * 4:(cb + 1) * 4, :], in_=cx.ps[bank][:].rearrange("p (a b) -> p a b", a=4)),
                    R=[("ps", bank)], W=[("xo", sl, cb)])
        P.dma("sp", xTv[:, :, tt * 128:(tt + 1) * 128], xo[sl][:],
              R=[("xo", sl, cb) for cb in range(4)], W=[("xT", c, tt // 8) for c in range(16)], sk=("xo", sl))
    P.barrier()
    P.release()


def phase_x_out(P, cx, xT, out_d):
    P.mark()
    xin = [P.sb([128, 16, 128], F32, "xin") for _ in range(2)]
    xo = [P.sb([128, 2048], F32, "xo") for _ in range(2)]
    xTv = xT.rearrange("c p t -> p c t")
    for tt in range(16):
        sl = tt % 2
        P.dma("sp", xin[sl][:], xTv[:, :, tt * 128:(tt + 1) * 128], R=[("xT", c, tt // 8) for c in range(16)],
              W=[("xin", sl)], sk=("xin", sl))
        for cb in range(4):
            bank = (tt * 4 + cb) % 4
            for j in range(4):
                c = cb * 4 + j
                P.op("pe", lambda e, bank=bank, j=j, c=c, sl=sl: e.transpose(
                    out=cx.ps[bank][:, j * 128:(j + 1) * 128], in_=xin[sl][:, c, :], identity=cx.ident[:]),
                    R=[("xin", sl), "ident"], W=[("ps", bank)])
            if cb % 2 == 0:
                P.op("dve", lambda e, bank=bank, cb=cb, sl=sl: e.tensor_copy(
                    out=xo[sl][:, cb * 512:(cb + 1) * 512], in_=cx.ps[bank][:]),
                    R=[("ps", bank)], W=[("xo", sl, cb)])
            else:
                P.op("act", lambda e, bank=bank, cb=cb, sl=sl: e.copy(
                    out=xo[sl][:, cb * 512:(cb + 1) * 512], in_=cx.ps[bank][:]),
                    R=[("ps", bank)], W=[("xo", sl, cb)])
        P.dma("sp", out_d[tt * 128:(tt + 1) * 128, :], xo[sl][:],
              R=[("xo", sl, cb) for cb in range(4)], W=[("out", tt)], sk=("xo", sl))
    P.barrier()
    P.release()


def rstd_from_ss(P, cx, ps_ss_keys, rstd, key, n=TB, width=D):
    nb = n // 512
    for b in range(nb):
        P.op("act", lambda e, b=b: e.activation(out=rstd[:, b * 512:(b + 1) * 512], in_=cx.ps[4 + b][:], func=AF.Ln,
                                                bias=cx.epsc[:], scale=1.0 / width),
             R=[("ps", 4 + b), "epsc"], W=[(key, b)])
        P.op("act", lambda e, b=b: e.activation(out=rstd[:, b * 512:(b + 1) * 512], in_=rstd[:, b * 512:(b + 1) * 512],
                                                func=AF.Exp, scale=-0.5),
             R=[(key, b)], W=[(key, b)])


def phase_prenorm(P, cx, xT, t0, n, hT, Acol, Bcol, hkey="hT", sdt=F32, xkey="xT"):
    P.mark()
    xs = [P.sb([128, n], sdt, "xs") for _ in range(3)]
    sq = [P.sb([128, n], BF16, "sq") for _ in range(2)]
    tmp = [P.sb([128, n], F32, "tmp") for _ in range(2)]
    rstd = P.sb([128, n], F32, "rstd")
    nb = n // 512
    half = t0 // 1024
    for c in range(16):
        sl = c % 3
        P.dma("sp", xs[sl][:], xT[c, :, t0:t0 + n], R=[(xkey, c, half)], W=[("xs", sl)], sk=("xs", sl))
        P.op("act", lambda e, sl=sl, c=c: e.activation(out=sq[c % 2][:], in_=xs[sl][:], func=AF.Square),
             R=[("xs", sl)], W=[("sq", c % 2)])
        for b in range(nb):
            P.op("pe", lambda e, b=b, c=c: e.matmul(cx.ps[4 + b][:], lhsT=cx.onesb[:], rhs=sq[c % 2][:, b * 512:(b + 1) * 512],
                                                    start=(c == 0), stop=(c == 15)),
                 R=[("sq", c % 2), "onesb"], W=[("ps", 4 + b)])
    rstd_from_ss(P, cx, None, rstd, "rstd", n)
    for c in range(16):
        sl = c % 3
        P.dma("sp", xs[sl][:], xT[c, :, t0:t0 + n], R=[(xkey, c, half)], W=[("xs", sl)], sk=("xs", sl))
        P.op("dve", lambda e, sl=sl, c=c: e.scalar_tensor_tensor(out=tmp[c % 2][:], in0=xs[sl][:], scalar=Acol[:, c:c + 1],
                                                                 in1=rstd[:], op0=ALU.mult, op1=ALU.mult),
             R=[("xs", sl), ("rstd", 0), ("rstd", 1), "vecs"], W=[("tmp", c % 2)])
        P.op("act", lambda e, c=c: e.activation(out=hT[:, c, 0:n], in_=tmp[c % 2][:], func=AF.Identity,
                                                bias=Bcol[:, c:c + 1], scale=1.0),
             R=[("tmp", c % 2), "vecs"], W=[(hkey, c)])
    P.barrier()
    P.release()


def phase_post(P, cx, xT, yT, t0, n, Ccol):
    P.mark()
    xs = [P.sb([128, n], F32, "xs") for _ in range(2)]
    ys = [P.sb([128, n], F32, "ys") for _ in range(2)]
    tm = [P.sb([128, n], F32, "tm") for _ in range(2)]
    xn = [P.sb([128, n], F32, "xn") for _ in range(2)]
    rstd = P.sb([128, n], F32, "rstd")
    half = t0 // 1024
    rstd_from_ss(P, cx, None, rstd, "rstd", n)
    for c in range(16):
        sl = c % 2
        P.dma("sp", ys[sl][:], yT[c, :, t0:t0 + n], R=[("yT", c)], W=[("ys", sl)], sk=("ys", sl))
        P.dma("sp", xs[sl][:], xT[c, :, t0:t0 + n], R=[("xT", c, half)], W=[("xs", sl)], sk=("xs", sl))
        P.op("dve", lambda e, sl=sl, c=c: e.scalar_tensor_tensor(out=tm[sl][:], in0=ys[sl][:], scalar=Ccol[:, c:c + 1],
                                                                 in1=rstd[:], op0=ALU.mult, op1=ALU.mult),
             R=[("ys", sl), ("rstd", 0), ("rstd", 1), "vecs"], W=[("tm", sl)])
        P.op("dve", lambda e, sl=sl: e.tensor_tensor(out=xn[sl][:], in0=tm[sl][:], in1=xs[sl][:], op=ALU.add),
             R=[("tm", sl), ("xs", sl)], W=[("xn", sl)])
        P.dma("act", xT[c, :, t0:t0 + n], xn[sl][:], R=[("xn", sl)], W=[("xT", c, half)], sk=("xn", sl))
    P.barrier()
    P.release()


def down_proj(P, cx, w_d, nfc, actT, yT, t0, n, akey):
    P.mark()
    nq = 4
    qs = [(nfc * q) // nq for q in range(nq + 1)]
    qmax = max(qs[q + 1] - qs[q] for q in range(nq))
    wst = [P.sb([128, qmax, 128], F32, "wdst") for _ in range(nq)]
    wbf = [P.sb([128, nfc, 128], BF16, "wdbf") for _ in range(2)]
    ysb = [P.sb([128, 512], F32, "ysb") for _ in range(2)]
    sq = [P.sb([128, 512], BF16, "sq") for _ in range(2)]
    wv = w_d.rearrange("(fc p) d -> p fc d", p=128)
    nb = n // 512
    it = 0
    def load_wd(dc):
        sl = dc % 2
        for q in range(nq):
            a, bb = qs[q], qs[q + 1]
            P.dma("sp", wst[q][:, 0:bb - a, :], wv[:, a:bb, dc * 128:(dc + 1) * 128], W=[("wdst", q)], sk=("wdst", q))
            if q % 2 == 0:
                P.op("dve", lambda e, sl=sl, a=a, bb=bb, q=q: e.tensor_copy(out=wbf[sl][:, a:bb, :], in_=wst[q][:, 0:bb - a, :]),
                     R=[("wdst", q)], W=[("wdbf", sl, q)])
            else:
                P.op("act", lambda e, sl=sl, a=a, bb=bb, q=q: e.copy(out=wbf[sl][:, a:bb, :], in_=wst[q][:, 0:bb - a, :]),
                     R=[("wdst", q)], W=[("wdbf", sl, q)])

    load_wd(0)
    for dc in range(16):
        sl = dc % 2
        if dc + 1 < 16:
            load_wd(dc + 1)
        for b in range(nb):
            bank = it % 4
            s2 = it % 2
            it += 1
            for fc in range(nfc):
                P.op("pe", lambda e, bank=bank, fc=fc, sl=sl, b=b: e.matmul(
                    cx.ps[bank][:], lhsT=wbf[sl][:, fc, :], rhs=actT[:, fc, b * 512:(b + 1) * 512],
                    start=(fc == 0), stop=(fc == nfc - 1)),
                    R=[("wdbf", sl, min(nq - 1, (fc * nq) // nfc)), (akey, fc)], W=[("ps", bank)])
            P.op("dve", lambda e, bank=bank, s2=s2: e.tensor_copy(out=ysb[s2][:], in_=cx.ps[bank][:]),
                 R=[("ps", bank)], W=[("ysb", s2)])

            P.op("act", lambda e, bank=bank, s2=s2: e.activation(out=sq[s2][:], in_=ysb[s2][:], func=AF.Square),
                 R=[("ysb", s2)], W=[("sq", s2)])
            P.op("pe", lambda e, s2=s2, b=b, dc=dc: e.matmul(cx.ps[4 + b][:], lhsT=cx.onesb[:], rhs=sq[s2][:],
                                                             start=(dc == 0), stop=(dc == 15)),
                 R=[("sq", s2), "onesb"], W=[("ps", 4 + b)])
            P.dma("sp", yT[dc, :, t0 + b * 512:t0 + (b + 1) * 512], ysb[s2][:], R=[("ysb", s2)], W=[("yT", dc)], sk=("ysb", s2))
    P.barrier()
    P.release()


def ffn_phaseA(P, cx, hT, actT, wg_d, wu_d):
    P.mark()
    wst = [[P.sb([128, 16, 128], F32, "wst") for _ in range(2)] for _ in range(2)]
    wbf = [[P.sb([128, 16, 128], BF16, "wbf") for _ in range(2)] for _ in range(2)]
    sg = [P.sb([128, 512], BF16, "sg") for _ in range(2)]
    wviews = [wg_d.rearrange("(kc p) f -> p kc f", p=128), wu_d.rearrange("(kc p) f -> p kc f", p=128)]
    it = 0

    def load_w(fc):
        sl = fc % 2
        for m in range(2):
            P.dma("sp", wst[m][sl][:], wviews[m][:, :, fc * 128:(fc + 1) * 128],
                  W=[("wst", m, sl)], sk=("wst", m, sl))
            if m == 0:
                P.op("dve", lambda e, m=m, sl=sl: e.tensor_copy(out=wbf[m][sl][:], in_=wst[m][sl][:]),
                     R=[("wst", m, sl)], W=[("wbf", m, sl)])
            else:
                P.op("act", lambda e, m=m, sl=sl: e.copy(out=wbf[m][sl][:], in_=wst[m][sl][:]),
                     R=[("wst", m, sl)], W=[("wbf", m, sl)])

    load_w(0)
    for fc in range(FCN):
        sl = fc % 2
        if fc + 1 < FCN:
            load_w(fc + 1)
        for b in range(2):
            bg = (it % 2) * 2
            bu = bg + 1
            it += 1
            for m, bank in ((0, bg), (1, bu)):
                for kc in range(16):
                    P.op("pe", lambda e, m=m, bank=bank, kc=kc, sl=sl, b=b: e.matmul(
                        cx.ps[bank][:], lhsT=wbf[m][sl][:, kc, :], rhs=hT[:, kc, b * 512:(b + 1) * 512],
                        start=(kc == 0), stop=(kc == 15)),
                        R=[("wbf", m, sl), ("hT", kc)], W=[("ps", bank)])
            s2 = it % 2
            P.op("act", lambda e, bg=bg, s2=s2: e.activation(out=sg[s2][:], in_=cx.ps[bg][:], func=AF.Silu),
                 R=[("ps", bg)], W=[("sg", s2)])
            P.op("dve", lambda e, bu=bu, s2=s2, fc=fc, b=b: e.tensor_tensor(
                out=actT[:, fc, b * 512:(b + 1) * 512], in0=sg[s2][:], in1=cx.ps[bu][:], op=ALU.mult),
                R=[("sg", s2), ("ps", bu)], W=[("actT", fc)])
    P.barrier()
    P.release()


def ffn_sublayer(P, cx, xT, yT, wg_d, wu_d, wd_d, Acol, Bcol, Ccol):
    for half in range(2):
        t0 = half * TB
        P.mark()
        actT = P.sb([128, FCN, TB], BF16, "actT")
        P.mark()
        hT = P.sb([128, 16, TB], BF16, "hT")
        phase_prenorm(P, cx, xT, t0, TB, hT, Acol, Bcol)
        ffn_phaseA(P, cx, hT, actT, wg_d, wu_d)
        P.release()
        down_proj(P, cx, wd_d, FCN, actT, yT, t0, TB, "actT")
        phase_post(P, cx, xT, yT, t0, TB, Ccol)
        P.release()

OZ, OX, ODT, OQ, OKK, OV = 0, 2048, 5120, 5152, 7200, 9248
VEC_LAYOUT = [("c", 16), ("b_mod", 288), ("norm_pre", 96), ("norm_post", 96), ("conv_w", 96), ("conv_b", 24),
              ("hyb_norm_g", 16), ("sgu_b_in", 64), ("invf", 1), ("zero", 16)]
VOFF = {}
_o = 0
for _n, _w in VEC_LAYOUT:
    VOFF[_n] = _o
    _o += _w
NVEC = _o
ROW_LAYOUT = [("dt_bias", 32), ("a_log", 32), ("d_skip", 32), ("ln_g", 4096), ("ln_b", 4096), ("b_sp", 1024)]
ROFF = {}
_o = 0
for _n, _w in ROW_LAYOUT:
    ROFF[_n] = _o
    _o += _w
NROW = _o


def bc_mid(ap, reps):
    a = ap.ap
    return bass.AP(ap.tensor, ap.offset, [list(a[0]), [0, reps], list(a[1])])


def bc_last(ap, reps):
    a = ap.ap
    return bass.AP(ap.tensor, ap.offset, [list(a[0]), list(a[1]), [0, reps]])


class WStream:
    def __init__(self, P, n=2, name="ws", width=128, engines=("dve", "act"), nst=None):
        self.engines = engines
        self.P = P
        self.n = n
        self.nst = nst or n
        self.name = name
        self.st = [P.sb([128, 16, width], F32, name + "st") for _ in range(self.nst)]
        self.bf = [P.sb([128, 16, width], BF16, name + "bf") for _ in range(n)]
        self.i = 0

    def get(self, wview, c0, ncols):
        P = self.P
        sl = self.i % self.n
        ss = self.i % self.nst
        self.i += 1
        st, bf = self.st[ss], self.bf[sl]
        P.dma("sp", st[:, :, 0:ncols], wview[:, :, c0:c0 + ncols], W=[(self.name, "st", ss)], sk=(self.name, ss))
        if self.engines[self.i % len(self.engines)] == "dve":
            P.op("dve", lambda e: e.tensor_copy(out=bf[:, :, 0:ncols], in_=st[:, :, 0:ncols]),
                 R=[(self.name, "st", ss)], W=[(self.name, "bf", sl)])
        else:
            P.op("act", lambda e: e.copy(out=bf[:, :, 0:ncols], in_=st[:, :, 0:ncols]),
                 R=[(self.name, "st", ss)], W=[(self.name, "bf", sl)])
        return bf[:, :, 0:ncols], (self.name, "bf", sl)


def proj_fm(P, cx, wbf, wkey, hT, hkey, ntb, banks, evac):
    for tb in range(ntb):
        bank = banks[cx.rr % len(banks)]
        cx.rr += 1
        for kc in range(16):
            P.op("pe", lambda e, bank=bank, kc=kc, tb=tb: e.matmul(cx.ps[bank], lhsT=wbf[:, kc, :], rhs=hT[:, kc, tb * 512:(tb + 1) * 512],
                                                                  start=(kc == 0), stop=(kc == 15)),
                 R=[wkey, hkey], W=[("ps", bank)])
        evac(bank, tb)


def phase_mod(P, cx, vecs, wmod_d, mods):
    P.mark()
    ca = P.sb([128, 16, 2], F32, "ca")
    sgm = P.sb([128, 16], F32, "sgm")
    cc = vecs[:, VOFF["c"]:VOFF["c"] + 16]
    P.op("act", lambda e: e.activation(out=sgm[:], in_=cc, func=AF.Sigmoid), R=["vecs"], W=["sgm"])
    for j in range(2):
        P.op("dve", lambda e, j=j: e.tensor_tensor(out=ca[:, :, j], in0=sgm[:], in1=cc, op=ALU.mult), R=["sgm", "vecs"], W=["ca"])
    cab = P.sb([128, 16, 2], BF16, "cab")
    P.op("dve", lambda e: e.tensor_copy(out=cab[:], in_=ca[:]), R=["ca"], W=["cab"])
    wst = [P.sb([128, 16, 128], F32, "wm") for _ in range(3)]
    wbf = [P.sb([128, 16, 128], BF16, "wmb") for _ in range(3)]
    for l in range(2):
        wv = wmod_d[l].rearrange("(kc p) n -> p kc n", p=128)
        for j in range(144):
            it = l * 144 + j
            sl = it % 3
            P.dma("sp", wst[sl][:], wv[:, :, j * 128:(j + 1) * 128], W=[("wm", sl)], sk=("wm", sl))
            if it % 2 == 0:
                P.op("dve", lambda e, sl=sl: e.tensor_copy(out=wbf[sl][:], in_=wst[sl][:]), R=[("wm", sl)], W=[("wmb", sl)])
            else:
                P.op("act", lambda e, sl=sl: e.copy(out=wbf[sl][:], in_=wst[sl][:]), R=[("wm", sl)], W=[("wmb", sl)])
            for kc in range(16):
                P.op("pe", lambda e, l=l, j=j, kc=kc, sl=sl: e.matmul(cx.ps[l][:, 2 * j:2 * j + 2], lhsT=wbf[sl][:, kc, :], rhs=cab[:, kc, :],
                                                                     start=(kc == 0), stop=(kc == 15)),
                     R=[("wmb", sl), "cab"], W=[("ps", l)])
        P.op("dve", lambda e, l=l: e.tensor_tensor(out=mods[:, l * 144:(l + 1) * 144],
                                                   in0=cx.ps[l][:, 0:288].rearrange("p (j t) -> p j t", t=2)[:, :, 0],
                                                   in1=vecs[:, VOFF["b_mod"] + l * 144:VOFF["b_mod"] + (l + 1) * 144], op=ALU.add),
             R=[("ps", l), "vecs"], W=["mods"])
    P.barrier()
    P.release()


def sub_vectors(P, cx, vecs, mods, abc, l, s, rw):
    base = l * 144 + s * 48
    gpre = vecs[:, VOFF["norm_pre"] + (l * 3 + s) * 16:VOFF["norm_pre"] + (l * 3 + s + 1) * 16]
    gpost = vecs[:, VOFF["norm_post"] + (l * 3 + s) * 16:VOFF["norm_post"] + (l * 3 + s + 1) * 16]
    P.op("dve", lambda e: e.scalar_tensor_tensor(out=abc[:, 0:16], in0=mods[:, base + 16:base + 32], scalar=1.0, in1=gpre,
                                                 op0=ALU.add, op1=ALU.mult), R=["mods", "vecs"], W=["abc"])
    P.op("dve", lambda e: e.tensor_copy(out=abc[:, 16:32], in_=mods[:, base:base + 16]), R=["mods"], W=["abc"])
    P.op("dve", lambda e: e.scalar_tensor_tensor(out=abc[:, 32:48], in0=mods[:, base + 32:base + 48], scalar=1.0, in1=gpost,
                                                 op0=ALU.add, op1=ALU.mult), R=["mods", "vecs"], W=["abc"])
    if rw != 1.0:
        P.op("dve", lambda e: e.tensor_scalar(out=abc[:, 32:48], in0=abc[:, 32:48], scalar1=float(rw), scalar2=None, op0=ALU.mult),
             R=["abc"], W=["abc"])
    P.barrier()


def gelu_a(P, cx, bank, bias_col, tmps, idx, n=512):
    xb, t1, sg = tmps[idx % len(tmps)]
    k = ("gl", idx % len(tmps))
    P.op("act", lambda e: e.activation(out=xb[:, 0:n], in_=cx.ps[bank][:, 0:n], func=AF.Identity, bias=bias_col, scale=1.0),
         R=[("ps", bank), "vecs"], W=[k + ("xb",)])
    P.op("act", lambda e: e.activation(out=t1[:, 0:n], in_=xb[:, 0:n], func=AF.Square), R=[k + ("xb",)], W=[k + ("t1",)])
    P.op("dve", lambda e: e.tensor_scalar(out=t1[:, 0:n], in0=t1[:, 0:n], scalar1=0.044715, scalar2=1.0, op0=ALU.mult, op1=ALU.add),
         R=[k + ("t1",)], W=[k + ("t1",)])
    P.op("dve", lambda e: e.tensor_tensor(out=t1[:, 0:n], in0=t1[:, 0:n], in1=xb[:, 0:n], op=ALU.mult), R=[k + ("t1",), k + ("xb",)], W=[k + ("t1",)])


def gelu_b(P, cx, out_ap, okey, tmps, idx, n=512):
    xb, t1, sg = tmps[idx % len(tmps)]
    k = ("gl", idx % len(tmps))
    P.op("act", lambda e: e.activation(out=sg[:, 0:n], in_=t1[:, 0:n], func=AF.Sigmoid, scale=1.5957691216057308),
         R=[k + ("t1",)], W=[k + ("sg",)])
    P.op("dve", lambda e: e.tensor_tensor(out=out_ap, in0=xb[:, 0:n], in1=sg[:, 0:n], op=ALU.mult), R=[k + ("xb",), k + ("sg",)], W=[okey])


def sgu_sublayer(P, cx, xT, yT, vecs, rows_d, win_d, wsT_d, wout_d, triu, abc):
    NQ = 512
    Acol, Bcol, Ccol = abc[:, 0:16], abc[:, 16:32], abc[:, 32:48]
    wv = win_d.rearrange("(kc p) n -> p kc n", p=128)
    P.mark()
    wsT = P.sb([128, 8, 128], BF16, "wsT")
    bsp = P.sb([128, 8, 128], F32, "bsp")
    lng = P.sb([128, 4096], BF16, "lng")
    lnb = P.sb([128, 4096], BF16, "lnb")
    P.mark()
    lnf = P.sb([128, 4096], F32, "lnf")
    wsf = P.sb([128, 8, 128], F32, "wsf")
    P.dma("sp", wsf[:], wsT_d, W=["wsf"], sk="wsf")
    P.op("dve", lambda e: e.tensor_tensor(out=wsT[:], in0=wsf[:], in1=bc_mid(triu[:], 8), op=ALU.mult), R=["wsf", "triu"], W=["wsT"])
    P.dma("sp", bsp[:].rearrange("p a b -> p (a b)"), rows_d[0, ROFF["b_sp"]:ROFF["b_sp"] + 1024].partition_broadcast(128), W=["bsp"], sk="bsp")
    P.dma("sp", lnf[:], rows_d[0, ROFF["ln_g"]:ROFF["ln_g"] + 4096].partition_broadcast(128), W=["lnf"], sk="lnf")
    P.op("dve", lambda e: e.tensor_copy(out=lng[:], in_=lnf[:]), R=["lnf"], W=["lng"])
    P.dma("sp", lnf[:], rows_d[0, ROFF["ln_b"]:ROFF["ln_b"] + 4096].partition_broadcast(128), R=[], W=["lnf"], sk="lnf")
    P.op("dve", lambda e: e.tensor_copy(out=lnb[:], in_=lnf[:]), R=["lnf"], W=["lnb"])
    P.barrier()
    P.release()
    bin_off = VOFF["sgu_b_in"]
    def do_quarter(qt):
        t0 = qt * NQ
        P.mark()
        hT = P.sb([128, 16, NQ], BF16, "hT")
        gT = P.sb([128, 32, NQ], BF16, "gT")
        phase_prenorm(P, cx, xT, t0, NQ, hT, Acol, Bcol)
        P.mark()
        vtm = P.sb([128, 4, 4096], BF16, "vtm")
        ws = WStream(P, 2, "sgw", engines=("act",), nst=3)
        tmps = [(P.sb([128, 512], F32, "xb"), P.sb([128, 512], F32, "t1"), P.sb([128, 512], BF16, "sg")) for _ in range(2)]
        vT = [P.sb([128, 512], BF16, "vT") for _ in range(2)]
        ps7b = cx.ps[7].bitcast(BF16)
        nxt = [ws.get(wv, 4096, 128)]

        def v_front(f):
            wbf, wkey = nxt[0]
            if f + 1 < 32:
                nxt[0] = ws.get(wv, 4096 + (f + 1) * 128, 128)
            proj_fm(P, cx, wbf, wkey, hT, "hT", 1, [0, 1],
                    lambda bank, tb: gelu_a(P, cx, bank, vecs[:, bin_off + 32 + f:bin_off + 33 + f], tmps, f))

        def v_mid(f):
            gelu_b(P, cx, vT[f % 2][:], ("vT", f % 2), tmps, f)

        def v_back(f):
            for tt in range(4):
                P.op("pe", lambda e, tt=tt: e.transpose(out=ps7b[:, tt * 128:(tt + 1) * 128], in_=vT[f % 2][:, tt * 128:(tt + 1) * 128],
                                                        identity=cx.identb[:]), R=[("vT", f % 2), "identb"], W=[("ps", 7)])
            P.op("dve", lambda e: e.tensor_copy(out=vtm[:, :, f * 128:(f + 1) * 128],
                                                in_=ps7b[:, 0:512].rearrange("p (a b) -> p a b", a=4)),
                 R=[("ps", 7)], W=[("vtm", tt2) for tt2 in range(4)])

        for f in range(34):
            if f < 32:
                v_front(f)
            if 1 <= f <= 32:
                v_mid(f - 1)
            if f >= 2:
                v_back(f - 2)
        P.mark()
        junk = P.sb([128, 4096], BF16, "junk")
        st = P.sb([128, 16], F32, "lnst")
        for tt in range(4):
            k = ("vtm", tt)
            P.op("act", lambda e, tt=tt: e.activation(out=junk[:], in_=vtm[:, tt, :], func=AF.Identity, accum_out=st[:, 0:1]),
                 R=[k], W=["junk", "lnst"])
            P.op("act", lambda e, tt=tt: e.activation(out=junk[:], in_=vtm[:, tt, :], func=AF.Square, accum_out=st[:, 1:2]),
                 R=[k], W=["junk", "lnst"])
            P.op("dve", lambda e: e.tensor_scalar(out=st[:, 2:3], in0=st[:, 0:1], scalar1=1.0 / 4096, scalar2=None, op0=ALU.mult), R=["lnst"], W=["lnst"])
            P.op("dve", lambda e: e.tensor_tensor(out=st[:, 3:4], in0=st[:, 2:3], in1=st[:, 2:3], op=ALU.mult), R=["lnst"], W=["lnst"])
            P.op("dve", lambda e: e.scalar_tensor_tensor(out=st[:, 4:5], in0=st[:, 1:2], scalar=1.0 / 4096, in1=st[:, 3:4],
                                                         op0=ALU.mult, op1=ALU.subtract), R=["lnst"], W=["lnst"])
            P.op("act", lambda e: e.activation(out=st[:, 5:6], in_=st[:, 4:5], func=AF.Ln, bias=cx.epsc[:], scale=1.0), R=["lnst", "epsc"], W=["lnst"])
            P.op("act", lambda e: e.activation(out=st[:, 6:7], in_=st[:, 5:6], func=AF.Exp, scale=-0.5), R=["lnst"], W=["lnst"])
            P.op("dve", lambda e: e.scalar_tensor_tensor(out=st[:, 7:8], in0=st[:, 2:3], scalar=-1.0, in1=st[:, 6:7], op0=ALU.mult, op1=ALU.mult),
                 R=["lnst"], W=["lnst"])
            P.op("act", lambda e, tt=tt: e.activation(out=junk[:], in_=vtm[:, tt, :], func=AF.Identity, bias=st[:, 7:8], scale=st[:, 6:7]),
                 R=[k, "lnst", "junk"], W=["junk"])
            P.op("dve", lambda e: e.tensor_tensor(out=junk[:], in0=junk[:], in1=lng[:], op=ALU.mult), R=["junk", "lng"], W=["junk"])
            P.op("dve", lambda e, tt=tt: e.tensor_tensor(out=vtm[:, tt, :], in0=junk[:], in1=lnb[:], op=ALU.add), R=["junk", "lnb"], W=[k])
        P.release()
        uT = [P.sb([128, 512], F32, "uT") for _ in range(2)]
        mt = [P.sb([128, 512], F32, "mt") for _ in range(2)]
        nxu = [None]

        def u_front(f):
            if f == 0:
                nxu[0] = ws.get(wv, 0, 128)
            wbf, wkey = nxu[0]
            if f + 1 < 32:
                nxu[0] = ws.get(wv, (f + 1) * 128, 128)
            g = f // 4
            proj_fm(P, cx, wbf, wkey, hT, "hT", 1, [0, 1],
                    lambda bank, tb: gelu_a(P, cx, bank, vecs[:, bin_off + f:bin_off + f + 1], tmps, f))
            mb = 2 + (f % 2)
            for tt in range(4):
                P.op("pe", lambda e, tt=tt: e.matmul(cx.ps[mb][:, tt * 128:(tt + 1) * 128], lhsT=vtm[:, tt, f * 128:(f + 1) * 128],
                                                     rhs=wsT[:, g, :], start=True, stop=True),
                     R=[("vtm", tt), "wsT"], W=[("ps", mb)])
            P.op("dve", lambda e: e.tensor_tensor(out=mt[f % 2][:].rearrange("p (a b) -> p a b", a=4),
                                                  in0=cx.ps[mb].rearrange("p (a b) -> p a b", a=4),
                                                  in1=bc_mid(bsp[:, g, :], 4), op=ALU.add), R=[("ps", mb), "bsp"], W=[("mt", f % 2)])

        def u_back(f):
            gelu_b(P, cx, uT[f % 2][:], ("uT", f % 2), tmps, f)
            P.op("dve", lambda e: e.tensor_tensor(out=gT[:, f, :], in0=mt[f % 2][:], in1=uT[f % 2][:], op=ALU.mult),
                 R=[("mt", f % 2), ("uT", f % 2)], W=[("gT", f)])

        for f in range(33):
            if f < 32:
                u_front(f)
            if f >= 1:
                u_back(f - 1)
        P.barrier()
        P.release()
        down_proj(P, cx, wout_d, 32, gT, yT, t0, NQ, "gT")
        phase_post(P, cx, xT, yT, t0, NQ, Ccol)
        P.release()

    for qt in range(4):
        do_quarter(qt)
    P.release()

PI = 3.141592653589793


def rope_tables(P, cx, pos_d, invf_col, cos_t, sin_t):
    P.mark()
    pi_ = P.sb([128, 2048], I32, "posi")
    ang = P.sb([128, 2048], F32, "ang")
    ki = pi_
    kf = P.sb([128, 2048], F32, "kf")
    r = P.sb([128, 2048], F32, "r")
    m = kf
    P.dma("sp", pi_[:], pos_d[0, :].partition_broadcast(128), W=["posi"], sk="posi")
    P.op("dve", lambda e: e.tensor_copy(out=ang[:], in_=pi_[:]), R=["posi"], W=["ang"])
    P.op("dve", lambda e: e.tensor_scalar(out=ang[:], in0=ang[:], scalar1=invf_col, scalar2=None, op0=ALU.mult), R=["ang", "vecs"], W=["ang"])
    for which, dst in ((0, sin_t), (1, cos_t)):
        sh = 0.0 if which == 0 else PI / 2
        P.op("dve", lambda e, sh=sh: e.tensor_scalar(out=ki[:], in0=ang[:], scalar1=sh, scalar2=1.0 / (2 * PI), op0=ALU.add, op1=ALU.mult),
             R=["ang", "posi"], W=["ki", "posi"])
        P.op("dve", lambda e: e.tensor_copy(out=kf[:], in_=ki[:]), R=["ki", "m"], W=["kf", "m"])
        P.op("dve", lambda e: e.scalar_tensor_tensor(out=r[:], in0=kf[:], scalar=-2 * PI, in1=ang[:], op0=ALU.mult, op1=ALU.add),
             R=["kf", "ang"], W=["r"])
        if sh != 0.0:
            P.op("dve", lambda e, sh=sh: e.tensor_scalar(out=r[:], in0=r[:], scalar1=sh, scalar2=None, op0=ALU.add), R=["r"], W=["r"])
        P.op("dve", lambda e: e.tensor_scalar(out=m[:], in0=r[:], scalar1=PI, scalar2=None, op0=ALU.is_gt), R=["r", "kf"], W=["m", "kf"])
        P.op("dve", lambda e: e.scalar_tensor_tensor(out=r[:], in0=m[:], scalar=-2 * PI, in1=r[:], op0=ALU.mult, op1=ALU.add), R=["m", "r"], W=["r"])
        P.op("dve", lambda e: e.tensor_scalar(out=m[:], in0=r[:], scalar1=-PI, scalar2=None, op0=ALU.is_lt), R=["r", "kf"], W=["m", "kf"])
        P.op("dve", lambda e: e.scalar_tensor_tensor(out=r[:], in0=m[:], scalar=2 * PI, in1=r[:], op0=ALU.mult, op1=ALU.add), R=["m", "r"], W=["r"])
        P.op("dve", lambda e: e.tensor_scalar(out=r[:], in0=r[:], scalar1=PI, scalar2=-PI, op0=ALU.min, op1=ALU.max), R=["r"], W=["r"])
        P.op("act", lambda e, dst=dst: e.activation(out=dst[:], in_=r[:], func=AF.Sin), R=["r"], W=["trig"])
    P.barrier()
    P.release()


def hybrid_attention(P, cx, hT, vecs, pos_d, win_d, yab, psw_d, ms_d, HS=9):
    wv = win_d.rearrange("(kc p) n -> p kc n", p=128)
    P.mark()
    cos_t = P.sb([128, 2048], F32, "cos")
    sin_t = P.sb([128, 2048], F32, "sin")
    rope_tables(P, cx, pos_d, vecs[:, VOFF["invf"]:VOFF["invf"] + 1], cos_t, sin_t)
    if HS == 2:
        P.release()
        return
    psw = P.sb([128, 128], F32, "psw")
    ms = [P.sb([128, 16, 128], BF16, "ms") for _ in range(2)]
    P.mark()
    msf = P.sb([128, 2048], F32, "msf")
    P.dma("sp", psw[:], psw_d, W=["psw"], sk="psw")
    for w in range(2):
        P.dma("sp", msf[:], ms_d[w], W=["msf"], sk="msf")
        P.op("dve", lambda e, w=w: e.tensor_copy(out=ms[w][:].rearrange("p a b -> p (a b)"), in_=msf[:]), R=["msf"], W=["ms"])
    P.barrier()
    P.release()
    ws = WStream(P, 2, "aw")
    q32 = P.sb([128, 2048], F32, "q32")
    t1 = P.sb([128, 2048], F32, "t1")
    t2 = P.sb([128, 2048], F32, "t2")
    qk = [P.sb([128, 16, 128], BF16, "qr"), P.sb([128, 16, 128], BF16, "kr")]
    vaug = P.sb([128, 16, 132], BF16, "vaug")
    ybh = [P.sb([128, 2048], BF16, "ybh") for _ in range(2)]
    E = [P.sb([128, 512], BF16, "E") for _ in range(2)]
    PT = [P.sb([128, 4, 128], BF16, "PT") for _ in range(4)]
    rden = P.sb([128, 8], F32, "rden")
    obf = [P.sb([128, 128], BF16, "obf") for _ in range(2)]
    ps0b = cx.ps[0].bitcast(BF16)
    P.op("dve", lambda e: e.memset(vaug[:], 1.0), W=["vaug"])
    scale = 128 ** -0.5
    it_s = 0
    nheads = 16 if HS >= 4 else 1
    wseq = [off + hh_ * 128 for hh_ in range(nheads) for off in (OQ, OKK, OV)]
    wpos = [0]
    wnext = [ws.get(wv, wseq[0], 128)]

    def next_w():
        cur = wnext[0]
        wpos[0] += 1
        if wpos[0] < len(wseq):
            wnext[0] = ws.get(wv, wseq[wpos[0]], 128)
        return cur

    for h in range(nheads):
        for which, off in ((0, OQ), (1, OKK)):
            wbf, wkey = next_w()

            def evac_q(bank, tb):
                P.op("act", lambda e: e.copy(out=q32[:, tb * 512:(tb + 1) * 512], in_=cx.ps[bank]), R=[("ps", bank)], W=[("q32", tb)])
                rbk = 2 + tb % 2
                P.op("pe", lambda e: e.matmul(cx.ps[rbk], lhsT=psw[:], rhs=q32[:, tb * 512:(tb + 1) * 512], start=True, stop=True),
                     R=[("q32", tb), "psw"], W=[("ps", rbk)])
                P.op("dve", lambda e: e.tensor_tensor(out=t2[:, tb * 512:(tb + 1) * 512], in0=cx.ps[rbk], in1=sin_t[:, tb * 512:(tb + 1) * 512],
                                                      op=ALU.mult), R=[("ps", rbk), "trig"], W=[("t2", tb)])
                P.op("dve", lambda e: e.tensor_tensor(out=t1[:, tb * 512:(tb + 1) * 512], in0=q32[:, tb * 512:(tb + 1) * 512],
                                                       in1=cos_t[:, tb * 512:(tb + 1) * 512], op=ALU.mult), R=[("q32", tb), "trig"], W=[("t1", tb)])
            proj_fm(P, cx, wbf, wkey, hT, "hT", 4, [0, 1], evac_q)
            dst = qk[which]
            P.op("dve", lambda e, dst=dst: e.tensor_tensor(out=dst[:], in0=t1[:].rearrange("p (i r) -> p r i", r=16),
                                                           in1=t2[:].rearrange("p (i r) -> p r i", r=16), op=ALU.add),
                 R=[("t1", tb) for tb in range(4)] + [("t2", tb) for tb in range(4)], W=[("qk", which)])
        wbf, wkey = next_w()
        hTc = hT[:].rearrange("p k (i r) -> p k r i", r=16)
        for cb in range(4):
            bank = cb % 2
            for j in range(4):
                c = cb * 4 + j
                for kc in range(16):
                    P.op("pe", lambda e, c=c, j=j, kc=kc, bank=bank, wbf=wbf: e.matmul(cx.ps[bank][:, j * 128:(j + 1) * 128], lhsT=hTc[:, kc, c, :],
                                                                             rhs=wbf[:, kc, :], start=(kc == 0), stop=(kc == 15)),
                         R=[wkey, "hT"], W=[("ps", bank)])
            P.op("act", lambda e, cb=cb, bank=bank: e.copy(out=vaug[:, cb * 4:(cb + 1) * 4, 0:128],
                                                          in_=cx.ps[bank].rearrange("p (a b) -> p a b", a=4)),
                 R=[("ps", bank)], W=["vaug"])
        yb = ybh[h % 2]
        ybv = yb[:].rearrange("p (i r) -> p r i", r=16)
        def att_front(rg, c, sb_, es, pt):
            P.op("pe", lambda e: e.matmul(cx.ps[sb_], lhsT=qk[1][:, c, :],
                                          rhs=qk[0][:, rg * 4:(rg + 1) * 4, :].rearrange("p a b -> p (a b)"),
                                          start=True, stop=True),
                 R=[("qk", 0), ("qk", 1)], W=[("ps", sb_)])
            P.op("act", lambda e: e.activation(out=E[es][:], in_=cx.ps[sb_], func=AF.Exp, scale=scale),
                 R=[("ps", sb_)], W=[("E", es)])
            for wsel in (1, 0):
                js = [j for j in range(4) if (1 if c > rg * 4 + j else 0) == wsel]
                if not js:
                    continue
                j0, nj = js[0], len(js)
                e0 = (rg * 4 + j0 - c) % 16
                P.op("dve", lambda e, j0=j0, nj=nj, e0=e0, wsel=wsel: e.tensor_tensor(
                    out=PT[pt][:, j0:j0 + nj, :], in0=E[es][:].rearrange("p (a b) -> p a b", a=4)[:, j0:j0 + nj, :],
                    in1=ms[wsel][:, e0:e0 + nj, :], op=ALU.mult), R=[("E", es), "ms"], W=[("PT", pt)])

        def att_back(rg, c, pt, ybv=ybv, h=h):
            for j in range(4):
                ob = 4 + j
                P.op("pe", lambda e, j=j, ob=ob: e.matmul(cx.ps[ob][:, 0:129], lhsT=PT[pt][:, j, :],
                                                          rhs=vaug[:, c, 0:129], start=(c == 0), stop=(c == 15)),
                     R=[("PT", pt), "vaug"], W=[("po", j)])
            if c != 15:
                return
            for j in range(4):
                r = rg * 4 + j
                ob = 4 + j
                o2 = j % 2
                P.op("dve", lambda e, j=j, ob=ob: e.reciprocal(out=rden[:, j:j + 1], in_=cx.ps[ob][:, 128:129]),
                     R=[("po", j)], W=[("rden", j)])
                P.op("dve", lambda e, j=j, ob=ob, o2=o2: e.tensor_scalar(out=obf[o2][:], in0=cx.ps[ob][:, 0:128],
                                                                         scalar1=rden[:, j:j + 1], scalar2=None, op0=ALU.mult),
                     R=[("po", j), ("rden", j)], W=[("obf", o2)])
                P.op("pe", lambda e, o2=o2: e.transpose(out=ps0b[:, o2 * 128:(o2 + 1) * 128], in_=obf[o2][:], identity=cx.identb[:]),
                     R=[("obf", o2), "identb"], W=[("ps", 0)])
                P.op("act", lambda e, o2=o2, r=r: e.copy(out=ybv[:, r, :], in_=ps0b[:, o2 * 128:(o2 + 1) * 128]),
                     R=[("ps", 0)], W=[("ybh", h % 2)])

        pend = []
        for rg in range(4):
            for c in range(16):
                sb_ = 2 + it_s % 2
                es = it_s % 2
                pt = it_s % 4
                it_s += 1
                att_front(rg, c, sb_, es, pt)
                pend.append((rg, c, pt))
                if len(pend) > 1:
                    att_back(*pend.pop(0))
        while pend:
            att_back(*pend.pop(0))
        P.dma("sp", yab[16 + h], yb[:], R=[("ybh", h % 2)], W=[("yab", 16 + h)], sk=("ybh", h % 2))
    P.barrier()
    P.release()


def hybrid_ssd(P, cx, hT, vecs, rows_d, win_d, yab, triu, acs_d, HS=9):
    wv = win_d.rearrange("(kc p) n -> p kc n", p=128)
    P.mark()
    rb = P.sb([128, 96], F32, "rb")
    P.dma("sp", rb[:], rows_d[0, 0:96].partition_broadcast(128), W=["rb"], sk="rb")
    negm = P.sb([128, 128], F32, "negm")
    P.op("dve", lambda e: e.tensor_scalar(out=negm[:], in0=triu[:], scalar1=1.0, scalar2=30000.0, op0=ALU.subtract, op1=ALU.mult),
         R=["triu"], W=["negm"])
    onesf = P.sb([128, 128], F32, "onesf")
    P.op("dve", lambda e: e.memset(onesf[:], 1.0), W=["onesf"])
    onec = P.sb([128, 1], F32, "onec")
    P.op("dve", lambda e: e.memset(onec[:], 1.0), W=["onec"])
    dt = P.sb([128, 16, 32], F32, "dt")
    acs = P.sb([128, 16, 32], F32, "acs")
    eacs = P.sb([128, 16, 32], F32, "eacs")
    cdb = P.sb([128, 16, 32], F32, "cdb")
    wsd = P.sb([128, 16, 32], F32, "wsd")
    f3 = lambda t: t[:].rearrange("p a b -> p (a b)")
    P.mark()
    adt = P.sb([128, 16, 32], F32, "adt")
    acsT = P.sb([32, 16, 128], F32, "acsT")
    tA = P.sb([128, 16, 32], F32, "tA")
    tB = P.sb([128, 16, 32], F32, "tB")
    ea = P.sb([128, 32], F32, "ea")
    wdt_st = P.sb([128, 16, 32], F32, "wdtst")
    wdt = P.sb([128, 16, 32], BF16, "wdt")
    P.dma("sp", wdt_st[:], wv[:, :, ODT:ODT + 32], W=["wdtst"], sk="wdtst")
    P.op("dve", lambda e: e.tensor_copy(out=wdt[:], in_=wdt_st[:]), R=["wdtst"], W=["wdt"])
    for tt in range(16):
        for kc in range(16):
            P.op("pe", lambda e, tt=tt, kc=kc: e.matmul(cx.ps[0][:, tt * 32:(tt + 1) * 32], lhsT=hT[:, kc, tt * 128:(tt + 1) * 128],
                                                        rhs=wdt[:, kc, :], start=(kc == 0), stop=(kc == 15)), R=["wdt", "hT"], W=[("ps", 0)])
    P.op("dve", lambda e: e.tensor_tensor(out=tA[:], in0=cx.ps[0].rearrange("p (a b) -> p a b", a=16), in1=bc_mid(rb[:, 0:32], 16), op=ALU.add),
         R=[("ps", 0), "rb"], W=["tA"])
    P.op("dve", lambda e: e.scalar_tensor_tensor(out=f3(tB), in0=f3(tA), scalar=-1.0, in1=f3(tA), op0=ALU.mult, op1=ALU.max), R=["tA"], W=["tB"])
    P.op("act", lambda e: e.activation(out=f3(tB), in_=f3(tB), func=AF.Exp, scale=-1.0), R=["tB"], W=["tB"])
    P.op("act", lambda e: e.activation(out=f3(tB), in_=f3(tB), func=AF.Ln, bias=onec[:], scale=1.0), R=["tB", "onec"], W=["tB"])
    P.op("dve", lambda e: e.scalar_tensor_tensor(out=f3(dt), in0=f3(tA), scalar=0.0, in1=f3(tB), op0=ALU.max, op1=ALU.add), R=["tA", "tB"], W=["dt"])
    import os
    HP = int(os.environ.get("HP", "9"))
    if HP <= 1:
        P.barrier(); P.release(); P.release(); return
    P.op("act", lambda e: e.activation(out=ea[:], in_=rb[:, 32:64], func=AF.Exp), R=["rb"], W=["ea"])
    P.op("dve", lambda e: e.scalar_tensor_tensor(out=adt[:], in0=dt[:], scalar=-1.0, in1=bc_mid(ea[:], 16), op0=ALU.mult, op1=ALU.mult),
         R=["dt", "ea"], W=["adt"])
    HQ = int(os.environ.get("HQ", "9"))
    if HQ <= 1:
        P.barrier(); P.release(); P.release(); return
    for tt in range(16):
        P.op("pe", lambda e, tt=tt: e.matmul(cx.ps[1][:, tt * 32:(tt + 1) * 32], lhsT=triu[:], rhs=adt[:, tt, :], start=True, stop=True),
             R=["triu", "adt"], W=[("ps", 1)])
    if HQ <= 2:
        P.op("dve", lambda e: e.tensor_copy(out=f3(acs), in_=cx.ps[1]), R=[("ps", 1)], W=["acs"])
        P.barrier(); P.release(); P.release(); return
    P.op("pe", lambda e: e.matmul(cx.ps[2], lhsT=onesf[:], rhs=f3(adt), start=True, stop=True), R=["onesf", "adt"], W=[("ps", 2)])
    P.op("dve", lambda e: e.tensor_copy(out=f3(acs), in_=cx.ps[1]), R=[("ps", 1)], W=["acs"])
    if HQ <= 3:
        P.barrier(); P.release(); P.release(); return
    P.op("act", lambda e: e.activation(out=f3(eacs), in_=cx.ps[1], func=AF.Exp), R=[("ps", 1)], W=["eacs"])
    if HQ <= 4:
        P.barrier(); P.release(); P.release(); return
    P.op("act", lambda e: e.activation(out=f3(cdb), in_=cx.ps[2], func=AF.Exp), R=[("ps", 2)], W=["cdb"])
    if HQ <= 5:
        P.barrier(); P.release(); P.release(); return
    P.op("dve", lambda e: e.tensor_tensor(out=f3(tA), in0=cx.ps[2], in1=f3(acs), op=ALU.subtract), R=[("ps", 2), "acs", "dt"], W=["tA"])
    P.op("act", lambda e: e.activation(out=f3(tA), in_=f3(tA), func=AF.Exp), R=["tA"], W=["tA"])
    P.op("dve", lambda e: e.tensor_tensor(out=f3(wsd), in0=f3(tA), in1=f3(dt), op=ALU.mult), R=["tA", "dt"], W=["wsd"])
    if HP <= 2:
        P.barrier(); P.release(); P.release(); return
    for tt in range(16):
        bank = 3 + tt // 4
        P.op("pe", lambda e, tt=tt, bank=bank: e.transpose(out=cx.ps[bank][0:32, (tt % 4) * 128:(tt % 4 + 1) * 128], in_=acs[:, tt, :],
                                                           identity=cx.ident[:]), R=["acs", "ident"], W=[("ps", bank)])
    for b4 in range(4):
        P.op("dve", lambda e, b4=b4: e.tensor_copy(out=acsT[:, b4 * 4:(b4 + 1) * 4, :],
                                                   in_=cx.ps[3 + b4][0:32, :].rearrange("p (a b) -> p a b", a=4)),
             R=[("ps", 3 + b4)], W=["acsT"])
    if HP <= 3:
        P.barrier(); P.release(); P.release(); return
    P.dma("sp", acs_d, acsT[:], R=["acsT"], W=["acs_d"], sk="acsT")
    P.barrier()
    P.release()
    if HS == 5:
        P.release()
        return
    ws = WStream(P, 1, "sw")
    xpad = [P.sb([128, 516], F32, "xpad") for _ in range(2)]
    ca = [P.sb([128, 512], F32, "ca") for _ in range(2)]
    cvb = P.sb([128, 2048], BF16, "cvb")
    BT = P.sb([128, 2048], BF16, "BT")
    CT = P.sb([128, 2048], BF16, "CT")
    xs_tm = P.sb([128, 16, 512], BF16, "xs_tm")
    z_tm = P.sb([128, 16, 512], BF16, "z_tm")
    B_tm = P.sb([128, 16, 128], BF16, "B_tm")
    yaT = [P.sb([128, 4, 128], BF16, "yaT") for _ in range(2)]
    rowsb = P.sb([128, 8, 128], F32, "rowsb")
    dec = P.sb([128, 8, 128], BF16, "dec")
    LT = [P.sb([128, 8, 128], BF16, "LT") for _ in range(2)]
    cbs = P.sb([128, 128], F32, "cbs")
    xdt = [P.sb([128, 512], BF16, "xdt") for _ in range(2)]
    xw = [P.sb([128, 512], BF16, "xw") for _ in range(2)]
    carry = P.sb([128, 512], F32, "carry")
    prevb = P.sb([128, 512], BF16, "prevb")
    y1 = P.sb([128, 512], F32, "y1")
    y2 = P.sb([128, 512], F32, "y2")
    yg = P.sb([128, 512], BF16, "yg")
    ps7b = cx.ps[7].bitcast(BF16)
    cw = VOFF["conv_w"]
    cbo = VOFF["conv_b"]
    v3 = lambda ap, a: ap.rearrange("p (a b) -> p a b", a=a)
    for g in range(4 if HS >= 7 else 1):
        chunks = [(OX + g * 512 + j * 128, g * 4 + j, "xs", j) for j in range(4)] + \
                 [(OX + 2048 + g * 128, 16 + g, "B", 0), (OX + 2560 + g * 128, 20 + g, "C", 0)]
        for (col, ch, kind, j) in chunks:
            wbf, wkey = ws.get(wv, col, 128)
            dstT = {"xs": cvb, "B": BT, "C": CT}[kind]
            P.op("dve", lambda e: e.memset(xpad[0][:, 0:3], 0.0), R=[("xpad", 0)], W=[("xpad", 0)])

            def evac_c(bank, tb, ch=ch, dstT=dstT, kind=kind):
                xp = xpad[tb % 2]
                xn_ = xpad[(tb + 1) % 2]
                k0 = ("xpad", tb % 2)
                k1 = ("xpad", (tb + 1) % 2)
                P.op("act", lambda e: e.copy(out=xp[:, 3:515], in_=cx.ps[bank]), R=[("ps", bank)], W=[k0])
                if tb < 3:
                    P.op("dve", lambda e: e.tensor_copy(out=xn_[:, 0:3], in_=xp[:, 512:515]), R=[k0], W=[k1])
                P.op("dve", lambda e: e.tensor_scalar(out=ca[0][:], in0=xp[:, 0:512], scalar1=vecs[:, cw + ch:cw + ch + 1],
                                                      scalar2=vecs[:, cbo + ch:cbo + ch + 1], op0=ALU.mult, op1=ALU.add),
                     R=[k0, "vecs"], W=["ca0"])
                P.op("dve", lambda e: e.scalar_tensor_tensor(out=ca[1][:], in0=xp[:, 1:513], scalar=vecs[:, cw + 24 + ch:cw + 24 + ch + 1],
                                                              in1=ca[0][:], op0=ALU.mult, op1=ALU.add), R=[k0, "ca0", "vecs"], W=["ca1"])
                P.op("dve", lambda e: e.scalar_tensor_tensor(out=ca[0][:], in0=xp[:, 2:514], scalar=vecs[:, cw + 48 + ch:cw + 48 + ch + 1],
                                                             in1=ca[1][:], op0=ALU.mult, op1=ALU.add), R=[k0, "ca1", "vecs"], W=["ca0"])
                P.op("dve", lambda e: e.scalar_tensor_tensor(out=ca[1][:], in0=xp[:, 3:515], scalar=vecs[:, cw + 72 + ch:cw + 72 + ch + 1],
                                                              in1=ca[0][:], op0=ALU.mult, op1=ALU.add), R=[k0, "ca0", "vecs"], W=["ca1"])
                P.op("act", lambda e: e.activation(out=dstT[:, tb * 512:(tb + 1) * 512], in_=ca[1][:], func=AF.Silu),
                     R=["ca1", "xs_tm"], W=[(kind + "T", tb)])
            proj_fm(P, cx, wbf, wkey, hT, "hT", 4, [0, 1], evac_c)
            if kind in ("xs", "B"):
                for t4 in range(4):
                    for q in range(4):
                        tt = t4 * 4 + q
                        P.op("pe", lambda e, tt=tt, q=q, dstT=dstT: e.transpose(out=ps7b[:, q * 128:(q + 1) * 128], in_=dstT[:, tt * 128:(tt + 1) * 128],
                                                                               identity=cx.identb[:]), R=[(kind + "T", t4), "identb"], W=[("ps", 7)])
                    if kind == "xs":
                        P.op("dve", lambda e, t4=t4, j=j: e.tensor_copy(out=xs_tm[:, t4 * 4:(t4 + 1) * 4, j * 128:(j + 1) * 128],
                                                                        in_=v3(ps7b[:, 0:512], 4)), R=[("ps", 7)], W=["xs_tm"])
                    else:
                        P.op("dve", lambda e, t4=t4: e.tensor_copy(out=B_tm[:, t4 * 4:(t4 + 1) * 4, :], in_=v3(ps7b[:, 0:512], 4)),
                             R=[("ps", 7)], W=["B_tm"])
        for j in range(4):
            wbf, wkey = ws.get(wv, OZ + g * 512 + j * 128, 128)

            def evac_z(bank, tb, j=j):
                P.op("act", lambda e: e.activation(out=cvb[:, tb * 512:(tb + 1) * 512], in_=cx.ps[bank], func=AF.Silu),
                     R=[("ps", bank), "xs_tm"], W=[("zT", tb), ("xsT", tb)])
                for q in range(4):
                    P.op("pe", lambda e, q=q: e.transpose(out=ps7b[:, q * 128:(q + 1) * 128], in_=cvb[:, tb * 512 + q * 128:tb * 512 + (q + 1) * 128],
                                                          identity=cx.identb[:]), R=[("zT", tb), "identb"], W=[("ps", 7)])
                P.op("dve", lambda e: e.tensor_copy(out=z_tm[:, tb * 4:(tb + 1) * 4, j * 128:(j + 1) * 128], in_=v3(ps7b[:, 0:512], 4)),
                     R=[("ps", 7)], W=["z_tm"])
            proj_fm(P, cx, wbf, wkey, hT, "hT", 4, [0, 1], evac_z)
        g8 = slice(g * 8, (g + 1) * 8)
        BTk = [("BT", t) for t in range(4)]
        CTk = [("CT", t) for t in range(4)]
        def ssd_front(tt, g=g, g8=g8):
            tsl = slice(tt * 128, (tt + 1) * 128)
            sl = tt % 2
            P.dma("sp", rowsb[:], acs_d[g * 8:(g + 1) * 8, tt, :].partition_broadcast(128), R=["acs_d"], W=["rowsb"], sk="rowsb")
            P.op("pe", lambda e: e.matmul(cx.ps[0][:, 0:128], lhsT=BT[:, tsl], rhs=CT[:, tsl], start=True, stop=True),
                 R=BTk + CTk, W=[("ps", 0)])
            P.op("act", lambda e: e.copy(out=cbs[:], in_=cx.ps[0][:, 0:128]), R=[("ps", 0)], W=["cbs"])
            P.op("dve", lambda e: e.tensor_tensor(out=rowsb[:], in0=rowsb[:], in1=bc_last(acs[:, tt, g8], 128), op=ALU.subtract),
                 R=["rowsb", "acs"], W=["rowsb"])
            P.op("dve", lambda e: e.tensor_tensor(out=rowsb[:], in0=rowsb[:], in1=bc_mid(negm[:], 8), op=ALU.add), R=["rowsb", "negm"], W=["rowsb"])
            P.op("act", lambda e: e.activation(out=dec[:], in_=rowsb[:], func=AF.Exp), R=["rowsb"], W=["dec"])
            P.op("dve", lambda e: e.tensor_tensor(out=LT[sl][:], in0=dec[:], in1=bc_mid(cbs[:], 8), op=ALU.mult), R=["dec", "cbs"], W=[("LT", sl)])
            P.op("dve", lambda e: e.tensor_tensor(out=v3(xdt[sl][:], 8), in0=v3(xs_tm[:, tt, :], 8),
                                                  in1=bc_last(dt[:, tt, g8], 64), op=ALU.mult), R=["xs_tm", "dt"], W=[("xdt", sl)])
            P.op("dve", lambda e: e.tensor_tensor(out=v3(xw[sl][:], 8), in0=v3(xs_tm[:, tt, :], 8),
                                                  in1=bc_last(wsd[:, tt, g8], 64), op=ALU.mult), R=["xs_tm", "wsd"], W=[("xw", sl)])

        def ssd_back(tt, g=g, g8=g8):
            tsl = slice(tt * 128, (tt + 1) * 128)
            sl = tt % 2
            for hh in range(8):
                P.op("pe", lambda e, hh=hh: e.matmul(cx.ps[3][:, hh * 64:(hh + 1) * 64], lhsT=LT[sl][:, hh, :], rhs=xdt[sl][:, hh * 64:(hh + 1) * 64],
                                                     start=True, stop=True), R=[("LT", sl), ("xdt", sl)], W=[("ps", 3)])
            if tt > 0:
                P.op("pe", lambda e: e.matmul(cx.ps[4], lhsT=CT[:, tsl], rhs=prevb[:], start=True, stop=True),
                     R=CTk + ["prevb"], W=[("ps", 4)])
            if tt < 15:
                P.op("pe", lambda e: e.matmul(cx.ps[5], lhsT=B_tm[:, tt, :], rhs=xw[sl][:], start=True, stop=True),
                     R=["B_tm", ("xw", sl)], W=[("ps", 5)])
            if tt > 0:
                P.op("dve", lambda e: e.tensor_tensor(out=v3(y1[:], 8), in0=v3(cx.ps[4], 8),
                                                      in1=bc_last(eacs[:, tt, g8], 64), op=ALU.mult), R=[("ps", 4), "eacs"], W=["y1"])
                P.op("dve", lambda e: e.tensor_tensor(out=y1[:], in0=y1[:], in1=cx.ps[3], op=ALU.add), R=["y1", ("ps", 3)], W=["y1"])
            else:
                P.op("dve", lambda e: e.tensor_copy(out=y1[:], in_=cx.ps[3]), R=[("ps", 3)], W=["y1"])
            P.op("dve", lambda e: e.tensor_tensor(out=v3(y2[:], 8), in0=v3(xs_tm[:, tt, :], 8),
                                                  in1=bc_last(rb[:, 64 + g * 8:64 + (g + 1) * 8], 64), op=ALU.mult), R=["xs_tm", "rb"], W=["y2"])
            P.op("dve", lambda e: e.tensor_tensor(out=y2[:], in0=y2[:], in1=y1[:], op=ALU.add), R=["y2", "y1"], W=["y2"])
            P.op("dve", lambda e: e.tensor_tensor(out=yg[:], in0=y2[:], in1=z_tm[:, tt, :], op=ALU.mult), R=["y2", "z_tm"], W=["yg"])
            for q in range(4):
                P.op("pe", lambda e, q=q: e.transpose(out=ps7b[:, q * 128:(q + 1) * 128], in_=yg[:, q * 128:(q + 1) * 128], identity=cx.identb[:]),
                     R=["yg", "identb"], W=[("ps", 7)])
            P.op("act", lambda e: e.copy(out=yaT[sl][:], in_=v3(ps7b[:, 0:512], 4)), R=[("ps", 7)], W=[("yaT", sl)])
            P.dma("act", yab[g * 4:(g + 1) * 4, :, tsl].rearrange("j p t -> p j t"), yaT[sl][:], R=[("yaT", sl)], W=["yab"], sk=("yaT", sl))
            if tt < 15:
                if tt == 0:
                    P.op("dve", lambda e: e.tensor_copy(out=carry[:], in_=cx.ps[5]), R=[("ps", 5)], W=["carry"])
                else:
                    P.op("dve", lambda e: e.tensor_tensor(out=v3(carry[:], 8), in0=v3(carry[:], 8),
                                                          in1=bc_last(cdb[:, tt, g8], 64), op=ALU.mult), R=["carry", "cdb"], W=["carry"])
                    P.op("dve", lambda e: e.tensor_tensor(out=carry[:], in0=carry[:], in1=cx.ps[5], op=ALU.add), R=["carry", ("ps", 5)], W=["carry"])
                P.op("act", lambda e: e.copy(out=prevb[:], in_=carry[:]), R=["carry"], W=["prevb"])

        ssd_front(0)
        for tt in range(16):
            if tt + 1 < 16:
                ssd_front(tt + 1)
            ssd_back(tt)
        P.barrier()
    P.release()


def hybrid_sublayer(P, cx, xT, yT, vecs, rows_d, pos_d, win_d, wout_d, yab, triu, acs_d, psw_d, ms_d, abc):
    Acol, Bcol, Ccol = abc[:, 0:16], abc[:, 16:32], abc[:, 32:48]
    P.mark()
    hT = P.sb([128, 16, 2048], BF16, "hT")
    for half in range(2):
        phase_prenorm(P, cx, xT, half * 1024, 1024, hT[:, :, half * 1024:(half + 1) * 1024], Acol, Bcol)
    import os
    HS = int(os.environ.get("HS", "9"))
    if HS >= 2:
        hybrid_attention(P, cx, hT, vecs, pos_d, win_d, yab, psw_d, ms_d, HS)
    if HS >= 5:
        hybrid_ssd(P, cx, hT, vecs, rows_d, win_d, yab, triu, acs_d, HS)
    P.release()
    if HS < 9:
        return
    for half in range(2):
        t0 = half * 1024
        P.mark()
        cat = P.sb([128, 32, 1024], BF16, "cat")
        phase_prenorm(P, cx, yab, t0, 1024, cat[:, 0:16, :], vecs[:, VOFF["hyb_norm_g"]:VOFF["hyb_norm_g"] + 16],
                      vecs[:, VOFF["zero"]:VOFF["zero"] + 16], hkey="cat", sdt=BF16, xkey="yab")
        for hh in range(16):
            P.dma("sp", cat[:, 16 + hh, :], yab[16 + hh, :, t0:t0 + 1024], W=[("cat", 16 + hh)], sk=("cat", hh % 4))
        P.barrier()
        down_proj(P, cx, wout_d, 32, cat, yT, t0, 1024, "cat")
        phase_post(P, cx, xT, yT, t0, 1024, Ccol)
        P.release()

from concourse.bass_utils import run_bass_kernel_spmd

ROPE_THETA = 10000.0


def build_program(dbg=False, stop=99):
    nc = bass.Bass("TRN2", target_bir_lowering=False)
    P = Prog(nc)
    cx = Ctx()
    dt_in = lambda name, shape, dtype=F32: nc.dram_tensor(name, list(shape), dtype, kind="ExternalInput").ap()
    x_d = dt_in("x", [S, D])
    vec_d = dt_in("vecs", [128, NVEC])
    rows_d = dt_in("rows", [1, NROW])
    pos_d = dt_in("pos", [1, S], I32)
    wmod_d = dt_in("w_mod", [2, D, 18432])
    wg_d = dt_in("ffn_w_gate", [2, 2, D, FF])
    wu_d = dt_in("ffn_w_up", [2, 2, D, FF])
    wd_d = dt_in("ffn_w_down", [2, 2, FF, D])
    hin_d = dt_in("hyb_w_in", [D, 11296])
    hout_d = dt_in("hyb_w_out", [4096, D])
    sin_d = dt_in("sgu_w_in", [D, 8192])
    sout_d = dt_in("sgu_w_out", [4096, D])
    wsT_d = dt_in("wsT", [128, 8, 128])
    ms_d = dt_in("ms", [2, 128, 2048])
    psw_d = dt_in("psw", [128, 128])
    triu_d = dt_in("triu", [128, 128])
    out_d = nc.dram_tensor("out", [S, D], F32, kind="ExternalOutput").ap()
    xT = nc.dram_tensor("xT", [16, 128, S], F32).ap()
    yT = nc.dram_tensor("yT", [16, 128, S], F32).ap()
    yab = nc.dram_tensor("yab", [32, 128, S], BF16).ap()
    acs_d = nc.dram_tensor("acs_d", [32, 16, 128], F32).ap()
    dbgs = []
    setup_common(P, nc, cx)
    vecs = P.sb([128, NVEC], F32, "vecs")
    triu = P.sb([128, 128], F32, "triu")
    mods = P.sb([128, 288], F32, "mods")
    abc = P.sb([128, 48], F32, "abc")
    P.dma("sp", vecs[:], vec_d, W=["vecs"], sk="vecs")
    P.dma("sp", triu[:], triu_d, W=["triu"], sk="triu")
    P.barrier()
    phase_x_in(P, cx, x_d, xT)
    phase_mod(P, cx, vecs, wmod_d, mods)
    step = 0

    def snap():
        nonlocal step
        if dbg:
            d = nc.dram_tensor("dbg%d" % step, [16, 128, S], F32, kind="ExternalOutput").ap()
            P.dma("sp", d, xT, sk="dbg")
            P.barrier()
        step += 1
        return step >= stop

    done = False
    for l in range(2):
        if done:
            break
        sub_vectors(P, cx, vecs, mods, abc, l, 0, 0.5)
        ffn_sublayer(P, cx, xT, yT, wg_d[l, 0], wu_d[l, 0], wd_d[l, 0], abc[:, 0:16], abc[:, 16:32], abc[:, 32:48])
        if snap():
            break
        sub_vectors(P, cx, vecs, mods, abc, l, 1, 1.0)
        if l == 0:
            hybrid_sublayer(P, cx, xT, yT, vecs, rows_d, pos_d, hin_d, hout_d, yab, triu, acs_d, psw_d, ms_d, abc)
        else:
            sgu_sublayer(P, cx, xT, yT, vecs, rows_d, sin_d, wsT_d, sout_d, triu, abc)
        if snap():
            break
        sub_vectors(P, cx, vecs, mods, abc, l, 2, 0.5)
        ffn_sublayer(P, cx, xT, yT, wg_d[l, 1], wu_d[l, 1], wd_d[l, 1], abc[:, 0:16], abc[:, 16:32], abc[:, 32:48])
        if snap():
            break
    phase_x_out(P, cx, xT, out_d)
    P.finalize()
    return nc


def _lay(v):
    return np.ascontiguousarray(np.asarray(v, np.float32).reshape(-1, 128).T)


def _attn_masks():
    k = np.arange(128)[:, None]
    i = np.arange(128)[None, :]
    diff = i - k
    ms = np.zeros((2, 128, 16, 128), np.float32)
    for w in range(2):
        for e in range(16):
            if e == 0:
                if w == 1:
                    continue
                m = (diff >= 0).astype(np.float32) + ((diff >= 0) & (diff <= 32)) + ((diff >= 0) & (diff <= 8))
            elif e % 4 == 0:
                m = ((diff >= w) & (diff <= w + 31)).astype(np.float32) + ((diff >= w) & (diff <= w + 7))
            else:
                m = ((diff >= w) & (diff <= w + 7)).astype(np.float32)
            ms[w, :, e, :] = m
    return np.ascontiguousarray(ms.reshape(2, 128, 2048))


def make_in_maps(inp, cores):
    f = lambda k: np.asarray(inp[k], np.float32)
    invf = (ROPE_THETA ** (-(np.arange(128) % 64).astype(np.float64) / 64.0)).astype(np.float32)[:, None]
    shared_vec = [np.concatenate([_lay(f("b_mod")[l]) for l in range(2)], axis=1),
                  np.concatenate([_lay(f("norm_pre")[l, s]) for l in range(2) for s in range(3)], axis=1),
                  np.concatenate([_lay(f("norm_post")[l, s]) for l in range(2) for s in range(3)], axis=1),
                  np.concatenate([_lay(f("hyb_conv_w")[0, j]) for j in range(4)], axis=1),
                  _lay(f("hyb_conv_b")[0]), _lay(f("hyb_norm_g")[0]), _lay(f("sgu_b_in")[0]), invf, np.zeros((128, 16), np.float32)]
    rows = np.concatenate([f("hyb_dt_bias")[0], f("hyb_a_log")[0], f("hyb_d_skip")[0], f("sgu_ln_g")[0], f("sgu_ln_b")[0],
                           f("sgu_b_spatial")[0].reshape(-1)])[None, :].astype(np.float32)
    psw = np.zeros((128, 128), np.float32)
    for m in range(64):
        psw[m + 64, m] = -1.0
        psw[m, m + 64] = 1.0
    shared = dict(rows=np.ascontiguousarray(rows), w_mod=f("w_mod"), ffn_w_gate=f("ffn_w_gate"), ffn_w_up=f("ffn_w_up"),
                  ffn_w_down=f("ffn_w_down"), hyb_w_in=f("hyb_w_in")[0], hyb_w_out=f("hyb_w_out")[0], sgu_w_in=f("sgu_w_in")[0],
                  sgu_w_out=f("sgu_w_out")[0], wsT=np.ascontiguousarray(f("sgu_w_spatial")[0].transpose(2, 0, 1)),
                  ms=_attn_masks(), psw=psw, triu=np.triu(np.ones((128, 128), np.float32)), ident=np.eye(128, dtype=np.float32))
    maps = []
    for b in cores:
        vecs = np.concatenate([_lay(f("c")[b])] + shared_vec, axis=1)
        assert vecs.shape == (128, NVEC), vecs.shape
        m = dict(shared)
        m["x"] = np.ascontiguousarray(f("x")[b])
        m["vecs"] = np.ascontiguousarray(vecs)
        m["pos"] = np.ascontiguousarray(np.asarray(inp["positions"])[b:b + 1].astype(np.int32))
        maps.append(m)
    return maps


_NC = None


def kernel(**inp):
    global _NC
    if _NC is None:
        _NC = build_program()
    maps = make_in_maps(inp, list(range(8)))
    res = run_bass_kernel_spmd(_NC, maps, core_ids=list(range(8)))
    return np.stack([np.asarray(r["out"], np.float32) for r in res.results], axis=0)
```

```python
import numpy as np
import concourse.bass as bass
import concourse.mybir as mybir

F32 = mybir.dt.float32
BF16 = mybir.dt.bfloat16
I32 = mybir.dt.int32
AF = mybir.ActivationFunctionType
ALU = mybir.AluOpType
AX = mybir.AxisListType
CE = ("pe", "act", "dve", "pool")
ALLE = ("pe", "act", "dve", "pool", "sp")


class _Op:
    __slots__ = ("fn", "deps", "inc", "semval", "dma")

    def __init__(self, fn, deps, dma=None):
        self.fn = fn
        self.deps = deps
        self.inc = False
        self.semval = 0
        self.dma = dma


class Prog:
    def __init__(self, nc, n_dma_sems=80):
        self.nc = nc
        self.ops = {e: [] for e in ALLE}
        self.esem = {e: nc.alloc_semaphore("es_" + e) for e in CE}
        self.dpool = [nc.alloc_semaphore("ds%d" % i) for i in range(n_dma_sems)]
        self.dcount = {id(s): 0 for s in self.dpool}
        self.dfree = list(self.dpool)
        self.dmap = {}
        self.lastw = {}
        self.readers = {}
        self.sb_off = 16640
        self.sb_marks = []
        self.uid = 0

    def sb(self, shape, dtype, name=None):
        self.uid += 1
        t = self.nc.alloc_sbuf_tensor_at("%s_%d" % (name or "t", self.uid), list(shape), dtype, offset=self.sb_off)
        nb = int(np.prod(shape[1:])) * mybir.dt.size(dtype)
        nb = (nb + 63) // 64 * 64
        self.sb_off += nb
        assert self.sb_off <= 196608, ("SBUF overflow", self.sb_off, name)
        return t

    def mark(self):
        self.sb_marks.append(self.sb_off)

    def release(self):
        self.sb_off = self.sb_marks.pop()

    def _collect(self, R, W):
        deps = []
        for k in R:
            t = self.lastw.get(k)
            if t is not None:
                deps.append(t)
        for k in W:
            t = self.lastw.get(k)
            if t is not None:
                deps.append(t)
            r = self.readers.get(k)
            if r:
                deps.extend(r.values())
        return deps

    def _update(self, R, W, tok, who):
        for k in R:
            self.readers.setdefault(k, {})[who] = tok
        for k in W:
            self.lastw[k] = tok
            self.readers[k] = {}

    def op(self, eng, fn, R=(), W=()):
        if eng != "pe":
            pk = [k for k in R if isinstance(k, tuple) and k[0] in ("ps", "po")]
            if pk:
                R = [k for k in R if k not in pk]
                W = list(W) + pk
        deps = self._collect(R, W)
        o = _Op(fn, deps)
        self.ops[eng].append(o)
        tok = ("e", eng, len(self.ops[eng]) - 1)
        self._update(R, W, tok, eng)
        return tok

    def _dsem(self, key):
        s = self.dmap.get(key)
        if s is None:
            assert self.dfree, "out of DMA semaphores"
            s = self.dfree.pop(0)
            self.dmap[key] = s
        return s

    def dma(self, q, out, in_, R=(), W=(), sk=None, **kw):
        assert sk is not None
        s = self._dsem(sk)
        self.dcount[id(s)] += 16
        v = self.dcount[id(s)]
        deps = self._collect(R, W)
        o = _Op(lambda e: e.dma_start(out=out, in_=in_, **kw), deps, dma=(s, v))
        self.ops[q].append(o)
        tok = ("d", s, v)
        self._update(R, W, tok, ("d", id(s)))
        return tok

    def barrier(self):
        deps = []
        for e in CE:
            if self.ops[e]:
                for i in range(len(self.ops[e]) - 1, -1, -1):
                    if self.ops[e][i].dma is None and self.ops[e][i].fn is not None:
                        deps.append(("e", e, i))
                        break
        for s in self.dpool:
            if self.dcount[id(s)] > 0:
                deps.append(("d", s, self.dcount[id(s)]))
        for e in ALLE:
            self.ops[e].append(_Op(None, list(deps)))
        self.lastw = {}
        self.readers = {}
        self.dmap = {}
        self.dfree = list(self.dpool)

    def finalize(self):
        nc = self.nc
        for e in ALLE:
            for o in self.ops[e]:
                for d in o.deps:
                    if d[0] == "e":
                        self.ops[d[1]][d[2]].inc = True
        for e in CE:
            c = 0
            for o in self.ops[e]:
                if o.inc:
                    c += 1
                    o.semval = c
        ops = self.ops
        esem = self.esem
        stats = {}

        def replay(ename):
            def run(eng):
                seen = {}
                nw = 0
                for o in ops[ename]:
                    for d in o.deps:
                        if d[0] == "e":
                            if d[1] == "pe" and ename == "pe":
                                continue
                            s = esem[d[1]]
                            v = ops[d[1]][d[2]].semval
                            k = d[1]
                        else:
                            s = d[1]
                            v = d[2]
                            k = id(s)
                        if seen.get(k, 0) >= v:
                            continue
                        seen[k] = v
                        eng.wait_ge(s, v)
                        nw += 1
                    if o.fn is None:
                        continue
                    ins = o.fn(eng)
                    if o.dma is not None:
                        ins.then_inc(o.dma[0], 16)
                    elif o.inc:
                        ins.then_inc(esem[ename], 1)
                stats[ename] = (len(ops[ename]), nw)
            return run

        with nc.Block() as block:
            block.tensor(replay("pe"))
            block.scalar(replay("act"))
            block.vector(replay("dve"))
            block.gpsimd(replay("pool"))
            block.sync(replay("sp"))
        return stats

S = 2048
D = 2048
KC = 16
FF = 5632
FCN = 44
TB = 1024
EPS = 1e-6


class Ctx:
    pass


def setup_common(P, nc, cx):
    cx.psall = nc.alloc_psum_tensor("psall", [128, 4096], F32)
    cx.ps = [cx.psall[:, i * 512:(i + 1) * 512] for i in range(8)]
    cx.ident_d = nc.dram_tensor("ident", [128, 128], F32, kind="ExternalInput").ap()
    cx.ident = P.sb([128, 128], F32, "ident")
    cx.identb = P.sb([128, 128], BF16, "identb")
    cx.onesb = P.sb([128, 128], BF16, "onesb")
    cx.epsc = P.sb([128, 1], F32, "epsc")
    P.dma("sp", cx.ident[:], cx.ident_d, W=["ident"], sk="ident")
    P.op("dve", lambda e: e.tensor_copy(out=cx.identb[:], in_=cx.ident[:]), R=["ident"], W=["identb"])
    P.op("dve", lambda e: e.memset(cx.onesb[:], 1.0), W=["onesb"])
    P.op("dve", lambda e: e.memset(cx.epsc[:], EPS), W=["epsc"])
    cx.rr = 0


def psb(cx, i, n=512):
    return cx.ps[i][:, 0:n]


def phase_x_in(P, cx, x_d, xT):
    P.mark()
    xin = [P.sb([128, 2048], F32, "xin") for _ in range(2)]
    xo = [P.sb([128, 16, 128], F32, "xo") for _ in range(2)]
    xTv = xT.rearrange("c p t -> p c t")
    for tt in range(16):
        sl = tt % 2
        P.dma("sp", xin[sl][:], x_d[tt * 128:(tt + 1) * 128, :], W=[("xin", sl)], sk=("xin", sl))
        for cb in range(4):
            bank = (tt * 4 + cb) % 4
            for j in range(4):
                c = cb * 4 + j
                P.op("pe", lambda e, bank=bank, j=j, c=c, sl=sl: e.transpose(
                    out=cx.ps[bank][:, j * 128:(j + 1) * 128], in_=xin[sl][:, c * 128:(c + 1) * 128], identity=cx.ident[:]),
                    R=[("xin", sl), "ident"], W=[("ps", bank)])
            eng = "act" if cb % 2 else "dve"
            if eng == "dve":
                P.op("dve", lambda e, bank=bank, cb=cb, sl=sl: e.tensor_copy(
                    out=xo[sl][:, cb * 4:(cb + 1) * 4, :], in_=cx.ps[bank][:].rearrange("p (a b) -> p a b", a=4)),
                    R=[("ps", bank)], W=[("xo", sl, cb)])
            else:
                P.op("act", lambda e, bank=bank, cb=cb, sl=sl: e.copy(
                    out=xo[sl][:, cb * 4:(cb + 1) * 4, :], in_=cx.ps[bank][:].rearrange("p (a b) -> p a b", a=4)),
                    R=[("ps", bank)], W=[("xo", sl, cb)])
        P.dma("sp", xTv[:, :, tt * 128:(tt + 1) * 128], xo[sl][:],
              R=[("xo", sl, cb) for cb in range(4)], W=[("xT", c, tt // 8) for c in range(16)], sk=("xo", sl))
    P.barrier()
    P.release()


def phase_x_out(P, cx, xT, out_d):
    P.mark()
    xin = [P.sb([128, 16, 128], F32, "xin") for _ in range(2)]
    xo = [P.sb([128, 2048], F32, "xo") for _ in range(2)]
    xTv = xT.rearrange("c p t -> p c t")
    for tt in range(16):
        sl = tt % 2
        P.dma("sp", xin[sl][:], xTv[:, :, tt * 128:(tt + 1) * 128], R=[("xT", c, tt // 8) for c in range(16)],
              W=[("xin", sl)], sk=("xin", sl))
        for cb in range(4):
            bank = (tt * 4 + cb) % 4
            for j in range(4):
                c = cb * 4 + j
                P.op("pe", lambda e, bank=bank, j=j, c=c, sl=sl: e.transpose(
                    out=cx.ps[bank][:, j * 128:(j + 1) * 128], in_=xin[sl][:, c, :], identity=cx.ident[:]),
                    R=[("xin", sl), "ident"], W=[("ps", bank)])
            if cb % 2 == 0:
                P.op("dve", lambda e, bank=bank, cb=cb, sl=sl: e.tensor_copy(
                    out=xo[sl][:, cb * 512:(cb + 1) * 512], in_=cx.ps[bank][:]),
                    R=[("ps", bank)], W=[("xo", sl, cb)])
            else:
                P.op("act", lambda e, bank=bank, cb=cb, sl=sl: e.copy(
                    out=xo[sl][:, cb * 512:(cb + 1) * 512], in_=cx.ps[bank][:]),
                    R=[("ps", bank)], W=[("xo", sl, cb)])
        P.dma("sp", out_d[tt * 128:(tt + 1) * 128, :], xo[sl][:],
              R=[("xo", sl, cb) for cb in range(4)], W=[("out", tt)], sk=("xo", sl))
    P.barrier()
    P.release()


def rstd_from_ss(P, cx, ps_ss_keys, rstd, key, n=TB, width=D):
    nb = n // 512
    for b in range(nb):
        P.op("act", lambda e, b=b: e.activation(out=rstd[:, b * 512:(b + 1) * 512], in_=cx.ps[4 + b][:], func=AF.Ln,
                                                bias=cx.epsc[:], scale=1.0 / width),
             R=[("ps", 4 + b), "epsc"], W=[(key, b)])
        P.op("act", lambda e, b=b: e.activation(out=rstd[:, b * 512:(b + 1) * 512], in_=rstd[:, b * 512:(b + 1) * 512],
                                                func=AF.Exp, scale=-0.5),
             R=[(key, b)], W=[(key, b)])


def phase_prenorm(P, cx, xT, t0, n, hT, Acol, Bcol, hkey="hT", sdt=F32, xkey="xT"):
    P.mark()
    xs = [P.sb([128, n], sdt, "xs") for _ in range(3)]
    sq = [P.sb([128, n], BF16, "sq") for _ in range(2)]
    tmp = [P.sb([128, n], F32, "tmp") for _ in range(2)]
    rstd = P.sb([128, n], F32, "rstd")
    nb = n // 512
    half = t0 // 1024
    for c in range(16):
        sl = c % 3
        P.dma("sp", xs[sl][:], xT[c, :, t0:t0 + n], R=[(xkey, c, half)], W=[("xs", sl)], sk=("xs", sl))
        P.op("act", lambda e, sl=sl, c=c: e.activation(out=sq[c % 2][:], in_=xs[sl][:], func=AF.Square),
             R=[("xs", sl)], W=[("sq", c % 2)])
        for b in range(nb):
            P.op("pe", lambda e, b=b, c=c: e.matmul(cx.ps[4 + b][:], lhsT=cx.onesb[:], rhs=sq[c % 2][:, b * 512:(b + 1) * 512],
                                                    start=(c == 0), stop=(c == 15)),
                 R=[("sq", c % 2), "onesb"], W=[("ps", 4 + b)])
    rstd_from_ss(P, cx, None, rstd, "rstd", n)
    for c in range(16):
        sl = c % 3
        P.dma("sp", xs[sl][:], xT[c, :, t0:t0 + n], R=[(xkey, c, half)], W=[("xs", sl)], sk=("xs", sl))
        P.op("dve", lambda e, sl=sl, c=c: e.scalar_tensor_tensor(out=tmp[c % 2][:], in0=xs[sl][:], scalar=Acol[:, c:c + 1],
                                                                 in1=rstd[:], op0=ALU.mult, op1=ALU.mult),
             R=[("xs", sl), ("rstd", 0), ("rstd", 1), "vecs"], W=[("tmp", c % 2)])
        P.op("act", lambda e, c=c: e.activation(out=hT[:, c, 0:n], in_=tmp[c % 2][:], func=AF.Identity,
                                                bias=Bcol[:, c:c + 1], scale=1.0),
             R=[("tmp", c % 2), "vecs"], W=[(hkey, c)])
    P.barrier()
    P.release()


def phase_post(P, cx, xT, yT, t0, n, Ccol):
    P.mark()
    xs = [P.sb([128, n], F32, "xs") for _ in range(2)]
    ys = [P.sb([128, n], F32, "ys") for _ in range(2)]
    tm = [P.sb([128, n], F32, "tm") for _ in range(2)]
    xn = [P.sb([128, n], F32, "xn") for _ in range(2)]
    rstd = P.sb([128, n], F32, "rstd")
    half = t0 // 1024
    rstd_from_ss(P, cx, None, rstd, "rstd", n)
    for c in range(16):
        sl = c % 2
        P.dma("sp", ys[sl][:], yT[c, :, t0:t0 + n], R=[("yT", c)], W=[("ys", sl)], sk=("ys", sl))
        P.dma("sp", xs[sl][:], xT[c, :, t0:t0 + n], R=[("xT", c, half)], W=[("xs", sl)], sk=("xs", sl))
        P.op("dve", lambda e, sl=sl, c=c: e.scalar_tensor_tensor(out=tm[sl][:], in0=ys[sl][:], scalar=Ccol[:, c:c + 1],
                                                                 in1=rstd[:], op0=ALU.mult, op1=ALU.mult),
             R=[("ys", sl), ("rstd", 0), ("rstd", 1), "vecs"], W=[("tm", sl)])
        P.op("dve", lambda e, sl=sl: e.tensor_tensor(out=xn[sl][:], in0=tm[sl][:], in1=xs[sl][:], op=ALU.add),
             R=[("tm", sl), ("xs", sl)], W=[("xn", sl)])
        P.dma("act", xT[c, :, t0:t0 + n], xn[sl][:], R=[("xn", sl)], W=[("xT", c, half)], sk=("xn", sl))
    P.barrier()
    P.release()


def down_proj(P, cx, w_d, nfc, actT, yT, t0, n, akey):
    P.mark()
    nq = 4
    qs = [(nfc * q) // nq for q in range(nq + 1)]
    qmax = max(qs[q + 1] - qs[q] for q in range(nq))
    wst = [P.sb([128, qmax, 128], F32, "wdst") for _ in range(nq)]
    wbf = [P.sb([128, nfc, 128], BF16, "wdbf") for _ in range(2)]
    ysb = [P.sb([128, 512], F32, "ysb") for _ in range(2)]
    sq = [P.sb([128, 512], BF16, "sq") for _ in range(2)]
    wv = w_d.rearrange("(fc p) d -> p fc d", p=128)
    nb = n // 512
    it = 0
    def load_wd(dc):
        sl = dc % 2
        for q in range(nq):
            a, bb = qs[q], qs[q + 1]
            P.dma("sp", wst[q][:, 0:bb - a, :], wv[:, a:bb, dc * 128:(dc + 1) * 128], W=[("wdst", q)], sk=("wdst", q))
            if q % 2 == 0:
                P.op("dve", lambda e, sl=sl, a=a, bb=bb, q=q: e.tensor_copy(out=wbf[sl][:, a:bb, :], in_=wst[q][:, 0:bb - a, :]),
                     R=[("wdst", q)], W=[("wdbf", sl, q)])
            else:
                P.op("act", lambda e, sl=sl, a=a, bb=bb, q=q: e.copy(out=wbf[sl][:, a:bb, :], in_=wst[q][:, 0:bb - a, :]),
                     R=[("wdst", q)], W=[("wdbf", sl, q)])

    load_wd(0)
    for dc in range(16):
        sl = dc % 2
        if dc + 1 < 16:
            load_wd(dc + 1)
        for b in range(nb):
            bank = it % 4
            s2 = it % 2
            it += 1
            for fc in range(nfc):
                P.op("pe", lambda e, bank=bank, fc=fc, sl=sl, b=b: e.matmul(
                    cx.ps[bank][:], lhsT=wbf[sl][:, fc, :], rhs=actT[:, fc, b * 512:(b + 1) * 512],
                    start=(fc == 0), stop=(fc == nfc - 1)),
                    R=[("wdbf", sl, min(nq - 1, (fc * nq) // nfc)), (akey, fc)], W=[("ps", bank)])
            P.op("dve", lambda e, bank=bank, s2=s2: e.tensor_copy(out=ysb[s2][:], in_=cx.ps[bank][:]),
                 R=[("ps", bank)], W=[("ysb", s2)])

            P.op("act", lambda e, bank=bank, s2=s2: e.activation(out=sq[s2][:], in_=ysb[s2][:], func=AF.Square),
                 R=[("ysb", s2)], W=[("sq", s2)])
            P.op("pe", lambda e, s2=s2, b=b, dc=dc: e.matmul(cx.ps[4 + b][:], lhsT=cx.onesb[:], rhs=sq[s2][:],
                                                             start=(dc == 0), stop=(dc == 15)),
                 R=[("sq", s2), "onesb"], W=[("ps", 4 + b)])
            P.dma("sp", yT[dc, :, t0 + b * 512:t0 + (b + 1) * 512], ysb[s2][:], R=[("ysb", s2)], W=[("yT", dc)], sk=("ysb", s2))
    P.barrier()
    P.release()


def ffn_phaseA(P, cx, hT, actT, wg_d, wu_d):
    P.mark()
    wst = [[P.sb([128, 16, 128], F32, "wst") for _ in range(2)] for _ in range(2)]
    wbf = [[P.sb([128, 16, 128], BF16, "wbf") for _ in range(2)] for _ in range(2)]
    sg = [P.sb([128, 512], BF16, "sg") for _ in range(2)]
    wviews = [wg_d.rearrange("(kc p) f -> p kc f", p=128), wu_d.rearrange("(kc p) f -> p kc f", p=128)]
    it = 0

    def load_w(fc):
        sl = fc % 2
        for m in range(2):
            P.dma("sp", wst[m][sl][:], wviews[m][:, :, fc * 128:(fc + 1) * 128],
                  W=[("wst", m, sl)], sk=("wst", m, sl))
            if m == 0:
                P.op("dve", lambda e, m=m, sl=sl: e.tensor_copy(out=wbf[m][sl][:], in_=wst[m][sl][:]),
                     R=[("wst", m, sl)], W=[("wbf", m, sl)])
            else:
                P.op("act", lambda e, m=m, sl=sl: e.copy(out=wbf[m][sl][:], in_=wst[m][sl][:]),
                     R=[("wst", m, sl)], W=[("wbf", m, sl)])

    load_w(0)
    for fc in range(FCN):
        sl = fc % 2
        if fc + 1 < FCN:
            load_w(fc + 1)
        for b in range(2):
            bg = (it % 2) * 2
            bu = bg + 1
            it += 1
            for m, bank in ((0, bg), (1, bu)):
                for kc in range(16):
                    P.op("pe", lambda e, m=m, bank=bank, kc=kc, sl=sl, b=b: e.matmul(
                        cx.ps[bank][:], lhsT=wbf[m][sl][:, kc, :], rhs=hT[:, kc, b * 512:(b + 1) * 512],
                        start=(kc == 0), stop=(kc == 15)),
                        R=[("wbf", m, sl), ("hT", kc)], W=[("ps", bank)])
            s2 = it % 2
            P.op("act", lambda e, bg=bg, s2=s2: e.activation(out=sg[s2][:], in_=cx.ps[bg][:], func=AF.Silu),
                 R=[("ps", bg)], W=[("sg", s2)])
            P.op("dve", lambda e, bu=bu, s2=s2, fc=fc, b=b: e.tensor_tensor(
                out=actT[:, fc, b * 512:(b + 1) * 512], in0=sg[s2][:], in1=cx.ps[bu][:], op=ALU.mult),
                R=[("sg", s2), ("ps", bu)], W=[("actT", fc)])
    P.barrier()
    P.release()


def ffn_sublayer(P, cx, xT, yT, wg_d, wu_d, wd_d, Acol, Bcol, Ccol):
    for half in range(2):
        t0 = half * TB
        P.mark()
        actT = P.sb([128, FCN, TB], BF16, "actT")
        P.mark()
        hT = P.sb([128, 16, TB], BF16, "hT")
        phase_prenorm(P, cx, xT, t0, TB, hT, Acol, Bcol)
        ffn_phaseA(P, cx, hT, actT, wg_d, wu_d)
        P.release()
        down_proj(P, cx, wd_d, FCN, actT, yT, t0, TB, "actT")
        phase_post(P, cx, xT, yT, t0, TB, Ccol)
        P.release()

OZ, OX, ODT, OQ, OKK, OV = 0, 2048, 5120, 5152, 7200, 9248
VEC_LAYOUT = [("c", 16), ("b_mod", 288), ("norm_pre", 96), ("norm_post", 96), ("conv_w", 96), ("conv_b", 24),
              ("hyb_norm_g", 16), ("sgu_b_in", 64), ("invf", 1), ("zero", 16)]
VOFF = {}
_o = 0
for _n, _w in VEC_LAYOUT:
    VOFF[_n] = _o
    _o += _w
NVEC = _o
ROW_LAYOUT = [("dt_bias", 32), ("a_log", 32), ("d_skip", 32), ("ln_g", 4096), ("ln_b", 4096), ("b_sp", 1024)]
ROFF = {}
_o = 0
for _n, _w in ROW_LAYOUT:
    ROFF[_n] = _o
    _o += _w
NROW = _o


def bc_mid(ap, reps):
    a = ap.ap
    return bass.AP(ap.tensor, ap.offset, [list(a[0]), [0, reps], list(a[1])])


def bc_last(ap, reps):
    a = ap.ap
    return bass.AP(ap.tensor, ap.offset, [list(a[0]), list(a[1]), [0, reps]])


class WStream:
    def __init__(self, P, n=2, name="ws", width=128, engines=("dve", "act"), nst=None):
        self.engines = engines
        self.P = P
        self.n = n
        self.nst = nst or n
        self.name = name
        self.st = [P.sb([128, 16, width], F32, name + "st") for _ in range(self.nst)]
        self.bf = [P.sb([128, 16, width], BF16, name + "bf") for _ in range(n)]
        self.i = 0

    def get(self, wview, c0, ncols):
        P = self.P
        sl = self.i % self.n
        ss = self.i % self.nst
        self.i += 1
        st, bf = self.st[ss], self.bf[sl]
        P.dma("sp", st[:, :, 0:ncols], wview[:, :, c0:c0 + ncols], W=[(self.name, "st", ss)], sk=(self.name, ss))
        if self.engines[self.i % len(self.engines)] == "dve":
            P.op("dve", lambda e: e.tensor_copy(out=bf[:, :, 0:ncols], in_=st[:, :, 0:ncols]),
                 R=[(self.name, "st", ss)], W=[(self.name, "bf", sl)])
        else:
            P.op("act", lambda e: e.copy(out=bf[:, :, 0:ncols], in_=st[:, :, 0:ncols]),
                 R=[(self.name, "st", ss)], W=[(self.name, "bf", sl)])
        return bf[:, :, 0:ncols], (self.name, "bf", sl)


def proj_fm(P, cx, wbf, wkey, hT, hkey, ntb, banks, evac):
    for tb in range(ntb):
        bank = banks[cx.rr % len(banks)]
        cx.rr += 1
        for kc in range(16):
            P.op("pe", lambda e, bank=bank, kc=kc, tb=tb: e.matmul(cx.ps[bank], lhsT=wbf[:, kc, :], rhs=hT[:, kc, tb * 512:(tb + 1) * 512],
                                                                  start=(kc == 0), stop=(kc == 15)),
                 R=[wkey, hkey], W=[("ps", bank)])
        evac(bank, tb)


def phase_mod(P, cx, vecs, wmod_d, mods):
    P.mark()
    ca = P.sb([128, 16, 2], F32, "ca")
    sgm = P.sb([128, 16], F32, "sgm")
    cc = vecs[:, VOFF["c"]:VOFF["c"] + 16]
    P.op("act", lambda e: e.activation(out=sgm[:], in_=cc, func=AF.Sigmoid), R=["vecs"], W=["sgm"])
    for j in range(2):
        P.op("dve", lambda e, j=j: e.tensor_tensor(out=ca[:, :, j], in0=sgm[:], in1=cc, op=ALU.mult), R=["sgm", "vecs"], W=["ca"])
    cab = P.sb([128, 16, 2], BF16, "cab")
    P.op("dve", lambda e: e.tensor_copy(out=cab[:], in_=ca[:]), R=["ca"], W=["cab"])
    wst = [P.sb([128, 16, 128], F32, "wm") for _ in range(3)]
    wbf = [P.sb([128, 16, 128], BF16, "wmb") for _ in range(3)]
    for l in range(2):
        wv = wmod_d[l].rearrange("(kc p) n -> p kc n", p=128)
        for j in range(144):
            it = l * 144 + j
            sl = it % 3
            P.dma("sp", wst[sl][:], wv[:, :, j * 128:(j + 1) * 128], W=[("wm", sl)], sk=("wm", sl))
            if it % 2 == 0:
                P.op("dve", lambda e, sl=sl: e.tensor_copy(out=wbf[sl][:], in_=wst[sl][:]), R=[("wm", sl)], W=[("wmb", sl)])
            else:
                P.op("act", lambda e, sl=sl: e.copy(out=wbf[sl][:], in_=wst[sl][:]), R=[("wm", sl)], W=[("wmb", sl)])
            for kc in range(16):
                P.op("pe", lambda e, l=l, j=j, kc=kc, sl=sl: e.matmul(cx.ps[l][:, 2 * j:2 * j + 2], lhsT=wbf[sl][:, kc, :], rhs=cab[:, kc, :],
                                                                     start=(kc == 0), stop=(kc == 15)),
                     R=[("wmb", sl), "cab"], W=[("ps", l)])
        P.op("dve", lambda e, l=l: e.tensor_tensor(out=mods[:, l * 144:(l + 1) * 144],
                                                   in0=cx.ps[l][:, 0:288].rearrange("p (j t) -> p j t", t=2)[:, :, 0],
                                                   in1=vecs[:, VOFF["b_mod"] + l * 144:VOFF["b_mod"] + (l + 1) * 144], op=ALU.add),
             R=[("ps", l), "vecs"], W=["mods"])
    P.barrier()
    P.release()


def sub_vectors(P, cx, vecs, mods, abc, l, s, rw):
    base = l * 144 + s * 48
    gpre = vecs[:, VOFF["norm_pre"] + (l * 3 + s) * 16:VOFF["norm_pre"] + (l * 3 + s + 1) * 16]
    gpost = vecs[:, VOFF["norm_post"] + (l * 3 + s) * 16:VOFF["norm_post"] + (l * 3 + s + 1) * 16]
    P.op("dve", lambda e: e.scalar_tensor_tensor(out=abc[:, 0:16], in0=mods[:, base + 16:base + 32], scalar=1.0, in1=gpre,
                                                 op0=ALU.add, op1=ALU.mult), R=["mods", "vecs"], W=["abc"])
    P.op("dve", lambda e: e.tensor_copy(out=abc[:, 16:32], in_=mods[:, base:base + 16]), R=["mods"], W=["abc"])
    P.op("dve", lambda e: e.scalar_tensor_tensor(out=abc[:, 32:48], in0=mods[:, base + 32:base + 48], scalar=1.0, in1=gpost,
                                                 op0=ALU.add, op1=ALU.mult), R=["mods", "vecs"], W=["abc"])
    if rw != 1.0:
        P.op("dve", lambda e: e.tensor_scalar(out=abc[:, 32:48], in0=abc[:, 32:48], scalar1=float(rw), scalar2=None, op0=ALU.mult),
             R=["abc"], W=["abc"])
    P.barrier()


def gelu_a(P, cx, bank, bias_col, tmps, idx, n=512):
    xb, t1, sg = tmps[idx % len(tmps)]
    k = ("gl", idx % len(tmps))
    P.op("act", lambda e: e.activation(out=xb[:, 0:n], in_=cx.ps[bank][:, 0:n], func=AF.Identity, bias=bias_col, scale=1.0),
         R=[("ps", bank), "vecs"], W=[k + ("xb",)])
    P.op("act", lambda e: e.activation(out=t1[:, 0:n], in_=xb[:, 0:n], func=AF.Square), R=[k + ("xb",)], W=[k + ("t1",)])
    P.op("dve", lambda e: e.tensor_scalar(out=t1[:, 0:n], in0=t1[:, 0:n], scalar1=0.044715, scalar2=1.0, op0=ALU.mult, op1=ALU.add),
         R=[k + ("t1",)], W=[k + ("t1",)])
    P.op("dve", lambda e: e.tensor_tensor(out=t1[:, 0:n], in0=t1[:, 0:n], in1=xb[:, 0:n], op=ALU.mult), R=[k + ("t1",), k + ("xb",)], W=[k + ("t1",)])


def gelu_b(P, cx, out_ap, okey, tmps, idx, n=512):
    xb, t1, sg = tmps[idx % len(tmps)]
    k = ("gl", idx % len(tmps))
    P.op("act", lambda e: e.activation(out=sg[:, 0:n], in_=t1[:, 0:n], func=AF.Sigmoid, scale=1.5957691216057308),
         R=[k + ("t1",)], W=[k + ("sg",)])
    P.op("dve", lambda e: e.tensor_tensor(out=out_ap, in0=xb[:, 0:n], in1=sg[:, 0:n], op=ALU.mult), R=[k + ("xb",), k + ("sg",)], W=[okey])


def sgu_sublayer(P, cx, xT, yT, vecs, rows_d, win_d, wsT_d, wout_d, triu, abc):
    NQ = 512
    Acol, Bcol, Ccol = abc[:, 0:16], abc[:, 16:32], abc[:, 32:48]
    wv = win_d.rearrange("(kc p) n -> p kc n", p=128)
    P.mark()
    wsT = P.sb([128, 8, 128], BF16, "wsT")
    bsp = P.sb([128, 8, 128], F32, "bsp")
    lng = P.sb([128, 4096], BF16, "lng")
    lnb = P.sb([128, 4096], BF16, "lnb")
    P.mark()
    lnf = P.sb([128, 4096], F32, "lnf")
    wsf = P.sb([128, 8, 128], F32, "wsf")
    P.dma("sp", wsf[:], wsT_d, W=["wsf"], sk="wsf")
    P.op("dve", lambda e: e.tensor_tensor(out=wsT[:], in0=wsf[:], in1=bc_mid(triu[:], 8), op=ALU.mult), R=["wsf", "triu"], W=["wsT"])
    P.dma("sp", bsp[:].rearrange("p a b -> p (a b)"), rows_d[0, ROFF["b_sp"]:ROFF["b_sp"] + 1024].partition_broadcast(128), W=["bsp"], sk="bsp")
    P.dma("sp", lnf[:], rows_d[0, ROFF["ln_g"]:ROFF["ln_g"] + 4096].partition_broadcast(128), W=["lnf"], sk="lnf")
    P.op("dve", lambda e: e.tensor_copy(out=lng[:], in_=lnf[:]), R=["lnf"], W=["lng"])
    P.dma("sp", lnf[:], rows_d[0, ROFF["ln_b"]:ROFF["ln_b"] + 4096].partition_broadcast(128), R=[], W=["lnf"], sk="lnf")
    P.op("dve", lambda e: e.tensor_copy(out=lnb[:], in_=lnf[:]), R=["lnf"], W=["lnb"])
    P.barrier()
    P.release()
    bin_off = VOFF["sgu_b_in"]
    def do_quarter(qt):
        t0 = qt * NQ
        P.mark()
        hT = P.sb([128, 16, NQ], BF16, "hT")
        gT = P.sb([128, 32, NQ], BF16, "gT")
        phase_prenorm(P, cx, xT, t0, NQ, hT, Acol, Bcol)
        P.mark()
        vtm = P.sb([128, 4, 4096], BF16, "vtm")
        ws = WStream(P, 2, "sgw", engines=("act",), nst=3)
        tmps = [(P.sb([128, 512], F32, "xb"), P.sb([128, 512], F32, "t1"), P.sb([128, 512], BF16, "sg")) for _ in range(2)]
        vT = [P.sb([128, 512], BF16, "vT") for _ in range(2)]
        ps7b = cx.ps[7].bitcast(BF16)
        nxt = [ws.get(wv, 4096, 128)]

        def v_front(f):
            wbf, wkey = nxt[0]
            if f + 1 < 32:
                nxt[0] = ws.get(wv, 4096 + (f + 1) * 128, 128)
            proj_fm(P, cx, wbf, wkey, hT, "hT", 1, [0, 1],
                    lambda bank, tb: gelu_a(P, cx, bank, vecs[:, bin_off + 32 + f:bin_off + 33 + f], tmps, f))

        def v_mid(f):
            gelu_b(P, cx, vT[f % 2][:], ("vT", f % 2), tmps, f)

        def v_back(f):
            for tt in range(4):
                P.op("pe", lambda e, tt=tt: e.transpose(out=ps7b[:, tt * 128:(tt + 1) * 128], in_=vT[f % 2][:, tt * 128:(tt + 1) * 128],
                                                        identity=cx.identb[:]), R=[("vT", f % 2), "identb"], W=[("ps", 7)])
            P.op("dve", lambda e: e.tensor_copy(out=vtm[:, :, f * 128:(f + 1) * 128],
                                                in_=ps7b[:, 0:512].rearrange("p (a b) -> p a b", a=4)),
                 R=[("ps", 7)], W=[("vtm", tt2) for tt2 in range(4)])

        for f in range(34):
            if f < 32:
                v_front(f)
            if 1 <= f <= 32:
                v_mid(f - 1)
            if f >= 2:
                v_back(f - 2)
        P.mark()
        junk = P.sb([128, 4096], BF16, "junk")
        st = P.sb([128, 16], F32, "lnst")
        for tt in range(4):
            k = ("vtm", tt)
            P.op("act", lambda e, tt=tt: e.activation(out=junk[:], in_=vtm[:, tt, :], func=AF.Identity, accum_out=st[:, 0:1]),
                 R=[k], W=["junk", "lnst"])
            P.op("act", lambda e, tt=tt: e.activation(out=junk[:], in_=vtm[:, tt, :], func=AF.Square, accum_out=st[:, 1:2]),
                 R=[k], W=["junk", "lnst"])
            P.op("dve", lambda e: e.tensor_scalar(out=st[:, 2:3], in0=st[:, 0:1], scalar1=1.0 / 4096, scalar2=None, op0=ALU.mult), R=["lnst"], W=["lnst"])
            P.op("dve", lambda e: e.tensor_tensor(out=st[:, 3:4], in0=st[:, 2:3], in1=st[:, 2:3], op=ALU.mult), R=["lnst"], W=["lnst"])
            P.op("dve", lambda e: e.scalar_tensor_tensor(out=st[:, 4:5], in0=st[:, 1:2], scalar=1.0 / 4096, in1=st[:, 3:4],
                                                         op0=ALU.mult, op1=ALU.subtract), R=["lnst"], W=["lnst"])
            P.op("act", lambda e: e.activation(out=st[:, 5:6], in_=st[:, 4:5], func=AF.Ln, bias=cx.epsc[:], scale=1.0), R=["lnst", "epsc"], W=["lnst"])
            P.op("act", lambda e: e.activation(out=st[:, 6:7], in_=st[:, 5:6], func=AF.Exp, scale=-0.5), R=["lnst"], W=["lnst"])
            P.op("dve", lambda e: e.scalar_tensor_tensor(out=st[:, 7:8], in0=st[:, 2:3], scalar=-1.0, in1=st[:, 6:7], op0=ALU.mult, op1=ALU.mult),
                 R=["lnst"], W=["lnst"])
            P.op("act", lambda e, tt=tt: e.activation(out=junk[:], in_=vtm[:, tt, :], func=AF.Identity, bias=st[:, 7:8], scale=st[:, 6:7]),
                 R=[k, "lnst", "junk"], W=["junk"])
            P.op("dve", lambda e: e.tensor_tensor(out=junk[:], in0=junk[:], in1=lng[:], op=ALU.mult), R=["junk", "lng"], W=["junk"])
            P.op("dve", lambda e, tt=tt: e.tensor_tensor(out=vtm[:, tt, :], in0=junk[:], in1=lnb[:], op=ALU.add), R=["junk", "lnb"], W=[k])
        P.release()
        uT = [P.sb([128, 512], F32, "uT") for _ in range(2)]
        mt = [P.sb([128, 512], F32, "mt") for _ in range(2)]
        nxu = [None]

        def u_front(f):
            if f == 0:
                nxu[0] = ws.get(wv, 0, 128)
            wbf, wkey = nxu[0]
            if f + 1 < 32:
                nxu[0] = ws.get(wv, (f + 1) * 128, 128)
            g = f // 4
            proj_fm(P, cx, wbf, wkey, hT, "hT", 1, [0, 1],
                    lambda bank, tb: gelu_a(P, cx, bank, vecs[:, bin_off + f:bin_off + f + 1], tmps, f))
            mb = 2 + (f % 2)
            for tt in range(4):
                P.op("pe", lambda e, tt=tt: e.matmul(cx.ps[mb][:, tt * 128:(tt + 1) * 128], lhsT=vtm[:, tt, f * 128:(f + 1) * 128],
                                                     rhs=wsT[:, g, :], start=True, stop=True),
                     R=[("vtm", tt), "wsT"], W=[("ps", mb)])
            P.op("dve", lambda e: e.tensor_tensor(out=mt[f % 2][:].rearrange("p (a b) -> p a b", a=4),
                                                  in0=cx.ps[mb].rearrange("p (a b) -> p a b", a=4),
                                                  in1=bc_mid(bsp[:, g, :], 4), op=ALU.add), R=[("ps", mb), "bsp"], W=[("mt", f % 2)])

        def u_back(f):
            gelu_b(P, cx, uT[f % 2][:], ("uT", f % 2), tmps, f)
            P.op("dve", lambda e: e.tensor_tensor(out=gT[:, f, :], in0=mt[f % 2][:], in1=uT[f % 2][:], op=ALU.mult),
                 R=[("mt", f % 2), ("uT", f % 2)], W=[("gT", f)])

        for f in range(33):
            if f < 32:
                u_front(f)
            if f >= 1:
                u_back(f - 1)
        P.barrier()
        P.release()
        down_proj(P, cx, wout_d, 32, gT, yT, t0, NQ, "gT")
        phase_post(P, cx, xT, yT, t0, NQ, Ccol)
        P.release()

    for qt in range(4):
        do_quarter(qt)
    P.release()

PI = 3.141592653589793


def rope_tables(P, cx, pos_d, invf_col, cos_t, sin_t):
    P.mark()
    pi_ = P.sb([128, 2048], I32, "posi")
    ang = P.sb([128, 2048], F32, "ang")
    ki = pi_
    kf = P.sb([128, 2048], F32, "kf")
    r = P.sb([128, 2048], F32, "r")
    m = kf
    P.dma("sp", pi_[:], pos_d[0, :].partition_broadcast(128), W=["posi"], sk="posi")
    P.op("dve", lambda e: e.tensor_copy(out=ang[:], in_=pi_[:]), R=["posi"], W=["ang"])
    P.op("dve", lambda e: e.tensor_scalar(out=ang[:], in0=ang[:], scalar1=invf_col, scalar2=None, op0=ALU.mult), R=["ang", "vecs"], W=["ang"])
    for which, dst in ((0, sin_t), (1, cos_t)):
        sh = 0.0 if which == 0 else PI / 2
        P.op("dve", lambda e, sh=sh: e.tensor_scalar(out=ki[:], in0=ang[:], scalar1=sh, scalar2=1.0 / (2 * PI), op0=ALU.add, op1=ALU.mult),
             R=["ang", "posi"], W=["ki", "posi"])
        P.op("dve", lambda e: e.tensor_copy(out=kf[:], in_=ki[:]), R=["ki", "m"], W=["kf", "m"])
        P.op("dve", lambda e: e.scalar_tensor_tensor(out=r[:], in0=kf[:], scalar=-2 * PI, in1=ang[:], op0=ALU.mult, op1=ALU.add),
             R=["kf", "ang"], W=["r"])
        if sh != 0.0:
            P.op("dve", lambda e, sh=sh: e.tensor_scalar(out=r[:], in0=r[:], scalar1=sh, scalar2=None, op0=ALU.add), R=["r"], W=["r"])
        P.op("dve", lambda e: e.tensor_scalar(out=m[:], in0=r[:], scalar1=PI, scalar2=None, op0=ALU.is_gt), R=["r", "kf"], W=["m", "kf"])
        P.op("dve", lambda e: e.scalar_tensor_tensor(out=r[:], in0=m[:], scalar=-2 * PI, in1=r[:], op0=ALU.mult, op1=ALU.add), R=["m", "r"], W=["r"])
        P.op("dve", lambda e: e.tensor_scalar(out=m[:], in0=r[:], scalar1=-PI, scalar2=None, op0=ALU.is_lt), R=["r", "kf"], W=["m", "kf"])
        P.op("dve", lambda e: e.scalar_tensor_tensor(out=r[:], in0=m[:], scalar=2 * PI, in1=r[:], op0=ALU.mult, op1=ALU.add), R=["m", "r"], W=["r"])
        P.op("dve", lambda e: e.tensor_scalar(out=r[:], in0=r[:], scalar1=PI, scalar2=-PI, op0=ALU.min, op1=ALU.max), R=["r"], W=["r"])
        P.op("act", lambda e, dst=dst: e.activation(out=dst[:], in_=r[:], func=AF.Sin), R=["r"], W=["trig"])
    P.barrier()
    P.release()


def hybrid_attention(P, cx, hT, vecs, pos_d, win_d, yab, psw_d, ms_d, HS=9):
    wv = win_d.rearrange("(kc p) n -> p kc n", p=128)
    P.mark()
    cos_t = P.sb([128, 2048], F32, "cos")
    sin_t = P.sb([128, 2048], F32, "sin")
    rope_tables(P, cx, pos_d, vecs[:, VOFF["invf"]:VOFF["invf"] + 1], cos_t, sin_t)
    if HS == 2:
        P.release()
        return
    psw = P.sb([128, 128], F32, "psw")
    ms = [P.sb([128, 16, 128], BF16, "ms") for _ in range(2)]
    P.mark()
    msf = P.sb([128, 2048], F32, "msf")
    P.dma("sp", psw[:], psw_d, W=["psw"], sk="psw")
    for w in range(2):
        P.dma("sp", msf[:], ms_d[w], W=["msf"], sk="msf")
        P.op("dve", lambda e, w=w: e.tensor_copy(out=ms[w][:].rearrange("p a b -> p (a b)"), in_=msf[:]), R=["msf"], W=["ms"])
    P.barrier()
    P.release()
    ws = WStream(P, 2, "aw")
    q32 = P.sb([128, 2048], F32, "q32")
    t1 = P.sb([128, 2048], F32, "t1")
    t2 = P.sb([128, 2048], F32, "t2")
    qk = [P.sb([128, 16, 128], BF16, "qr"), P.sb([128, 16, 128], BF16, "kr")]
    vaug = P.sb([128, 16, 132], BF16, "vaug")
    ybh = [P.sb([128, 2048], BF16, "ybh") for _ in range(2)]
    E = [P.sb([128, 512], BF16, "E") for _ in range(2)]
    PT = [P.sb([128, 4, 128], BF16, "PT") for _ in range(4)]
    rden = P.sb([128, 8], F32, "rden")
    obf = [P.sb([128, 128], BF16, "obf") for _ in range(2)]
    ps0b = cx.ps[0].bitcast(BF16)
    P.op("dve", lambda e: e.memset(vaug[:], 1.0), W=["vaug"])
    scale = 128 ** -0.5
    it_s = 0
    nheads = 16 if HS >= 4 else 1
    wseq = [off + hh_ * 128 for hh_ in range(nheads) for off in (OQ, OKK, OV)]
    wpos = [0]
    wnext = [ws.get(wv, wseq[0], 128)]

    def next_w():
        cur = wnext[0]
        wpos[0] += 1
        if wpos[0] < len(wseq):
            wnext[0] = ws.get(wv, wseq[wpos[0]], 128)
        return cur

    for h in range(nheads):
        for which, off in ((0, OQ), (1, OKK)):
            wbf, wkey = next_w()

            def evac_q(bank, tb):
                P.op("act", lambda e: e.copy(out=q32[:, tb * 512:(tb + 1) * 512], in_=cx.ps[bank]), R=[("ps", bank)], W=[("q32", tb)])
                rbk = 2 + tb % 2
                P.op("pe", lambda e: e.matmul(cx.ps[rbk], lhsT=psw[:], rhs=q32[:, tb * 512:(tb + 1) * 512], start=True, stop=True),
                     R=[("q32", tb), "psw"], W=[("ps", rbk)])
                P.op("dve", lambda e: e.tensor_tensor(out=t2[:, tb * 512:(tb + 1) * 512], in0=cx.ps[rbk], in1=sin_t[:, tb * 512:(tb + 1) * 512],
                                                      op=ALU.mult), R=[("ps", rbk), "trig"], W=[("t2", tb)])
                P.op("dve", lambda e: e.tensor_tensor(out=t1[:, tb * 512:(tb + 1) * 512], in0=q32[:, tb * 512:(tb + 1) * 512],
                                                       in1=cos_t[:, tb * 512:(tb + 1) * 512], op=ALU.mult), R=[("q32", tb), "trig"], W=[("t1", tb)])
            proj_fm(P, cx, wbf, wkey, hT, "hT", 4, [0, 1], evac_q)
            dst = qk[which]
            P.op("dve", lambda e, dst=dst: e.tensor_tensor(out=dst[:], in0=t1[:].rearrange("p (i r) -> p r i", r=16),
                                                           in1=t2[:].rearrange("p (i r) -> p r i", r=16), op=ALU.add),
                 R=[("t1", tb) for tb in range(4)] + [("t2", tb) for tb in range(4)], W=[("qk", which)])
        wbf, wkey = next_w()
        hTc = hT[:].rearrange("p k (i r) -> p k r i", r=16)
        for cb in range(4):
            bank = cb % 2
            for j in range(4):
                c = cb * 4 + j
                for kc in range(16):
                    P.op("pe", lambda e, c=c, j=j, kc=kc, bank=bank, wbf=wbf: e.matmul(cx.ps[bank][:, j * 128:(j + 1) * 128], lhsT=hTc[:, kc, c, :],
                                                                             rhs=wbf[:, kc, :], start=(kc == 0), stop=(kc == 15)),
                         R=[wkey, "hT"], W=[("ps", bank)])
            P.op("act", lambda e, cb=cb, bank=bank: e.copy(out=vaug[:, cb * 4:(cb + 1) * 4, 0:128],
                                                          in_=cx.ps[bank].rearrange("p (a b) -> p a b", a=4)),
                 R=[("ps", bank)], W=["vaug"])
        yb = ybh[h % 2]
        ybv = yb[:].rearrange("p (i r) -> p r i", r=16)
        def att_front(rg, c, sb_, es, pt):
            P.op("pe", lambda e: e.matmul(cx.ps[sb_], lhsT=qk[1][:, c, :],
                                          rhs=qk[0][:, rg * 4:(rg + 1) * 4, :].rearrange("p a b -> p (a b)"),
                                          start=True, stop=True),
                 R=[("qk", 0), ("qk", 1)], W=[("ps", sb_)])
            P.op("act", lambda e: e.activation(out=E[es][:], in_=cx.ps[sb_], func=AF.Exp, scale=scale),
                 R=[("ps", sb_)], W=[("E", es)])
            for wsel in (1, 0):
                js = [j for j in range(4) if (1 if c > rg * 4 + j else 0) == wsel]
                if not js:
                    continue
                j0, nj = js[0], len(js)
                e0 = (rg * 4 + j0 - c) % 16
                P.op("dve", lambda e, j0=j0, nj=nj, e0=e0, wsel=wsel: e.tensor_tensor(
                    out=PT[pt][:, j0:j0 + nj, :], in0=E[es][:].rearrange("p (a b) -> p a b", a=4)[:, j0:j0 + nj, :],
                    in1=ms[wsel][:, e0:e0 + nj, :], op=ALU.mult), R=[("E", es), "ms"], W=[("PT", pt)])

        def att_back(rg, c, pt, ybv=ybv, h=h):
            for j in range(4):
                ob = 4 + j
                P.op("pe", lambda e, j=j, ob=ob: e.matmul(cx.ps[ob][:, 0:129], lhsT=PT[pt][:, j, :],
                                                          rhs=vaug[:, c, 0:129], start=(c == 0), stop=(c == 15)),
                     R=[("PT", pt), "vaug"], W=[("po", j)])
            if c != 15:
                return
            for j in range(4):
                r = rg * 4 + j
                ob = 4 + j
                o2 = j % 2
                P.op("dve", lambda e, j=j, ob=ob: e.reciprocal(out=rden[:, j:j + 1], in_=cx.ps[ob][:, 128:129]),
                     R=[("po", j)], W=[("rden", j)])
                P.op("dve", lambda e, j=j, ob=ob, o2=o2: e.tensor_scalar(out=obf[o2][:], in0=cx.ps[ob][:, 0:128],
                                                                         scalar1=rden[:, j:j + 1], scalar2=None, op0=ALU.mult),
                     R=[("po", j), ("rden", j)], W=[("obf", o2)])
                P.op("pe", lambda e, o2=o2: e.transpose(out=ps0b[:, o2 * 128:(o2 + 1) * 128], in_=obf[o2][:], identity=cx.identb[:]),
                     R=[("obf", o2), "identb"], W=[("ps", 0)])
                P.op("act", lambda e, o2=o2, r=r: e.copy(out=ybv[:, r, :], in_=ps0b[:, o2 * 128:(o2 + 1) * 128]),
                     R=[("ps", 0)], W=[("ybh", h % 2)])

        pend = []
        for rg in range(4):
            for c in range(16):
                sb_ = 2 + it_s % 2
                es = it_s % 2
                pt = it_s % 4
                it_s += 1
                att_front(rg, c, sb_, es, pt)
                pend.append((rg, c, pt))
                if len(pend) > 2:
                    att_back(*pend.pop(0))
        while pend:
            att_back(*pend.pop(0))
        P.dma("sp", yab[16 + h], yb[:], R=[("ybh", h % 2)], W=[("yab", 16 + h)], sk=("ybh", h % 2))
    P.barrier()
    P.release()


def hybrid_ssd(P, cx, hT, vecs, rows_d, win_d, yab, triu, acs_d, HS=9):
    wv = win_d.rearrange("(kc p) n -> p kc n", p=128)
    P.mark()
    rb = P.sb([128, 96], F32, "rb")
    P.dma("sp", rb[:], rows_d[0, 0:96].partition_broadcast(128), W=["rb"], sk="rb")
    negm = P.sb([128, 128], F32, "negm")
    P.op("dve", lambda e: e.tensor_scalar(out=negm[:], in0=triu[:], scalar1=1.0, scalar2=30000.0, op0=ALU.subtract, op1=ALU.mult),
         R=["triu"], W=["negm"])
    onec = P.sb([128, 1], F32, "onec")
    P.op("dve", lambda e: e.memset(onec[:], 1.0), W=["onec"])
    dt = P.sb([128, 16, 32], F32, "dt")
    acs = P.sb([128, 16, 32], F32, "acs")
    eacs = P.sb([128, 16, 32], F32, "eacs")
    cdb = P.sb([128, 16, 32], F32, "cdb")
    wsd = P.sb([128, 16, 32], F32, "wsd")
    f3 = lambda t: t[:].rearrange("p a b -> p (a b)")
    P.mark()
    onesf = P.sb([128, 128], F32, "onesf")
    P.op("dve", lambda e: e.memset(onesf[:], 1.0), W=["onesf"])
    adt = P.sb([128, 16, 32], F32, "adt")
    acsT = P.sb([32, 16, 128], F32, "acsT")
    tA = P.sb([128, 16, 32], F32, "tA")
    tB = P.sb([128, 16, 32], F32, "tB")
    ea = P.sb([128, 32], F32, "ea")
    wdt_st = P.sb([128, 16, 32], F32, "wdtst")
    wdt = P.sb([128, 16, 32], BF16, "wdt")
    P.dma("sp", wdt_st[:], wv[:, :, ODT:ODT + 32], W=["wdtst"], sk="wdtst")
    P.op("dve", lambda e: e.tensor_copy(out=wdt[:], in_=wdt_st[:]), R=["wdtst"], W=["wdt"])
    for tt in range(16):
        for kc in range(16):
            P.op("pe", lambda e, tt=tt, kc=kc: e.matmul(cx.ps[0][:, tt * 32:(tt + 1) * 32], lhsT=hT[:, kc, tt * 128:(tt + 1) * 128],
                                                        rhs=wdt[:, kc, :], start=(kc == 0), stop=(kc == 15)), R=["wdt", "hT"], W=[("ps", 0)])
    P.op("dve", lambda e: e.tensor_tensor(out=tA[:], in0=cx.ps[0].rearrange("p (a b) -> p a b", a=16), in1=bc_mid(rb[:, 0:32], 16), op=ALU.add),
         R=[("ps", 0), "rb"], W=["tA"])
    P.op("dve", lambda e: e.scalar_tensor_tensor(out=f3(tB), in0=f3(tA), scalar=-1.0, in1=f3(tA), op0=ALU.mult, op1=ALU.max), R=["tA"], W=["tB"])
    P.op("act", lambda e: e.activation(out=f3(tB), in_=f3(tB), func=AF.Exp, scale=-1.0), R=["tB"], W=["tB"])
    P.op("act", lambda e: e.activation(out=f3(tB), in_=f3(tB), func=AF.Ln, bias=onec[:], scale=1.0), R=["tB", "onec"], W=["tB"])
    P.op("dve", lambda e: e.scalar_tensor_tensor(out=f3(dt), in0=f3(tA), scalar=0.0, in1=f3(tB), op0=ALU.max, op1=ALU.add), R=["tA", "tB"], W=["dt"])
    import os
    HP = int(os.environ.get("HP", "9"))
    if HP <= 1:
        P.barrier(); P.release(); P.release(); return
    P.op("act", lambda e: e.activation(out=ea[:], in_=rb[:, 32:64], func=AF.Exp), R=["rb"], W=["ea"])
    P.op("dve", lambda e: e.scalar_tensor_tensor(out=adt[:], in0=dt[:], scalar=-1.0, in1=bc_mid(ea[:], 16), op0=ALU.mult, op1=ALU.mult),
         R=["dt", "ea"], W=["adt"])
    HQ = int(os.environ.get("HQ", "9"))
    if HQ <= 1:
        P.barrier(); P.release(); P.release(); return
    for tt in range(16):
        P.op("pe", lambda e, tt=tt: e.matmul(cx.ps[1][:, tt * 32:(tt + 1) * 32], lhsT=triu[:], rhs=adt[:, tt, :], start=True, stop=True),
             R=["triu", "adt"], W=[("ps", 1)])
    if HQ <= 2:
        P.op("dve", lambda e: e.tensor_copy(out=f3(acs), in_=cx.ps[1]), R=[("ps", 1)], W=["acs"])
        P.barrier(); P.release(); P.release(); return
    P.op("pe", lambda e: e.matmul(cx.ps[2], lhsT=onesf[:], rhs=f3(adt), start=True, stop=True), R=["onesf", "adt"], W=[("ps", 2)])
    P.op("dve", lambda e: e.tensor_copy(out=f3(acs), in_=cx.ps[1]), R=[("ps", 1)], W=["acs"])
    if HQ <= 3:
        P.barrier(); P.release(); P.release(); return
    P.op("act", lambda e: e.activation(out=f3(eacs), in_=cx.ps[1], func=AF.Exp), R=[("ps", 1)], W=["eacs"])
    if HQ <= 4:
        P.barrier(); P.release(); P.release(); return
    P.op("act", lambda e: e.activation(out=f3(cdb), in_=cx.ps[2], func=AF.Exp), R=[("ps", 2)], W=["cdb"])
    if HQ <= 5:
        P.barrier(); P.release(); P.release(); return
    P.op("dve", lambda e: e.tensor_tensor(out=f3(tA), in0=cx.ps[2], in1=f3(acs), op=ALU.subtract), R=[("ps", 2), "acs", "dt"], W=["tA"])
    P.op("act", lambda e: e.activation(out=f3(tA), in_=f3(tA), func=AF.Exp), R=["tA"], W=["tA"])
    P.op("dve", lambda e: e.tensor_tensor(out=f3(wsd), in0=f3(tA), in1=f3(dt), op=ALU.mult), R=["tA", "dt"], W=["wsd"])
    if HP <= 2:
        P.barrier(); P.release(); P.release(); return
    for tt in range(16):
        bank = 3 + tt // 4
        P.op("pe", lambda e, tt=tt, bank=bank: e.transpose(out=cx.ps[bank][0:32, (tt % 4) * 128:(tt % 4 + 1) * 128], in_=acs[:, tt, :],
                                                           identity=cx.ident[:]), R=["acs", "ident"], W=[("ps", bank)])
    for b4 in range(4):
        P.op("dve", lambda e, b4=b4: e.tensor_copy(out=acsT[:, b4 * 4:(b4 + 1) * 4, :],
                                                   in_=cx.ps[3 + b4][0:32, :].rearrange("p (a b) -> p a b", a=4)),
             R=[("ps", 3 + b4)], W=["acsT"])
    if HP <= 3:
        P.barrier(); P.release(); P.release(); return
    P.dma("sp", acs_d, acsT[:], R=["acsT"], W=["acs_d"], sk="acsT")
    P.barrier()
    P.release()
    if HS == 5:
        P.release()
        return
    ws = WStream(P, 2, "sw", nst=1)
    xpad = [P.sb([128, 516], F32, "xpad") for _ in range(2)]
    ca1 = P.sb([128, 512], F32, "ca")
    ca = [ca1, ca1]
    cvb = P.sb([128, 2048], BF16, "cvb")
    BT = P.sb([128, 2048], BF16, "BT")
    CT = P.sb([128, 2048], BF16, "CT")
    xs_tm = P.sb([128, 16, 512], BF16, "xs_tm")
    z_tm = P.sb([128, 16, 512], BF16, "z_tm")
    B_tm = P.sb([128, 16, 128], BF16, "B_tm")
    yaT = [P.sb([128, 4, 128], BF16, "yaT") for _ in range(2)]
    rowsb = P.sb([128, 8, 128], F32, "rowsb")
    dec = P.sb([128, 8, 128], BF16, "dec")
    LT = [P.sb([128, 8, 128], BF16, "LT") for _ in range(2)]
    cbs = P.sb([128, 128], F32, "cbs")
    xdt = [P.sb([128, 512], BF16, "xdt") for _ in range(2)]
    xw = [P.sb([128, 512], BF16, "xw") for _ in range(2)]
    carry = P.sb([128, 512], F32, "carry")
    prevb = P.sb([128, 512], BF16, "prevb")
    y1 = P.sb([128, 512], F32, "y1")
    y2 = P.sb([128, 512], F32, "y2")
    yg = P.sb([128, 512], BF16, "yg")
    ps7b = cx.ps[7].bitcast(BF16)
    cw = VOFF["conv_w"]
    cbo = VOFF["conv_b"]
    v3 = lambda ap, a: ap.rearrange("p (a b) -> p a b", a=a)
    ngr = 4 if HS >= 7 else 1
    sseq = []
    for g_ in range(ngr):
        sseq += [OX + g_ * 512 + j_ * 128 for j_ in range(4)] + [OX + 2048 + g_ * 128, OX + 2560 + g_ * 128] + \
                [OZ + g_ * 512 + j_ * 128 for j_ in range(4)]
    spos = [0]
    snext = [ws.get(wv, sseq[0], 128)]

    def next_ws(col):
        assert sseq[spos[0]] == col, (sseq[spos[0]], col)
        cur = snext[0]
        spos[0] += 1
        if spos[0] < len(sseq):
            snext[0] = ws.get(wv, sseq[spos[0]], 128)
        return cur

    for g in range(ngr):
        chunks = [(OX + g * 512 + j * 128, g * 4 + j, "xs", j) for j in range(4)] + \
                 [(OX + 2048 + g * 128, 16 + g, "B", 0), (OX + 2560 + g * 128, 20 + g, "C", 0)]
        for (col, ch, kind, j) in chunks:
            wbf, wkey = next_ws(col)
            dstT = {"xs": cvb, "B": BT, "C": CT}[kind]
            P.op("dve", lambda e: e.memset(xpad[0][:, 0:3], 0.0), R=[("xpad", 0)], W=[("xpad", 0)])

            def evac_c(bank, tb, ch=ch, dstT=dstT, kind=kind):
                xp = xpad[tb % 2]
                xn_ = xpad[(tb + 1) % 2]
                k0 = ("xpad", tb % 2)
                k1 = ("xpad", (tb + 1) % 2)
                P.op("act", lambda e: e.copy(out=xp[:, 3:515], in_=cx.ps[bank]), R=[("ps", bank)], W=[k0])
                if tb < 3:
                    P.op("dve", lambda e: e.tensor_copy(out=xn_[:, 0:3], in_=xp[:, 512:515]), R=[k0], W=[k1])
                P.op("dve", lambda e: e.tensor_scalar(out=ca[0][:], in0=xp[:, 0:512], scalar1=vecs[:, cw + ch:cw + ch + 1],
                                                      scalar2=vecs[:, cbo + ch:cbo + ch + 1], op0=ALU.mult, op1=ALU.add),
                     R=[k0, "vecs"], W=["ca1"])
                P.op("dve", lambda e: e.scalar_tensor_tensor(out=ca[1][:], in0=xp[:, 1:513], scalar=vecs[:, cw + 24 + ch:cw + 24 + ch + 1],
                                                              in1=ca[0][:], op0=ALU.mult, op1=ALU.add), R=[k0, "ca1", "vecs"], W=["ca1"])
                P.op("dve", lambda e: e.scalar_tensor_tensor(out=ca[0][:], in0=xp[:, 2:514], scalar=vecs[:, cw + 48 + ch:cw + 48 + ch + 1],
                                                             in1=ca[1][:], op0=ALU.mult, op1=ALU.add), R=[k0, "ca1", "vecs"], W=["ca1"])
                P.op("dve", lambda e: e.scalar_tensor_tensor(out=ca[1][:], in0=xp[:, 3:515], scalar=vecs[:, cw + 72 + ch:cw + 72 + ch + 1],
                                                              in1=ca[0][:], op0=ALU.mult, op1=ALU.add), R=[k0, "ca1", "vecs"], W=["ca1"])
                P.op("act", lambda e: e.activation(out=dstT[:, tb * 512:(tb + 1) * 512], in_=ca[1][:], func=AF.Silu),
                     R=["ca1", "xs_tm"], W=[(kind + "T", tb)])
            proj_fm(P, cx, wbf, wkey, hT, "hT", 4, [0, 1], evac_c)
            if kind in ("xs", "B"):
                for t4 in range(4):
                    for q in range(4):
                        tt = t4 * 4 + q
                        P.op("pe", lambda e, tt=tt, q=q, dstT=dstT: e.transpose(out=ps7b[:, q * 128:(q + 1) * 128], in_=dstT[:, tt * 128:(tt + 1) * 128],
                                                                               identity=cx.identb[:]), R=[(kind + "T", t4), "identb"], W=[("ps", 7)])
                    if kind == "xs":
                        P.op("dve", lambda e, t4=t4, j=j: e.tensor_copy(out=xs_tm[:, t4 * 4:(t4 + 1) * 4, j * 128:(j + 1) * 128],
                                                                        in_=v3(ps7b[:, 0:512], 4)), R=[("ps", 7)], W=["xs_tm"])
                    else:
                        P.op("dve", lambda e, t4=t4: e.tensor_copy(out=B_tm[:, t4 * 4:(t4 + 1) * 4, :], in_=v3(ps7b[:, 0:512], 4)),
                             R=[("ps", 7)], W=["B_tm"])
        for j in range(4):
            wbf, wkey = next_ws(OZ + g * 512 + j * 128)

            def evac_z(bank, tb, j=j):
                P.op("act", lambda e: e.activation(out=cvb[:, tb * 512:(tb + 1) * 512], in_=cx.ps[bank], func=AF.Silu),
                     R=[("ps", bank), "xs_tm"], W=[("zT", tb), ("xsT", tb)])
                for q in range(4):
                    P.op("pe", lambda e, q=q: e.transpose(out=ps7b[:, q * 128:(q + 1) * 128], in_=cvb[:, tb * 512 + q * 128:tb * 512 + (q + 1) * 128],
                                                          identity=cx.identb[:]), R=[("zT", tb), "identb"], W=[("ps", 7)])
                P.op("dve", lambda e: e.tensor_copy(out=z_tm[:, tb * 4:(tb + 1) * 4, j * 128:(j + 1) * 128], in_=v3(ps7b[:, 0:512], 4)),
                     R=[("ps", 7)], W=["z_tm"])
            proj_fm(P, cx, wbf, wkey, hT, "hT", 4, [0, 1], evac_z)
        g8 = slice(g * 8, (g + 1) * 8)
        BTk = [("BT", t) for t in range(4)]
        CTk = [("CT", t) for t in range(4)]
        def ssd_front(tt, g=g, g8=g8):
            tsl = slice(tt * 128, (tt + 1) * 128)
            sl = tt % 2
            P.dma("sp", rowsb[:], acs_d[g * 8:(g + 1) * 8, tt, :].partition_broadcast(128), R=["acs_d"], W=["rowsb"], sk="rowsb")
            P.op("pe", lambda e: e.matmul(cx.ps[0][:, 0:128], lhsT=BT[:, tsl], rhs=CT[:, tsl], start=True, stop=True),
                 R=BTk + CTk, W=[("ps", 0)])
            P.op("act", lambda e: e.copy(out=cbs[:], in_=cx.ps[0][:, 0:128]), R=[("ps", 0)], W=["cbs"])
            P.op("dve", lambda e: e.tensor_tensor(out=rowsb[:], in0=rowsb[:], in1=bc_last(acs[:, tt, g8], 128), op=ALU.subtract),
                 R=["rowsb", "acs"], W=["rowsb"])
            P.op("dve", lambda e: e.tensor_tensor(out=rowsb[:], in0=rowsb[:], in1=bc_mid(negm[:], 8), op=ALU.add), R=["rowsb", "negm"], W=["rowsb"])
            P.op("act", lambda e: e.activation(out=dec[:], in_=rowsb[:], func=AF.Exp), R=["rowsb"], W=["dec"])
            P.op("dve", lambda e: e.tensor_tensor(out=LT[sl][:], in0=dec[:], in1=bc_mid(cbs[:], 8), op=ALU.mult), R=["dec", "cbs"], W=[("LT", sl)])
            P.op("dve", lambda e: e.tensor_tensor(out=v3(xdt[sl][:], 8), in0=v3(xs_tm[:, tt, :], 8),
                                                  in1=bc_last(dt[:, tt, g8], 64), op=ALU.mult), R=["xs_tm", "dt"], W=[("xdt", sl)])
            P.op("dve", lambda e: e.tensor_tensor(out=v3(xw[sl][:], 8), in0=v3(xs_tm[:, tt, :], 8),
                                                  in1=bc_last(wsd[:, tt, g8], 64), op=ALU.mult), R=["xs_tm", "wsd"], W=[("xw", sl)])

        def ssd_back(tt, g=g, g8=g8):
            tsl = slice(tt * 128, (tt + 1) * 128)
            sl = tt % 2
            for hh in range(8):
                P.op("pe", lambda e, hh=hh: e.matmul(cx.ps[3][:, hh * 64:(hh + 1) * 64], lhsT=LT[sl][:, hh, :], rhs=xdt[sl][:, hh * 64:(hh + 1) * 64],
                                                     start=True, stop=True), R=[("LT", sl), ("xdt", sl)], W=[("ps", 3)])
            if tt > 0:
                P.op("pe", lambda e: e.matmul(cx.ps[4], lhsT=CT[:, tsl], rhs=prevb[:], start=True, stop=True),
                     R=CTk + ["prevb"], W=[("ps", 4)])
            if tt < 15:
                P.op("pe", lambda e: e.matmul(cx.ps[5], lhsT=B_tm[:, tt, :], rhs=xw[sl][:], start=True, stop=True),
                     R=["B_tm", ("xw", sl)], W=[("ps", 5)])
            if tt > 0:
                P.op("dve", lambda e: e.tensor_tensor(out=v3(y1[:], 8), in0=v3(cx.ps[4], 8),
                                                      in1=bc_last(eacs[:, tt, g8], 64), op=ALU.mult), R=[("ps", 4), "eacs"], W=["y1"])
                P.op("dve", lambda e: e.tensor_tensor(out=y1[:], in0=y1[:], in1=cx.ps[3], op=ALU.add), R=["y1", ("ps", 3)], W=["y1"])
            else:
                P.op("dve", lambda e: e.tensor_copy(out=y1[:], in_=cx.ps[3]), R=[("ps", 3)], W=["y1"])
            P.op("dve", lambda e: e.tensor_tensor(out=v3(y2[:], 8), in0=v3(xs_tm[:, tt, :], 8),
                                                  in1=bc_last(rb[:, 64 + g * 8:64 + (g + 1) * 8], 64), op=ALU.mult), R=["xs_tm", "rb"], W=["y2"])
            P.op("dve", lambda e: e.tensor_tensor(out=y2[:], in0=y2[:], in1=y1[:], op=ALU.add), R=["y2", "y1"], W=["y2"])
            P.op("dve", lambda e: e.tensor_tensor(out=yg[:], in0=y2[:], in1=z_tm[:, tt, :], op=ALU.mult), R=["y2", "z_tm"], W=["yg"])
            for q in range(4):
                P.op("pe", lambda e, q=q: e.transpose(out=ps7b[:, q * 128:(q + 1) * 128], in_=yg[:, q * 128:(q + 1) * 128], identity=cx.identb[:]),
                     R=["yg", "identb"], W=[("ps", 7)])
            P.op("act", lambda e: e.copy(out=yaT[sl][:], in_=v3(ps7b[:, 0:512], 4)), R=[("ps", 7)], W=[("yaT", sl)])
            P.dma("act", yab[g * 4:(g + 1) * 4, :, tsl].rearrange("j p t -> p j t"), yaT[sl][:], R=[("yaT", sl)], W=["yab"], sk=("yaT", sl))
            if tt < 15:
                if tt == 0:
                    P.op("dve", lambda e: e.tensor_copy(out=carry[:], in_=cx.ps[5]), R=[("ps", 5)], W=["carry"])
                else:
                    P.op("dve", lambda e: e.tensor_tensor(out=v3(carry[:], 8), in0=v3(carry[:], 8),
                                                          in1=bc_last(cdb[:, tt, g8], 64), op=ALU.mult), R=["carry", "cdb"], W=["carry"])
                    P.op("dve", lambda e: e.tensor_tensor(out=carry[:], in0=carry[:], in1=cx.ps[5], op=ALU.add), R=["carry", ("ps", 5)], W=["carry"])
                P.op("act", lambda e: e.copy(out=prevb[:], in_=carry[:]), R=["carry"], W=["prevb"])

        ssd_front(0)
        for tt in range(16):
            if tt + 1 < 16:
                ssd_front(tt + 1)
            ssd_back(tt)
        P.barrier()
    P.release()


def hybrid_sublayer(P, cx, xT, yT, vecs, rows_d, pos_d, win_d, wout_d, yab, triu, acs_d, psw_d, ms_d, abc):
    Acol, Bcol, Ccol = abc[:, 0:16], abc[:, 16:32], abc[:, 32:48]
    P.mark()
    hT = P.sb([128, 16, 2048], BF16, "hT")
    for half in range(2):
        phase_prenorm(P, cx, xT, half * 1024, 1024, hT[:, :, half * 1024:(half + 1) * 1024], Acol, Bcol)
    import os
    HS = int(os.environ.get("HS", "9"))
    if HS >= 2:
        hybrid_attention(P, cx, hT, vecs, pos_d, win_d, yab, psw_d, ms_d, HS)
    if HS >= 5:
        hybrid_ssd(P, cx, hT, vecs, rows_d, win_d, yab, triu, acs_d, HS)
    P.release()
    if HS < 9:
        return
    for half in range(2):
        t0 = half * 1024
        P.mark()
        cat = P.sb([128, 32, 1024], BF16, "cat")
        phase_prenorm(P, cx, yab, t0, 1024, cat[:, 0:16, :], vecs[:, VOFF["hyb_norm_g"]:VOFF["hyb_norm_g"] + 16],
                      vecs[:, VOFF["zero"]:VOFF["zero"] + 16], hkey="cat", sdt=BF16, xkey="yab")
        for hh in range(16):
            P.dma("sp", cat[:, 16 + hh, :], yab[16 + hh, :, t0:t0 + 1024], W=[("cat", 16 + hh)], sk=("cat", hh % 4))
        P.barrier()
        down_proj(P, cx, wout_d, 32, cat, yT, t0, 1024, "cat")
        phase_post(P, cx, xT, yT, t0, 1024, Ccol)
        P.release()

from concourse.bass_utils import run_bass_kernel_spmd

ROPE_THETA = 10000.0


def build_program(dbg=False, stop=99):
    nc = bass.Bass("TRN2", target_bir_lowering=False)
    P = Prog(nc)
    cx = Ctx()
    dt_in = lambda name, shape, dtype=F32: nc.dram_tensor(name, list(shape), dtype, kind="ExternalInput").ap()
    x_d = dt_in("x", [S, D])
    vec_d = dt_in("vecs", [128, NVEC])
    rows_d = dt_in("rows", [1, NROW])
    pos_d = dt_in("pos", [1, S], I32)
    wmod_d = dt_in("w_mod", [2, D, 18432])
    wg_d = dt_in("ffn_w_gate", [2, 2, D, FF])
    wu_d = dt_in("ffn_w_up", [2, 2, D, FF])
    wd_d = dt_in("ffn_w_down", [2, 2, FF, D])
    hin_d = dt_in("hyb_w_in", [D, 11296])
    hout_d = dt_in("hyb_w_out", [4096, D])
    sin_d = dt_in("sgu_w_in", [D, 8192])
    sout_d = dt_in("sgu_w_out", [4096, D])
    wsT_d = dt_in("wsT", [128, 8, 128])
    ms_d = dt_in("ms", [2, 128, 2048])
    psw_d = dt_in("psw", [128, 128])
    triu_d = dt_in("triu", [128, 128])
    out_d = nc.dram_tensor("out", [S, D], F32, kind="ExternalOutput").ap()
    xT = nc.dram_tensor("xT", [16, 128, S], F32).ap()
    yT = nc.dram_tensor("yT", [16, 128, S], F32).ap()
    yab = nc.dram_tensor("yab", [32, 128, S], BF16).ap()
    acs_d = nc.dram_tensor("acs_d", [32, 16, 128], F32).ap()
    dbgs = []
    setup_common(P, nc, cx)
    vecs = P.sb([128, NVEC], F32, "vecs")
    triu = P.sb([128, 128], F32, "triu")
    mods = P.sb([128, 288], F32, "mods")
    abc = P.sb([128, 48], F32, "abc")
    P.dma("sp", vecs[:], vec_d, W=["vecs"], sk="vecs")
    P.dma("sp", triu[:], triu_d, W=["triu"], sk="triu")
    P.barrier()
    phase_x_in(P, cx, x_d, xT)
    phase_mod(P, cx, vecs, wmod_d, mods)
    step = 0

    def snap():
        nonlocal step
        if dbg:
            d = nc.dram_tensor("dbg%d" % step, [16, 128, S], F32, kind="ExternalOutput").ap()
            P.dma("sp", d, xT, sk="dbg")
            P.barrier()
        step += 1
        return step >= stop

    done = False
    for l in range(2):
        if done:
            break
        sub_vectors(P, cx, vecs, mods, abc, l, 0, 0.5)
        ffn_sublayer(P, cx, xT, yT, wg_d[l, 0], wu_d[l, 0], wd_d[l, 0], abc[:, 0:16], abc[:, 16:32], abc[:, 32:48])
        if snap():
            break
        sub_vectors(P, cx, vecs, mods, abc, l, 1, 1.0)
        if l == 0:
            hybrid_sublayer(P, cx, xT, yT, vecs, rows_d, pos_d, hin_d, hout_d, yab, triu, acs_d, psw_d, ms_d, abc)
        else:
            sgu_sublayer(P, cx, xT, yT, vecs, rows_d, sin_d, wsT_d, sout_d, triu, abc)
        if snap():
            break
        sub_vectors(P, cx, vecs, mods, abc, l, 2, 0.5)
        ffn_sublayer(P, cx, xT, yT, wg_d[l, 1], wu_d[l, 1], wd_d[l, 1], abc[:, 0:16], abc[:, 16:32], abc[:, 32:48])
        if snap():
            break
    phase_x_out(P, cx, xT, out_d)
    P.finalize()
    return nc


def _lay(v):
    return np.ascontiguousarray(np.asarray(v, np.float32).reshape(-1, 128).T)


def _attn_masks():
    k = np.arange(128)[:, None]
    i = np.arange(128)[None, :]
    diff = i - k
    ms = np.zeros((2, 128, 16, 128), np.float32)
    for w in range(2):
        for e in range(16):
            if e == 0:
                if w == 1:
                    continue
                m = (diff >= 0).astype(np.float32) + ((diff >= 0) & (diff <= 32)) + ((diff >= 0) & (diff <= 8))
            elif e % 4 == 0:
                m = ((diff >= w) & (diff <= w + 31)).astype(np.float32) + ((diff >= w) & (diff <= w + 7))
            else:
                m = ((diff >= w) & (diff <= w + 7)).astype(np.float32)
            ms[w, :, e, :] = m
    return np.ascontiguousarray(ms.reshape(2, 128, 2048))


def make_in_maps(inp, cores):
    f = lambda k: np.asarray(inp[k], np.float32)
    invf = (ROPE_THETA ** (-(np.arange(128) % 64).astype(np.float64) / 64.0)).astype(np.float32)[:, None]
    shared_vec = [np.concatenate([_lay(f("b_mod")[l]) for l in range(2)], axis=1),
                  np.concatenate([_lay(f("norm_pre")[l, s]) for l in range(2) for s in range(3)], axis=1),
                  np.concatenate([_lay(f("norm_post")[l, s]) for l in range(2) for s in range(3)], axis=1),
                  np.concatenate([_lay(f("hyb_conv_w")[0, j]) for j in range(4)], axis=1),
                  _lay(f("hyb_conv_b")[0]), _lay(f("hyb_norm_g")[0]), _lay(f("sgu_b_in")[0]), invf, np.zeros((128, 16), np.float32)]
    rows = np.concatenate([f("hyb_dt_bias")[0], f("hyb_a_log")[0], f("hyb_d_skip")[0], f("sgu_ln_g")[0], f("sgu_ln_b")[0],
                           f("sgu_b_spatial")[0].reshape(-1)])[None, :].astype(np.float32)
    psw = np.zeros((128, 128), np.float32)
    for m in range(64):
        psw[m + 64, m] = -1.0
        psw[m, m + 64] = 1.0
    shared = dict(rows=np.ascontiguousarray(rows), w_mod=f("w_mod"), ffn_w_gate=f("ffn_w_gate"), ffn_w_up=f("ffn_w_up"),
                  ffn_w_down=f("ffn_w_down"), hyb_w_in=f("hyb_w_in")[0], hyb_w_out=f("hyb_w_out")[0], sgu_w_in=f("sgu_w_in")[0],
                  sgu_w_out=f("sgu_w_out")[0], wsT=np.ascontiguousarray(f("sgu_w_spatial")[0].transpose(2, 0, 1)),
                  ms=_attn_masks(), psw=psw, triu=np.triu(np.ones((128, 128), np.float32)), ident=np.eye(128, dtype=np.float32))
    maps = []
    for b in cores:
        vecs = np.concatenate([_lay(f("c")[b])] + shared_vec, axis=1)
        assert vecs.shape == (128, NVEC), vecs.shape
        m = dict(shared)
        m["x"] = np.ascontiguousarray(f("x")[b])
        m["vecs"] = np.ascontiguousarray(vecs)
        m["pos"] = np.ascontiguousarray(np.asarray(inp["positions"])[b:b + 1].astype(np.int32))
        maps.append(m)
    return maps


_NC = None


def kernel(**inp):
    global _NC
    if _NC is None:
        _NC = build_program()
    maps = make_in_maps(inp, list(range(8)))
    res = run_bass_kernel_spmd(_NC, maps, core_ids=list(range(8)))
    return np.stack([np.asarray(r["out"], np.float32) for r in res.results], axis=0)
```

```python
import numpy as np
import concourse.bass as bass
import concourse.mybir as mybir

F32 = mybir.dt.float32
BF16 = mybir.dt.bfloat16
I32 = mybir.dt.int32
AF = mybir.ActivationFunctionType
ALU = mybir.AluOpType
AX = mybir.AxisListType
CE = ("pe", "act", "dve", "pool")
ALLE = ("pe", "act", "dve", "pool", "sp")


class _Op:
    __slots__ = ("fn", "deps", "inc", "semval", "dma")

    def __init__(self, fn, deps, dma=None):
        self.fn = fn
        self.deps = deps
        self.inc = False
        self.semval = 0
        self.dma = dma


class Prog:
    def __init__(self, nc, n_dma_sems=80):
        self.nc = nc
        self.ops = {e: [] for e in ALLE}
        self.esem = {e: nc.alloc_semaphore("es_" + e) for e in CE}
        self.dpool = [nc.alloc_semaphore("ds%d" % i) for i in range(n_dma_sems)]
        self.dcount = {id(s): 0 for s in self.dpool}
        self.dfree = list(self.dpool)
        self.dmap = {}
        self.lastw = {}
        self.readers = {}
        self.sb_off = 16640
        self.sb_marks = []
        self.uid = 0

    def sb(self, shape, dtype, name=None):
        self.uid += 1
        t = self.nc.alloc_sbuf_tensor_at("%s_%d" % (name or "t", self.uid), list(shape), dtype, offset=self.sb_off)
        nb = int(np.prod(shape[1:])) * mybir.dt.size(dtype)
        nb = (nb + 63) // 64 * 64
        self.sb_off += nb
        assert self.sb_off <= 196608, ("SBUF overflow", self.sb_off, name)
        return t

    def mark(self):
        self.sb_marks.append(self.sb_off)

    def release(self):
        self.sb_off = self.sb_marks.pop()

    def _collect(self, R, W):
        deps = []
        for k in R:
            t = self.lastw.get(k)
            if t is not None:
                deps.append(t)
        for k in W:
            t = self.lastw.get(k)
            if t is not None:
                deps.append(t)
            r = self.readers.get(k)
            if r:
                deps.extend(r.values())
        return deps

    def _update(self, R, W, tok, who):
        for k in R:
            self.readers.setdefault(k, {})[who] = tok
        for k in W:
            self.lastw[k] = tok
            self.readers[k] = {}

    def op(self, eng, fn, R=(), W=()):
        if eng != "pe":
            pk = [k for k in R if isinstance(k, tuple) and k[0] in ("ps", "po")]
            if pk:
                R = [k for k in R if k not in pk]
                W = list(W) + pk
        deps = self._collect(R, W)
        o = _Op(fn, deps)
        self.ops[eng].append(o)
        tok = ("e", eng, len(self.ops[eng]) - 1)
        self._update(R, W, tok, eng)
        return tok

    def _dsem(self, key):
        s = self.dmap.get(key)
        if s is None:
            assert self.dfree, "out of DMA semaphores"
            s = self.dfree.pop(0)
            self.dmap[key] = s
        return s

    def dma(self, q, out, in_, R=(), W=(), sk=None, **kw):
        assert sk is not None
        s = self._dsem(sk)
        self.dcount[id(s)] += 16
        v = self.dcount[id(s)]
        deps = self._collect(R, W)
        o = _Op(lambda e: e.dma_start(out=out, in_=in_, **kw), deps, dma=(s, v))
        self.ops[q].append(o)
        tok = ("d", s, v)
        self._update(R, W, tok, ("d", id(s)))
        return tok

    def barrier(self):
        deps = []
        for e in CE:
            if self.ops[e]:
                for i in range(len(self.ops[e]) - 1, -1, -1):
                    if self.ops[e][i].dma is None and self.ops[e][i].fn is not None:
                        deps.append(("e", e, i))
                        break
        for s in self.dpool:
            if self.dcount[id(s)] > 0:
                deps.append(("d", s, self.dcount[id(s)]))
        for e in ALLE:
            self.ops[e].append(_Op(None, list(deps)))
        self.lastw = {}
        self.readers = {}
        self.dmap = {}
        self.dfree = list(self.dpool)

    def finalize(self):
        nc = self.nc
        for e in ALLE:
            for o in self.ops[e]:
                for d in o.deps:
                    if d[0] == "e":
                        self.ops[d[1]][d[2]].inc = True
        for e in CE:
            c = 0
            for o in self.ops[e]:
                if o.inc:
                    c += 1
                    o.semval = c
        ops = self.ops
        esem = self.esem
        stats = {}

        def replay(ename):
            def run(eng):
                seen = {}
                nw = 0
                for o in ops[ename]:
                    for d in o.deps:
                        if d[0] == "e":
                            if d[1] == "pe" and ename == "pe":
                                continue
                            s = esem[d[1]]
                            v = ops[d[1]][d[2]].semval
                            k = d[1]
                        else:
                            s = d[1]
                            v = d[2]
                            k = id(s)
                        if seen.get(k, 0) >= v:
                            continue
                        seen[k] = v
                        eng.wait_ge(s, v)
                        nw += 1
                    if o.fn is None:
                        continue
                    ins = o.fn(eng)
                    if o.dma is not None:
                        ins.then_inc(o.dma[0], 16)
                    elif o.inc:
                        ins.then_inc(esem[ename], 1)
                stats[ename] = (len(ops[ename]), nw)
            return run

        with nc.Block() as block:
            block.tensor(replay("pe"))
            block.scalar(replay("act"))
            block.vector(replay("dve"))
            block.gpsimd(replay("pool"))
            block.sync(replay("sp"))
        return stats

S = 2048
D = 2048
KC = 16
FF = 5632
FCN = 44
TB = 1024
EPS = 1e-6


class Ctx:
    pass


def setup_common(P, nc, cx):
    cx.psall = nc.alloc_psum_tensor("psall", [128, 4096], F32)
    cx.ps = [cx.psall[:, i * 512:(i + 1) * 512] for i in range(8)]
    cx.ident_d = nc.dram_tensor("ident", [128, 128], F32, kind="ExternalInput").ap()
    cx.ident = P.sb([128, 128], F32, "ident")
    cx.identb = P.sb([128, 128], BF16, "identb")
    cx.onesb = P.sb([128, 128], BF16, "onesb")
    cx.epsc = P.sb([128, 1], F32, "epsc")
    P.dma("sp", cx.ident[:], cx.ident_d, W=["ident"], sk="ident")
    P.op("dve", lambda e: e.tensor_copy(out=cx.identb[:], in_=cx.ident[:]), R=["ident"], W=["identb"])
    P.op("dve", lambda e: e.memset(cx.onesb[:], 1.0), W=["onesb"])
    P.op("dve", lambda e: e.memset(cx.epsc[:], EPS), W=["epsc"])
    cx.rr = 0


def psb(cx, i, n=512):
    return cx.ps[i][:, 0:n]


def phase_x_in(P, cx, x_d, xT):
    P.mark()
    xin = [P.sb([128, 2048], F32, "xin") for _ in range(2)]
    xo = [P.sb([128, 16, 128], F32, "xo") for _ in range(2)]
    xTv = xT.rearrange("c p t -> p c t")
    for tt in range(16):
        sl = tt % 2
        P.dma("sp", xin[sl][:], x_d[tt * 128:(tt + 1) * 128, :], W=[("xin", sl)], sk=("xin", sl))
        for cb in range(4):
            bank = (tt * 4 + cb) % 4
            for j in range(4):
                c = cb * 4 + j
                P.op("pe", lambda e, bank=bank, j=j, c=c, sl=sl: e.transpose(
                    out=cx.ps[bank][:, j * 128:(j + 1) * 128], in_=xin[sl][:, c * 128:(c + 1) * 128], identity=cx.ident[:]),
                    R=[("xin", sl), "ident"], W=[("ps", bank)])
            eng = "act" if cb % 2 else "dve"
            if eng == "dve":
                P.op("dve", lambda e, bank=bank, cb=cb, sl=sl: e.tensor_copy(
                    out=xo[sl][:, cb * 4:(cb + 1) * 4, :], in_=cx.ps[bank][:].rearrange("p (a b) -> p a b", a=4)),
                    R=[("ps", bank)], W=[("xo", sl, cb)])
            else:
                P.op("act", lambda e, bank=bank, cb=cb, sl=sl: e.copy(
                    out=xo[sl][:, cb * 4:(cb + 1) * 4, :], in_=cx.ps[bank][:].rearrange("p (a b) -> p a b", a=4)),
                    R=[("ps", bank)], W=[("xo", sl, cb)])
        P.dma("sp", xTv[:, :, tt * 128:(tt + 1) * 128], xo[sl][:],
              R=[("xo", sl, cb) for cb in range(4)], W=[("xT", c, tt // 8) for c in range(16)], sk=("xo", sl))
    P.barrier()
    P.release()


def phase_x_out(P, cx, xT, out_d):
    P.mark()
    xin = [P.sb([128, 16, 128], F32, "xin") for _ in range(2)]
    xo = [P.sb([128, 2048], F32, "xo") for _ in range(2)]
    xTv = xT.rearrange("c p t -> p c t")
    for tt in range(16):
        sl = tt % 2
        P.dma("sp", xin[sl][:], xTv[:, :, tt * 128:(tt + 1) * 128], R=[("xT", c, tt // 8) for c in range(16)],
              W=[("xin", sl)], sk=("xin", sl))
        for cb in range(4):
            bank = (tt * 4 + cb) % 4
            for j in range(4):
                c = cb * 4 + j
                P.op("pe", lambda e, bank=bank, j=j, c=c, sl=sl: e.transpose(
                    out=cx.ps[bank][:, j * 128:(j + 1) * 128], in_=xin[sl][:, c, :], identity=cx.ident[:]),
                    R=[("xin", sl), "ident"], W=[("ps", bank)])
            if cb % 2 == 0:
                P.op("dve", lambda e, bank=bank, cb=cb, sl=sl: e.tensor_copy(
                    out=xo[sl][:, cb * 512:(cb + 1) * 512], in_=cx.ps[bank][:]),
                    R=[("ps", bank)], W=[("xo", sl, cb)])
            else:
                P.op("act", lambda e, bank=bank, cb=cb, sl=sl: e.copy(
                    out=xo[sl][:, cb * 512:(cb + 1) * 512], in_=cx.ps[bank][:]),
                    R=[("ps", bank)], W=[("xo", sl, cb)])
        P.dma("sp", out_d[tt * 128:(tt + 1) * 128, :], xo[sl][:],
              R=[("xo", sl, cb) for cb in range(4)], W=[("out", tt)], sk=("xo", sl))
    P.barrier()
    P.release()


def rstd_from_ss(P, cx, ps_ss_keys, rstd, key, n=TB, width=D):
    nb = n // 512
    for b in range(nb):
        P.op("act", lambda e, b=b: e.activation(out=rstd[:, b * 512:(b + 1) * 512], in_=cx.ps[4 + b][:], func=AF.Ln,
                                                bias=cx.epsc[:], scale=1.0 / width),
             R=[("ps", 4 + b), "epsc"], W=[(key, b)])
        P.op("act", lambda e, b=b: e.activation(out=rstd[:, b * 512:(b + 1) * 512], in_=rstd[:, b * 512:(b + 1) * 512],
                                                func=AF.Exp, scale=-0.5),
             R=[(key, b)], W=[(key, b)])


def phase_prenorm(P, cx, xT, t0, n, hT, Acol, Bcol, hkey="hT", sdt=F32, xkey="xT"):
    P.mark()
    xs = [P.sb([128, n], sdt, "xs") for _ in range(3)]
    sq = [P.sb([128, n], BF16, "sq") for _ in range(2)]
    tmp = [P.sb([128, n], F32, "tmp") for _ in range(2)]
    rstd = P.sb([128, n], F32, "rstd")
    nb = n // 512
    half = t0 // 1024
    for c in range(16):
        sl = c % 3
        P.dma("sp", xs[sl][:], xT[c, :, t0:t0 + n], R=[(xkey, c, half)], W=[("xs", sl)], sk=("xs", sl))
        P.op("act", lambda e, sl=sl, c=c: e.activation(out=sq[c % 2][:], in_=xs[sl][:], func=AF.Square),
             R=[("xs", sl)], W=[("sq", c % 2)])
        for b in range(nb):
            P.op("pe", lambda e, b=b, c=c: e.matmul(cx.ps[4 + b][:], lhsT=cx.onesb[:], rhs=sq[c % 2][:, b * 512:(b + 1) * 512],
                                                    start=(c == 0), stop=(c == 15)),
                 R=[("sq", c % 2), "onesb"], W=[("ps", 4 + b)])
    rstd_from_ss(P, cx, None, rstd, "rstd", n)
    for c in range(16):
        sl = c % 3
        P.dma("sp", xs[sl][:], xT[c, :, t0:t0 + n], R=[(xkey, c, half)], W=[("xs", sl)], sk=("xs", sl))
        P.op("dve", lambda e, sl=sl, c=c: e.scalar_tensor_tensor(out=tmp[c % 2][:], in0=xs[sl][:], scalar=Acol[:, c:c + 1],
                                                                 in1=rstd[:], op0=ALU.mult, op1=ALU.mult),
             R=[("xs", sl), ("rstd", 0), ("rstd", 1), "vecs"], W=[("tmp", c % 2)])
        P.op("act", lambda e, c=c: e.activation(out=hT[:, c, 0:n], in_=tmp[c % 2][:], func=AF.Identity,
                                                bias=Bcol[:, c:c + 1], scale=1.0),
             R=[("tmp", c % 2), "vecs"], W=[(hkey, c)])
    P.barrier()
    P.release()


def phase_post(P, cx, xT, yT, t0, n, Ccol):
    P.mark()
    xs = [P.sb([128, n], F32, "xs") for _ in range(2)]
    ys = [P.sb([128, n], F32, "ys") for _ in range(2)]
    tm = [P.sb([128, n], F32, "tm") for _ in range(2)]
    xn = [P.sb([128, n], F32, "xn") for _ in range(2)]
    rstd = P.sb([128, n], F32, "rstd")
    half = t0 // 1024
    rstd_from_ss(P, cx, None, rstd, "rstd", n)
    for c in range(16):
        sl = c % 2
        P.dma("sp", ys[sl][:], yT[c, :, t0:t0 + n], R=[("yT", c)], W=[("ys", sl)], sk=("ys", sl))
        P.dma("sp", xs[sl][:], xT[c, :, t0:t0 + n], R=[("xT", c, half)], W=[("xs", sl)], sk=("xs", sl))
        P.op("dve", lambda e, sl=sl, c=c: e.scalar_tensor_tensor(out=tm[sl][:], in0=ys[sl][:], scalar=Ccol[:, c:c + 1],
                                                                 in1=rstd[:], op0=ALU.mult, op1=ALU.mult),
             R=[("ys", sl), ("rstd", 0), ("rstd", 1), "vecs"], W=[("tm", sl)])
        P.op("dve", lambda e, sl=sl: e.tensor_tensor(out=xn[sl][:], in0=tm[sl][:], in1=xs[sl][:], op=ALU.add),
             R=[("tm", sl), ("xs", sl)], W=[("xn", sl)])
        P.dma("act", xT[c, :, t0:t0 + n], xn[sl][:], R=[("xn", sl)], W=[("xT", c, half)], sk=("xn", sl))
    P.barrier()
    P.release()


def down_proj(P, cx, w_d, nfc, actT, yT, t0, n, akey):
    P.mark()
    nq = 4
    qs = [(nfc * q) // nq for q in range(nq + 1)]
    qmax = max(qs[q + 1] - qs[q] for q in range(nq))
    wst = [P.sb([128, qmax, 128], F32, "wdst") for _ in range(nq)]
    wbf = [P.sb([128, nfc, 128], BF16, "wdbf") for _ in range(2)]
    ysb = [P.sb([128, 512], F32, "ysb") for _ in range(2)]
    sq = [P.sb([128, 512], BF16, "sq") for _ in range(2)]
    wv = w_d.rearrange("(fc p) d -> p fc d", p=128)
    nb = n // 512
    it = 0
    def load_wd(dc):
        sl = dc % 2
        for q in range(nq):
            a, bb = qs[q], qs[q + 1]
            P.dma("sp", wst[q][:, 0:bb - a, :], wv[:, a:bb, dc * 128:(dc + 1) * 128], W=[("wdst", q)], sk=("wdst", q))
            if q % 2 == 0:
                P.op("dve", lambda e, sl=sl, a=a, bb=bb, q=q: e.tensor_copy(out=wbf[sl][:, a:bb, :], in_=wst[q][:, 0:bb - a, :]),
                     R=[("wdst", q)], W=[("wdbf", sl, q)])
            else:
                P.op("act", lambda e, sl=sl, a=a, bb=bb, q=q: e.copy(out=wbf[sl][:, a:bb, :], in_=wst[q][:, 0:bb - a, :]),
                     R=[("wdst", q)], W=[("wdbf", sl, q)])

    load_wd(0)
    for dc in range(16):
        sl = dc % 2
        if dc + 1 < 16:
            load_wd(dc + 1)
        for b in range(nb):
            bank = it % 4
            s2 = it % 2
            it += 1
            for fc in range(nfc):
                P.op("pe", lambda e, bank=bank, fc=fc, sl=sl, b=b: e.matmul(
                    cx.ps[bank][:], lhsT=wbf[sl][:, fc, :], rhs=actT[:, fc, b * 512:(b + 1) * 512],
                    start=(fc == 0), stop=(fc == nfc - 1)),
                    R=[("wdbf", sl, min(nq - 1, (fc * nq) // nfc)), (akey, fc)], W=[("ps", bank)])
            P.op("dve", lambda e, bank=bank, s2=s2: e.tensor_copy(out=ysb[s2][:], in_=cx.ps[bank][:]),
                 R=[("ps", bank)], W=[("ysb", s2)])

            P.op("act", lambda e, bank=bank, s2=s2: e.activation(out=sq[s2][:], in_=ysb[s2][:], func=AF.Square),
                 R=[("ysb", s2)], W=[("sq", s2)])
            P.op("pe", lambda e, s2=s2, b=b, dc=dc: e.matmul(cx.ps[4 + b][:], lhsT=cx.onesb[:], rhs=sq[s2][:],
                                                             start=(dc == 0), stop=(dc == 15)),
                 R=[("sq", s2), "onesb"], W=[("ps", 4 + b)])
            P.dma("sp", yT[dc, :, t0 + b * 512:t0 + (b + 1) * 512], ysb[s2][:], R=[("ysb", s2)], W=[("yT", dc)], sk=("ysb", s2))
    P.barrier()
    P.release()


def ffn_phaseA(P, cx, hT, actT, wg_d, wu_d):
    P.mark()
    wst = [[P.sb([128, 16, 128], F32, "wst") for _ in range(2)] for _ in range(2)]
    wbf = [[P.sb([128, 16, 128], BF16, "wbf") for _ in range(2)] for _ in range(2)]
    sg = [P.sb([128, 512], BF16, "sg") for _ in range(2)]
    wviews = [wg_d.rearrange("(kc p) f -> p kc f", p=128), wu_d.rearrange("(kc p) f -> p kc f", p=128)]
    it = 0

    def load_w(fc):
        sl = fc % 2
        for m in range(2):
            P.dma("sp", wst[m][sl][:], wviews[m][:, :, fc * 128:(fc + 1) * 128],
                  W=[("wst", m, sl)], sk=("wst", m, sl))
            if m == 0:
                P.op("dve", lambda e, m=m, sl=sl: e.tensor_copy(out=wbf[m][sl][:], in_=wst[m][sl][:]),
                     R=[("wst", m, sl)], W=[("wbf", m, sl)])
            else:
                P.op("act", lambda e, m=m, sl=sl: e.copy(out=wbf[m][sl][:], in_=wst[m][sl][:]),
                     R=[("wst", m, sl)], W=[("wbf", m, sl)])

    load_w(0)
    for fc in range(FCN):
        sl = fc % 2
        if fc + 1 < FCN:
            load_w(fc + 1)
        for b in range(2):
            bg = (it % 2) * 2
            bu = bg + 1
            it += 1
            for m, bank in ((0, bg), (1, bu)):
                for kc in range(16):
                    P.op("pe", lambda e, m=m, bank=bank, kc=kc, sl=sl, b=b: e.matmul(
                        cx.ps[bank][:], lhsT=wbf[m][sl][:, kc, :], rhs=hT[:, kc, b * 512:(b + 1) * 512],
                        start=(kc == 0), stop=(kc == 15)),
                        R=[("wbf", m, sl), ("hT", kc)], W=[("ps", bank)])
            s2 = it % 2
            P.op("act", lambda e, bg=bg, s2=s2: e.activation(out=sg[s2][:], in_=cx.ps[bg][:], func=AF.Silu),
                 R=[("ps", bg)], W=[("sg", s2)])
            P.op("dve", lambda e, bu=bu, s2=s2, fc=fc, b=b: e.tensor_tensor(
                out=actT[:, fc, b * 512:(b + 1) * 512], in0=sg[s2][:], in1=cx.ps[bu][:], op=ALU.mult),
                R=[("sg", s2), ("ps", bu)], W=[("actT", fc)])
    P.barrier()
    P.release()


def ffn_sublayer(P, cx, xT, yT, wg_d, wu_d, wd_d, Acol, Bcol, Ccol):
    for half in range(2):
        t0 = half * TB
        P.mark()
        actT = P.sb([128, FCN, TB], BF16, "actT")
        P.mark()
        hT = P.sb([128, 16, TB], BF16, "hT")
        phase_prenorm(P, cx, xT, t0, TB, hT, Acol, Bcol)
        ffn_phaseA(P, cx, hT, actT, wg_d, wu_d)
        P.release()
        down_proj(P, cx, wd_d, FCN, actT, yT, t0, TB, "actT")
        phase_post(P, cx, xT, yT, t0, TB, Ccol)
        P.release()

OZ, OX, ODT, OQ, OKK, OV = 0, 2048, 5120, 5152, 7200, 9248
VEC_LAYOUT = [("c", 16), ("b_mod", 288), ("norm_pre", 96), ("norm_post", 96), ("conv_w", 96), ("conv_b", 24),
              ("hyb_norm_g", 16), ("sgu_b_in", 64), ("invf", 1), ("zero", 16)]
VOFF = {}
_o = 0
for _n, _w in VEC_LAYOUT:
    VOFF[_n] = _o
    _o += _w
NVEC = _o
ROW_LAYOUT = [("dt_bias", 32), ("a_log", 32), ("d_skip", 32), ("ln_g", 4096), ("ln_b", 4096), ("b_sp", 1024)]
ROFF = {}
_o = 0
for _n, _w in ROW_LAYOUT:
    ROFF[_n] = _o
    _o += _w
NROW = _o


def bc_mid(ap, reps):
    a = ap.ap
    return bass.AP(ap.tensor, ap.offset, [list(a[0]), [0, reps], list(a[1])])


def bc_last(ap, reps):
    a = ap.ap
    return bass.AP(ap.tensor, ap.offset, [list(a[0]), list(a[1]), [0, reps]])


class WStream:
    def __init__(self, P, n=2, name="ws", width=128, engines=("dve", "act"), nst=None):
        self.engines = engines
        self.P = P
        self.n = n
        self.nst = nst or n
        self.name = name
        self.st = [P.sb([128, 16, width], F32, name + "st") for _ in range(self.nst)]
        self.bf = [P.sb([128, 16, width], BF16, name + "bf") for _ in range(n)]
        self.i = 0

    def get(self, wview, c0, ncols):
        P = self.P
        sl = self.i % self.n
        ss = self.i % self.nst
        self.i += 1
        st, bf = self.st[ss], self.bf[sl]
        P.dma("sp", st[:, :, 0:ncols], wview[:, :, c0:c0 + ncols], W=[(self.name, "st", ss)], sk=(self.name, ss))
        if self.engines[self.i % len(self.engines)] == "dve":
            P.op("dve", lambda e: e.tensor_copy(out=bf[:, :, 0:ncols], in_=st[:, :, 0:ncols]),
                 R=[(self.name, "st", ss)], W=[(self.name, "bf", sl)])
        else:
            P.op("act", lambda e: e.copy(out=bf[:, :, 0:ncols], in_=st[:, :, 0:ncols]),
                 R=[(self.name, "st", ss)], W=[(self.name, "bf", sl)])
        return bf[:, :, 0:ncols], (self.name, "bf", sl)


def proj_fm(P, cx, wbf, wkey, hT, hkey, ntb, banks, evac):
    for tb in range(ntb):
        bank = banks[cx.rr % len(banks)]
        cx.rr += 1
        for kc in range(16):
            P.op("pe", lambda e, bank=bank, kc=kc, tb=tb: e.matmul(cx.ps[bank], lhsT=wbf[:, kc, :], rhs=hT[:, kc, tb * 512:(tb + 1) * 512],
                                                                  start=(kc == 0), stop=(kc == 15)),
                 R=[wkey, hkey], W=[("ps", bank)])
        evac(bank, tb)


def phase_mod(P, cx, vecs, wmod_d, mods):
    P.mark()
    ca = P.sb([128, 16, 2], F32, "ca")
    sgm = P.sb([128, 16], F32, "sgm")
    cc = vecs[:, VOFF["c"]:VOFF["c"] + 16]
    P.op("act", lambda e: e.activation(out=sgm[:], in_=cc, func=AF.Sigmoid), R=["vecs"], W=["sgm"])
    for j in range(2):
        P.op("dve", lambda e, j=j: e.tensor_tensor(out=ca[:, :, j], in0=sgm[:], in1=cc, op=ALU.mult), R=["sgm", "vecs"], W=["ca"])
    cab = P.sb([128, 16, 2], BF16, "cab")
    P.op("dve", lambda e: e.tensor_copy(out=cab[:], in_=ca[:]), R=["ca"], W=["cab"])
    wst = [P.sb([128, 16, 128], F32, "wm") for _ in range(3)]
    wbf = [P.sb([128, 16, 128], BF16, "wmb") for _ in range(3)]
    for l in range(2):
        for j in range(144):
            it = l * 144 + j
            sl = it % 3
            P.dma("sp", wst[sl][:], wmod_d[l, j], W=[("wm", sl)], sk=("wm", sl))
            if it % 2 == 0:
                P.op("dve", lambda e, sl=sl: e.tensor_copy(out=wbf[sl][:], in_=wst[sl][:]), R=[("wm", sl)], W=[("wmb", sl)])
            else:
                P.op("act", lambda e, sl=sl: e.copy(out=wbf[sl][:], in_=wst[sl][:]), R=[("wm", sl)], W=[("wmb", sl)])
            for kc in range(16):
                P.op("pe", lambda e, l=l, j=j, kc=kc, sl=sl: e.matmul(cx.ps[l][:, 2 * j:2 * j + 2], lhsT=wbf[sl][:, kc, :], rhs=cab[:, kc, :],
                                                                     start=(kc == 0), stop=(kc == 15)),
                     R=[("wmb", sl), "cab"], W=[("ps", l)])
        P.op("dve", lambda e, l=l: e.tensor_tensor(out=mods[:, l * 144:(l + 1) * 144],
                                                   in0=cx.ps[l][:, 0:288].rearrange("p (j t) -> p j t", t=2)[:, :, 0],
                                                   in1=vecs[:, VOFF["b_mod"] + l * 144:VOFF["b_mod"] + (l + 1) * 144], op=ALU.add),
             R=[("ps", l), "vecs"], W=["mods"])
    P.barrier()
    P.release()


def sub_vectors(P, cx, vecs, mods, abc, l, s, rw):
    base = l * 144 + s * 48
    gpre = vecs[:, VOFF["norm_pre"] + (l * 3 + s) * 16:VOFF["norm_pre"] + (l * 3 + s + 1) * 16]
    gpost = vecs[:, VOFF["norm_post"] + (l * 3 + s) * 16:VOFF["norm_post"] + (l * 3 + s + 1) * 16]
    P.op("dve", lambda e: e.scalar_tensor_tensor(out=abc[:, 0:16], in0=mods[:, base + 16:base + 32], scalar=1.0, in1=gpre,
                                                 op0=ALU.add, op1=ALU.mult), R=["mods", "vecs"], W=["abc"])
    P.op("dve", lambda e: e.tensor_copy(out=abc[:, 16:32], in_=mods[:, base:base + 16]), R=["mods"], W=["abc"])
    P.op("dve", lambda e: e.scalar_tensor_tensor(out=abc[:, 32:48], in0=mods[:, base + 32:base + 48], scalar=1.0, in1=gpost,
                                                 op0=ALU.add, op1=ALU.mult), R=["mods", "vecs"], W=["abc"])
    if rw != 1.0:
        P.op("dve", lambda e: e.tensor_scalar(out=abc[:, 32:48], in0=abc[:, 32:48], scalar1=float(rw), scalar2=None, op0=ALU.mult),
             R=["abc"], W=["abc"])
    P.barrier()


def gelu_a(P, cx, bank, bias_col, tmps, idx, n=512):
    xb, t1, sg = tmps[idx % len(tmps)]
    k = ("gl", idx % len(tmps))
    P.op("act", lambda e: e.activation(out=xb[:, 0:n], in_=cx.ps[bank][:, 0:n], func=AF.Identity, bias=bias_col, scale=1.0),
         R=[("ps", bank), "vecs"], W=[k + ("xb",)])
    P.op("act", lambda e: e.activation(out=t1[:, 0:n], in_=xb[:, 0:n], func=AF.Square), R=[k + ("xb",)], W=[k + ("t1",)])
    P.op("dve", lambda e: e.tensor_scalar(out=t1[:, 0:n], in0=t1[:, 0:n], scalar1=0.044715, scalar2=1.0, op0=ALU.mult, op1=ALU.add),
         R=[k + ("t1",)], W=[k + ("t1",)])
    P.op("dve", lambda e: e.tensor_tensor(out=t1[:, 0:n], in0=t1[:, 0:n], in1=xb[:, 0:n], op=ALU.mult), R=[k + ("t1",), k + ("xb",)], W=[k + ("t1",)])


def gelu_b(P, cx, out_ap, okey, tmps, idx, n=512):
    xb, t1, sg = tmps[idx % len(tmps)]
    k = ("gl", idx % len(tmps))
    P.op("act", lambda e: e.activation(out=sg[:, 0:n], in_=t1[:, 0:n], func=AF.Sigmoid, scale=1.5957691216057308),
         R=[k + ("t1",)], W=[k + ("sg",)])
    P.op("dve", lambda e: e.tensor_tensor(out=out_ap, in0=xb[:, 0:n], in1=sg[:, 0:n], op=ALU.mult), R=[k + ("xb",), k + ("sg",)], W=[okey])


def sgu_sublayer(P, cx, xT, yT, vecs, rows_d, win_d, wsT_d, wout_d, triu, abc):
    NQ = 512
    Acol, Bcol, Ccol = abc[:, 0:16], abc[:, 16:32], abc[:, 32:48]
    wv = win_d.rearrange("(kc p) n -> p kc n", p=128)
    P.mark()
    wsT = P.sb([128, 8, 128], BF16, "wsT")
    bsp = P.sb([128, 8, 128], F32, "bsp")
    lng = P.sb([128, 4096], BF16, "lng")
    lnb = P.sb([128, 4096], BF16, "lnb")
    P.mark()
    lnf = P.sb([128, 4096], F32, "lnf")
    wsf = P.sb([128, 8, 128], F32, "wsf")
    P.dma("sp", wsf[:], wsT_d, W=["wsf"], sk="wsf")
    P.op("dve", lambda e: e.tensor_tensor(out=wsT[:], in0=wsf[:], in1=bc_mid(triu[:], 8), op=ALU.mult), R=["wsf", "triu"], W=["wsT"])
    P.dma("sp", bsp[:].rearrange("p a b -> p (a b)"), rows_d[0, ROFF["b_sp"]:ROFF["b_sp"] + 1024].partition_broadcast(128), W=["bsp"], sk="bsp")
    P.dma("sp", lnf[:], rows_d[0, ROFF["ln_g"]:ROFF["ln_g"] + 4096].partition_broadcast(128), W=["lnf"], sk="lnf")
    P.op("dve", lambda e: e.tensor_copy(out=lng[:], in_=lnf[:]), R=["lnf"], W=["lng"])
    P.dma("sp", lnf[:], rows_d[0, ROFF["ln_b"]:ROFF["ln_b"] + 4096].partition_broadcast(128), R=[], W=["lnf"], sk="lnf")
    P.op("dve", lambda e: e.tensor_copy(out=lnb[:], in_=lnf[:]), R=["lnf"], W=["lnb"])
    P.barrier()
    P.release()
    bin_off = VOFF["sgu_b_in"]
    def do_quarter(qt):
        t0 = qt * NQ
        P.mark()
        hT = P.sb([128, 16, NQ], BF16, "hT")
        gT = P.sb([128, 32, NQ], BF16, "gT")
        phase_prenorm(P, cx, xT, t0, NQ, hT, Acol, Bcol)
        P.mark()
        vtm = P.sb([128, 4, 4096], BF16, "vtm")
        ws = WStream(P, 2, "sgw", engines=("act",), nst=3)
        tmps = [(P.sb([128, 512], F32, "xb"), P.sb([128, 512], F32, "t1"), P.sb([128, 512], BF16, "sg")) for _ in range(2)]
        vT = [P.sb([128, 512], BF16, "vT") for _ in range(2)]
        ps7b = cx.ps[7].bitcast(BF16)
        nxt = [ws.get(wv, 4096, 128)]

        def v_front(f):
            wbf, wkey = nxt[0]
            if f + 1 < 32:
                nxt[0] = ws.get(wv, 4096 + (f + 1) * 128, 128)
            proj_fm(P, cx, wbf, wkey, hT, "hT", 1, [0, 1],
                    lambda bank, tb: gelu_a(P, cx, bank, vecs[:, bin_off + 32 + f:bin_off + 33 + f], tmps, f))

        def v_mid(f):
            gelu_b(P, cx, vT[f % 2][:], ("vT", f % 2), tmps, f)

        def v_back(f):
            for tt in range(4):
                P.op("pe", lambda e, tt=tt: e.transpose(out=ps7b[:, tt * 128:(tt + 1) * 128], in_=vT[f % 2][:, tt * 128:(tt + 1) * 128],
                                                        identity=cx.identb[:]), R=[("vT", f % 2), "identb"], W=[("ps", 7)])
            P.op("dve", lambda e: e.tensor_copy(out=vtm[:, :, f * 128:(f + 1) * 128],
                                                in_=ps7b[:, 0:512].rearrange("p (a b) -> p a b", a=4)),
                 R=[("ps", 7)], W=[("vtm", tt2) for tt2 in range(4)])

        for f in range(34):
            if f < 32:
                v_front(f)
            if 1 <= f <= 32:
                v_mid(f - 1)
            if f >= 2:
                v_back(f - 2)
        P.mark()
        junk = P.sb([128, 4096], BF16, "junk")
        st = P.sb([128, 16], F32, "lnst")
        for tt in range(4):
            k = ("vtm", tt)
            P.op("act", lambda e, tt=tt: e.activation(out=junk[:], in_=vtm[:, tt, :], func=AF.Identity, accum_out=st[:, 0:1]),
                 R=[k], W=["junk", "lnst"])
            P.op("act", lambda e, tt=tt: e.activation(out=junk[:], in_=vtm[:, tt, :], func=AF.Square, accum_out=st[:, 1:2]),
                 R=[k], W=["junk", "lnst"])
            P.op("dve", lambda e: e.tensor_scalar(out=st[:, 2:3], in0=st[:, 0:1], scalar1=1.0 / 4096, scalar2=None, op0=ALU.mult), R=["lnst"], W=["lnst"])
            P.op("dve", lambda e: e.tensor_tensor(out=st[:, 3:4], in0=st[:, 2:3], in1=st[:, 2:3], op=ALU.mult), R=["lnst"], W=["lnst"])
            P.op("dve", lambda e: e.scalar_tensor_tensor(out=st[:, 4:5], in0=st[:, 1:2], scalar=1.0 / 4096, in1=st[:, 3:4],
                                                         op0=ALU.mult, op1=ALU.subtract), R=["lnst"], W=["lnst"])
            P.op("act", lambda e: e.activation(out=st[:, 5:6], in_=st[:, 4:5], func=AF.Ln, bias=cx.epsc[:], scale=1.0), R=["lnst", "epsc"], W=["lnst"])
            P.op("act", lambda e: e.activation(out=st[:, 6:7], in_=st[:, 5:6], func=AF.Exp, scale=-0.5), R=["lnst"], W=["lnst"])
            P.op("dve", lambda e: e.scalar_tensor_tensor(out=st[:, 7:8], in0=st[:, 2:3], scalar=-1.0, in1=st[:, 6:7], op0=ALU.mult, op1=ALU.mult),
                 R=["lnst"], W=["lnst"])
            P.op("act", lambda e, tt=tt: e.activation(out=junk[:], in_=vtm[:, tt, :], func=AF.Identity, bias=st[:, 7:8], scale=st[:, 6:7]),
                 R=[k, "lnst", "junk"], W=["junk"])
            P.op("dve", lambda e: e.tensor_tensor(out=junk[:], in0=junk[:], in1=lng[:], op=ALU.mult), R=["junk", "lng"], W=["junk"])
            P.op("dve", lambda e, tt=tt: e.tensor_tensor(out=vtm[:, tt, :], in0=junk[:], in1=lnb[:], op=ALU.add), R=["junk", "lnb"], W=[k])
        P.release()
        uT = [P.sb([128, 512], F32, "uT") for _ in range(2)]
        mt = [P.sb([128, 512], F32, "mt") for _ in range(2)]
        nxu = [None]

        def u_front(f):
            if f == 0:
                nxu[0] = ws.get(wv, 0, 128)
            wbf, wkey = nxu[0]
            if f + 1 < 32:
                nxu[0] = ws.get(wv, (f + 1) * 128, 128)
            g = f // 4
            proj_fm(P, cx, wbf, wkey, hT, "hT", 1, [0, 1],
                    lambda bank, tb: gelu_a(P, cx, bank, vecs[:, bin_off + f:bin_off + f + 1], tmps, f))
            mb = 2 + (f % 2)
            for tt in range(4):
                P.op("pe", lambda e, tt=tt: e.matmul(cx.ps[mb][:, tt * 128:(tt + 1) * 128], lhsT=vtm[:, tt, f * 128:(f + 1) * 128],
                                                     rhs=wsT[:, g, :], start=True, stop=True),
                     R=[("vtm", tt), "wsT"], W=[("ps", mb)])
            P.op("dve", lambda e: e.tensor_tensor(out=mt[f % 2][:].rearrange("p (a b) -> p a b", a=4),
                                                  in0=cx.ps[mb].rearrange("p (a b) -> p a b", a=4),
                                                  in1=bc_mid(bsp[:, g, :], 4), op=ALU.add), R=[("ps", mb), "bsp"], W=[("mt", f % 2)])

        def u_back(f):
            gelu_b(P, cx, uT[f % 2][:], ("uT", f % 2), tmps, f)
            P.op("dve", lambda e: e.tensor_tensor(out=gT[:, f, :], in0=mt[f % 2][:], in1=uT[f % 2][:], op=ALU.mult),
                 R=[("mt", f % 2), ("uT", f % 2)], W=[("gT", f)])

        for f in range(33):
            if f < 32:
                u_front(f)
            if f >= 1:
                u_back(f - 1)
        P.barrier()
        P.release()
        down_proj(P, cx, wout_d, 32, gT, yT, t0, NQ, "gT")
        phase_post(P, cx, xT, yT, t0, NQ, Ccol)
        P.release()

    for qt in range(4):
        do_quarter(qt)
    P.release()

PI = 3.141592653589793


def rope_tables(P, cx, pos_d, invf_col, cos_t, sin_t):
    P.mark()
    pi_ = P.sb([128, 2048], I32, "posi")
    ang = P.sb([128, 2048], F32, "ang")
    ki = pi_
    kf = P.sb([128, 2048], F32, "kf")
    r = P.sb([128, 2048], F32, "r")
    m = kf
    P.dma("sp", pi_[:], pos_d[0, :].partition_broadcast(128), W=["posi"], sk="posi")
    P.op("dve", lambda e: e.tensor_copy(out=ang[:], in_=pi_[:]), R=["posi"], W=["ang"])
    P.op("dve", lambda e: e.tensor_scalar(out=ang[:], in0=ang[:], scalar1=invf_col, scalar2=None, op0=ALU.mult), R=["ang", "vecs"], W=["ang"])
    for which, dst in ((0, sin_t), (1, cos_t)):
        sh = 0.0 if which == 0 else PI / 2
        P.op("dve", lambda e, sh=sh: e.tensor_scalar(out=ki[:], in0=ang[:], scalar1=sh, scalar2=1.0 / (2 * PI), op0=ALU.add, op1=ALU.mult),
             R=["ang", "posi"], W=["ki", "posi"])
        P.op("dve", lambda e: e.tensor_copy(out=kf[:], in_=ki[:]), R=["ki", "m"], W=["kf", "m"])
        P.op("dve", lambda e: e.scalar_tensor_tensor(out=r[:], in0=kf[:], scalar=-2 * PI, in1=ang[:], op0=ALU.mult, op1=ALU.add),
             R=["kf", "ang"], W=["r"])
        if sh != 0.0:
            P.op("dve", lambda e, sh=sh: e.tensor_scalar(out=r[:], in0=r[:], scalar1=sh, scalar2=None, op0=ALU.add), R=["r"], W=["r"])
        P.op("dve", lambda e: e.tensor_scalar(out=m[:], in0=r[:], scalar1=PI, scalar2=None, op0=ALU.is_gt), R=["r", "kf"], W=["m", "kf"])
        P.op("dve", lambda e: e.scalar_tensor_tensor(out=r[:], in0=m[:], scalar=-2 * PI, in1=r[:], op0=ALU.mult, op1=ALU.add), R=["m", "r"], W=["r"])
        P.op("dve", lambda e: e.tensor_scalar(out=m[:], in0=r[:], scalar1=-PI, scalar2=None, op0=ALU.is_lt), R=["r", "kf"], W=["m", "kf"])
        P.op("dve", lambda e: e.scalar_tensor_tensor(out=r[:], in0=m[:], scalar=2 * PI, in1=r[:], op0=ALU.mult, op1=ALU.add), R=["m", "r"], W=["r"])
        P.op("dve", lambda e: e.tensor_scalar(out=r[:], in0=r[:], scalar1=PI, scalar2=-PI, op0=ALU.min, op1=ALU.max), R=["r"], W=["r"])
        P.op("act", lambda e, dst=dst: e.activation(out=dst[:], in_=r[:], func=AF.Sin), R=["r"], W=["trig"])
    P.barrier()
    P.release()


def hybrid_attention(P, cx, hT, vecs, pos_d, win_d, yab, psw_d, ms_d, HS=9):
    wv = win_d.rearrange("(kc p) n -> p kc n", p=128)
    P.mark()
    cos_t = P.sb([128, 2048], F32, "cos")
    sin_t = P.sb([128, 2048], F32, "sin")
    rope_tables(P, cx, pos_d, vecs[:, VOFF["invf"]:VOFF["invf"] + 1], cos_t, sin_t)
    if HS == 2:
        P.release()
        return
    psw = P.sb([128, 128], F32, "psw")
    ms = [P.sb([128, 16, 128], BF16, "ms") for _ in range(2)]
    P.mark()
    msf = P.sb([128, 2048], F32, "msf")
    P.dma("sp", psw[:], psw_d, W=["psw"], sk="psw")
    for w in range(2):
        P.dma("sp", msf[:], ms_d[w], W=["msf"], sk="msf")
        P.op("dve", lambda e, w=w: e.tensor_copy(out=ms[w][:].rearrange("p a b -> p (a b)"), in_=msf[:]), R=["msf"], W=["ms"])
    P.barrier()
    P.release()
    ws = WStream(P, 2, "aw")
    q32 = P.sb([128, 2048], F32, "q32")
    t1 = P.sb([128, 2048], F32, "t1")
    t2 = P.sb([128, 2048], F32, "t2")
    qk = [P.sb([128, 16, 128], BF16, "qr"), P.sb([128, 16, 128], BF16, "kr")]
    vaug = P.sb([128, 16, 132], BF16, "vaug")
    ybh = [P.sb([128, 2048], BF16, "ybh") for _ in range(2)]
    E = [P.sb([128, 512], BF16, "E") for _ in range(2)]
    PT = [P.sb([128, 4, 128], BF16, "PT") for _ in range(4)]
    rden = P.sb([128, 8], F32, "rden")
    obf = [P.sb([128, 128], BF16, "obf") for _ in range(2)]
    ps0b = cx.ps[0].bitcast(BF16)
    P.op("dve", lambda e: e.memset(vaug[:], 1.0), W=["vaug"])
    scale = 128 ** -0.5
    it_s = 0
    nheads = 16 if HS >= 4 else 1
    wseq = [off + hh_ * 128 for hh_ in range(nheads) for off in (OQ, OKK, OV)]
    wpos = [0]
    wnext = [ws.get(wv, wseq[0], 128)]

    def next_w():
        cur = wnext[0]
        wpos[0] += 1
        if wpos[0] < len(wseq):
            wnext[0] = ws.get(wv, wseq[wpos[0]], 128)
        return cur

    for h in range(nheads):
        for which, off in ((0, OQ), (1, OKK)):
            wbf, wkey = next_w()

            def evac_q(bank, tb):
                P.op("act", lambda e: e.copy(out=q32[:, tb * 512:(tb + 1) * 512], in_=cx.ps[bank]), R=[("ps", bank)], W=[("q32", tb)])
                rbk = 2 + tb % 2
                P.op("pe", lambda e: e.matmul(cx.ps[rbk], lhsT=psw[:], rhs=q32[:, tb * 512:(tb + 1) * 512], start=True, stop=True),
                     R=[("q32", tb), "psw"], W=[("ps", rbk)])
                P.op("dve", lambda e: e.tensor_tensor(out=t2[:, tb * 512:(tb + 1) * 512], in0=cx.ps[rbk], in1=sin_t[:, tb * 512:(tb + 1) * 512],
                                                      op=ALU.mult), R=[("ps", rbk), "trig"], W=[("t2", tb)])
                P.op("dve", lambda e: e.tensor_tensor(out=t1[:, tb * 512:(tb + 1) * 512], in0=q32[:, tb * 512:(tb + 1) * 512],
                                                       in1=cos_t[:, tb * 512:(tb + 1) * 512], op=ALU.mult), R=[("q32", tb), "trig"], W=[("t1", tb)])
            proj_fm(P, cx, wbf, wkey, hT, "hT", 4, [0, 1], evac_q)
            dst = qk[which]
            P.op("dve", lambda e, dst=dst: e.tensor_tensor(out=dst[:], in0=t1[:].rearrange("p (i r) -> p r i", r=16),
                                                           in1=t2[:].rearrange("p (i r) -> p r i", r=16), op=ALU.add),
                 R=[("t1", tb) for tb in range(4)] + [("t2", tb) for tb in range(4)], W=[("qk", which)])
        wbf, wkey = next_w()
        hTc = hT[:].rearrange("p k (i r) -> p k r i", r=16)
        for cb in range(4):
            bank = cb % 2
            for j in range(4):
                c = cb * 4 + j
                for kc in range(16):
                    P.op("pe", lambda e, c=c, j=j, kc=kc, bank=bank, wbf=wbf: e.matmul(cx.ps[bank][:, j * 128:(j + 1) * 128], lhsT=hTc[:, kc, c, :],
                                                                             rhs=wbf[:, kc, :], start=(kc == 0), stop=(kc == 15)),
                         R=[wkey, "hT"], W=[("ps", bank)])
            P.op("act", lambda e, cb=cb, bank=bank: e.copy(out=vaug[:, cb * 4:(cb + 1) * 4, 0:128],
                                                          in_=cx.ps[bank].rearrange("p (a b) -> p a b", a=4)),
                 R=[("ps", bank)], W=["vaug"])
        yb = ybh[h % 2]
        ybv = yb[:].rearrange("p (i r) -> p r i", r=16)
        def att_front(rg, c, sb_, es, pt):
            P.op("pe", lambda e: e.matmul(cx.ps[sb_], lhsT=qk[1][:, c, :],
                                          rhs=qk[0][:, rg * 4:(rg + 1) * 4, :].rearrange("p a b -> p (a b)"),
                                          start=True, stop=True),
                 R=[("qk", 0), ("qk", 1)], W=[("ps", sb_)])
            P.op("act", lambda e: e.activation(out=E[es][:], in_=cx.ps[sb_], func=AF.Exp, scale=scale),
                 R=[("ps", sb_)], W=[("E", es)])
            for wsel in (1, 0):
                js = [j for j in range(4) if (1 if c > rg * 4 + j else 0) == wsel]
                if not js:
                    continue
                j0, nj = js[0], len(js)
                e0 = (rg * 4 + j0 - c) % 16
                P.op("dve", lambda e, j0=j0, nj=nj, e0=e0, wsel=wsel: e.tensor_tensor(
                    out=PT[pt][:, j0:j0 + nj, :], in0=E[es][:].rearrange("p (a b) -> p a b", a=4)[:, j0:j0 + nj, :],
                    in1=ms[wsel][:, e0:e0 + nj, :], op=ALU.mult), R=[("E", es), "ms"], W=[("PT", pt)])

        def att_back(rg, c, pt, ybv=ybv, h=h):
            for j in range(4):
                ob = 4 + j
                P.op("pe", lambda e, j=j, ob=ob: e.matmul(cx.ps[ob][:, 0:129], lhsT=PT[pt][:, j, :],
                                                          rhs=vaug[:, c, 0:129], start=(c == 0), stop=(c == 15)),
                     R=[("PT", pt), "vaug"], W=[("po", j)])
            if c != 15:
                return
            for j in range(4):
                r = rg * 4 + j
                ob = 4 + j
                o2 = j % 2
                P.op("dve", lambda e, j=j, ob=ob: e.reciprocal(out=rden[:, j:j + 1], in_=cx.ps[ob][:, 128:129]),
                     R=[("po", j)], W=[("rden", j)])
                P.op("dve", lambda e, j=j, ob=ob, o2=o2: e.tensor_scalar(out=obf[o2][:], in0=cx.ps[ob][:, 0:128],
                                                                         scalar1=rden[:, j:j + 1], scalar2=None, op0=ALU.mult),
                     R=[("po", j), ("rden", j)], W=[("obf", o2)])
                P.op("pe", lambda e, o2=o2: e.transpose(out=ps0b[:, o2 * 128:(o2 + 1) * 128], in_=obf[o2][:], identity=cx.identb[:]),
                     R=[("obf", o2), "identb"], W=[("ps", 0)])
                P.op("act", lambda e, o2=o2, r=r: e.copy(out=ybv[:, r, :], in_=ps0b[:, o2 * 128:(o2 + 1) * 128]),
                     R=[("ps", 0)], W=[("ybh", h % 2)])

        pend = []
        for rg in range(4):
            for c in range(16):
                sb_ = 2 + it_s % 2
                es = it_s % 2
                pt = it_s % 4
                it_s += 1
                att_front(rg, c, sb_, es, pt)
                pend.append((rg, c, pt))
                if len(pend) > 2:
                    att_back(*pend.pop(0))
        while pend:
            att_back(*pend.pop(0))
        P.dma("sp", yab[16 + h], yb[:], R=[("ybh", h % 2)], W=[("yab", 16 + h)], sk=("ybh", h % 2))
    P.barrier()
    P.release()


def hybrid_ssd(P, cx, hT, vecs, rows_d, win_d, yab, triu, acs_d, HS=9):
    wv = win_d.rearrange("(kc p) n -> p kc n", p=128)
    P.mark()
    rb = P.sb([128, 96], F32, "rb")
    P.dma("sp", rb[:], rows_d[0, 0:96].partition_broadcast(128), W=["rb"], sk="rb")
    negm = P.sb([128, 128], F32, "negm")
    P.op("dve", lambda e: e.tensor_scalar(out=negm[:], in0=triu[:], scalar1=1.0, scalar2=30000.0, op0=ALU.subtract, op1=ALU.mult),
         R=["triu"], W=["negm"])
    onec = P.sb([128, 1], F32, "onec")
    P.op("dve", lambda e: e.memset(onec[:], 1.0), W=["onec"])
    dt = P.sb([128, 16, 32], F32, "dt")
    acs = P.sb([128, 16, 32], F32, "acs")
    eacs = P.sb([128, 16, 32], F32, "eacs")
    cdb = P.sb([128, 16, 32], F32, "cdb")
    wsd = P.sb([128, 16, 32], F32, "wsd")
    f3 = lambda t: t[:].rearrange("p a b -> p (a b)")
    P.mark()
    onesf = P.sb([128, 128], F32, "onesf")
    P.op("dve", lambda e: e.memset(onesf[:], 1.0), W=["onesf"])
    adt = P.sb([128, 16, 32], F32, "adt")
    acsT = P.sb([32, 16, 128], F32, "acsT")
    tA = P.sb([128, 16, 32], F32, "tA")
    tB = P.sb([128, 16, 32], F32, "tB")
    ea = P.sb([128, 32], F32, "ea")
    wdt_st = P.sb([128, 16, 32], F32, "wdtst")
    wdt = P.sb([128, 16, 32], BF16, "wdt")
    P.dma("sp", wdt_st[:], wv[:, :, ODT:ODT + 32], W=["wdtst"], sk="wdtst")
    P.op("dve", lambda e: e.tensor_copy(out=wdt[:], in_=wdt_st[:]), R=["wdtst"], W=["wdt"])
    for tt in range(16):
        for kc in range(16):
            P.op("pe", lambda e, tt=tt, kc=kc: e.matmul(cx.ps[0][:, tt * 32:(tt + 1) * 32], lhsT=hT[:, kc, tt * 128:(tt + 1) * 128],
                                                        rhs=wdt[:, kc, :], start=(kc == 0), stop=(kc == 15)), R=["wdt", "hT"], W=[("ps", 0)])
    P.op("dve", lambda e: e.tensor_tensor(out=tA[:], in0=cx.ps[0].rearrange("p (a b) -> p a b", a=16), in1=bc_mid(rb[:, 0:32], 16), op=ALU.add),
         R=[("ps", 0), "rb"], W=["tA"])
    P.op("dve", lambda e: e.scalar_tensor_tensor(out=f3(tB), in0=f3(tA), scalar=-1.0, in1=f3(tA), op0=ALU.mult, op1=ALU.max), R=["tA"], W=["tB"])
    P.op("act", lambda e: e.activation(out=f3(tB), in_=f3(tB), func=AF.Exp, scale=-1.0), R=["tB"], W=["tB"])
    P.op("act", lambda e: e.activation(out=f3(tB), in_=f3(tB), func=AF.Ln, bias=onec[:], scale=1.0), R=["tB", "onec"], W=["tB"])
    P.op("dve", lambda e: e.scalar_tensor_tensor(out=f3(dt), in0=f3(tA), scalar=0.0, in1=f3(tB), op0=ALU.max, op1=ALU.add), R=["tA", "tB"], W=["dt"])
    import os
    HP = int(os.environ.get("HP", "9"))
    if HP <= 1:
        P.barrier(); P.release(); P.release(); return
    P.op("act", lambda e: e.activation(out=ea[:], in_=rb[:, 32:64], func=AF.Exp), R=["rb"], W=["ea"])
    P.op("dve", lambda e: e.scalar_tensor_tensor(out=adt[:], in0=dt[:], scalar=-1.0, in1=bc_mid(ea[:], 16), op0=ALU.mult, op1=ALU.mult),
         R=["dt", "ea"], W=["adt"])
    HQ = int(os.environ.get("HQ", "9"))
    if HQ <= 1:
        P.barrier(); P.release(); P.release(); return
    for tt in range(16):
        P.op("pe", lambda e, tt=tt: e.matmul(cx.ps[1][:, tt * 32:(tt + 1) * 32], lhsT=triu[:], rhs=adt[:, tt, :], start=True, stop=True),
             R=["triu", "adt"], W=[("ps", 1)])
    if HQ <= 2:
        P.op("dve", lambda e: e.tensor_copy(out=f3(acs), in_=cx.ps[1]), R=[("ps", 1)], W=["acs"])
        P.barrier(); P.release(); P.release(); return
    P.op("pe", lambda e: e.matmul(cx.ps[2], lhsT=onesf[:], rhs=f3(adt), start=True, stop=True), R=["onesf", "adt"], W=[("ps", 2)])
    P.op("dve", lambda e: e.tensor_copy(out=f3(acs), in_=cx.ps[1]), R=[("ps", 1)], W=["acs"])
    if HQ <= 3:
        P.barrier(); P.release(); P.release(); return
    P.op("act", lambda e: e.activation(out=f3(eacs), in_=cx.ps[1], func=AF.Exp), R=[("ps", 1)], W=["eacs"])
    if HQ <= 4:
        P.barrier(); P.release(); P.release(); return
    P.op("act", lambda e: e.activation(out=f3(cdb), in_=cx.ps[2], func=AF.Exp), R=[("ps", 2)], W=["cdb"])
    if HQ <= 5:
        P.barrier(); P.release(); P.release(); return
    P.op("dve", lambda e: e.tensor_tensor(out=f3(tA), in0=cx.ps[2], in1=f3(acs), op=ALU.subtract), R=[("ps", 2), "acs", "dt"], W=["tA"])
    P.op("act", lambda e: e.activation(out=f3(tA), in_=f3(tA), func=AF.Exp), R=["tA"], W=["tA"])
    P.op("dve", lambda e: e.tensor_tensor(out=f3(wsd), in0=f3(tA), in1=f3(dt), op=ALU.mult), R=["tA", "dt"], W=["wsd"])
    if HP <= 2:
        P.barrier(); P.release(); P.release(); return
    for tt in range(16):
        bank = 3 + tt // 4
        P.op("pe", lambda e, tt=tt, bank=bank: e.transpose(out=cx.ps[bank][0:32, (tt % 4) * 128:(tt % 4 + 1) * 128], in_=acs[:, tt, :],
                                                           identity=cx.ident[:]), R=["acs", "ident"], W=[("ps", bank)])
    for b4 in range(4):
        P.op("dve", lambda e, b4=b4: e.tensor_copy(out=acsT[:, b4 * 4:(b4 + 1) * 4, :],
                                                   in_=cx.ps[3 + b4][0:32, :].rearrange("p (a b) -> p a b", a=4)),
             R=[("ps", 3 + b4)], W=["acsT"])
    if HP <= 3:
        P.barrier(); P.release(); P.release(); return
    P.dma("sp", acs_d, acsT[:], R=["acsT"], W=["acs_d"], sk="acsT")
    P.barrier()
    P.release()
    if HS == 5:
        P.release()
        return
    ws = WStream(P, 2, "sw", nst=1)
    xpad = [P.sb([128, 516], F32, "xpad") for _ in range(2)]
    ca1 = P.sb([128, 512], F32, "ca")
    ca = [ca1, ca1]
    cvb = P.sb([128, 2048], BF16, "cvb")
    BT = P.sb([128, 2048], BF16, "BT")
    CT = P.sb([128, 2048], BF16, "CT")
    xs_tm = P.sb([128, 16, 512], BF16, "xs_tm")
    z_tm = P.sb([128, 16, 512], BF16, "z_tm")
    B_tm = P.sb([128, 16, 128], BF16, "B_tm")
    yaT = [P.sb([128, 4, 128], BF16, "yaT") for _ in range(2)]
    rowsb = P.sb([128, 8, 128], F32, "rowsb")
    dec = P.sb([128, 8, 128], BF16, "dec")
    LT = [P.sb([128, 8, 128], BF16, "LT") for _ in range(2)]
    cbs = P.sb([128, 128], F32, "cbs")
    xdt = [P.sb([128, 512], BF16, "xdt") for _ in range(2)]
    xw = [P.sb([128, 512], BF16, "xw") for _ in range(2)]
    carry = P.sb([128, 512], F32, "carry")
    prevb = P.sb([128, 512], BF16, "prevb")
    y1 = P.sb([128, 512], F32, "y1")
    y2 = P.sb([128, 512], F32, "y2")
    yg = P.sb([128, 512], BF16, "yg")
    ps7b = cx.ps[7].bitcast(BF16)
    cw = VOFF["conv_w"]
    cbo = VOFF["conv_b"]
    v3 = lambda ap, a: ap.rearrange("p (a b) -> p a b", a=a)
    ngr = 4 if HS >= 7 else 1
    sseq = []
    for g_ in range(ngr):
        sseq += [OX + g_ * 512 + j_ * 128 for j_ in range(4)] + [OX + 2048 + g_ * 128, OX + 2560 + g_ * 128] + \
                [OZ + g_ * 512 + j_ * 128 for j_ in range(4)]
    spos = [0]
    snext = [ws.get(wv, sseq[0], 128)]

    def next_ws(col):
        assert sseq[spos[0]] == col, (sseq[spos[0]], col)
        cur = snext[0]
        spos[0] += 1
        if spos[0] < len(sseq):
            snext[0] = ws.get(wv, sseq[spos[0]], 128)
        return cur

    for g in range(ngr):
        chunks = [(OX + g * 512 + j * 128, g * 4 + j, "xs", j) for j in range(4)] + \
                 [(OX + 2048 + g * 128, 16 + g, "B", 0), (OX + 2560 + g * 128, 20 + g, "C", 0)]
        for (col, ch, kind, j) in chunks:
            wbf, wkey = next_ws(col)
            dstT = {"xs": cvb, "B": BT, "C": CT}[kind]
            P.op("dve", lambda e: e.memset(xpad[0][:, 0:3], 0.0), R=[("xpad", 0)], W=[("xpad", 0)])

            def evac_c(bank, tb, ch=ch, dstT=dstT, kind=kind):
                xp = xpad[tb % 2]
                xn_ = xpad[(tb + 1) % 2]
                k0 = ("xpad", tb % 2)
                k1 = ("xpad", (tb + 1) % 2)
                P.op("act", lambda e: e.copy(out=xp[:, 3:515], in_=cx.ps[bank]), R=[("ps", bank)], W=[k0])
                if tb < 3:
                    P.op("dve", lambda e: e.tensor_copy(out=xn_[:, 0:3], in_=xp[:, 512:515]), R=[k0], W=[k1])
                P.op("dve", lambda e: e.tensor_scalar(out=ca[0][:], in0=xp[:, 0:512], scalar1=vecs[:, cw + ch:cw + ch + 1],
                                                      scalar2=vecs[:, cbo + ch:cbo + ch + 1], op0=ALU.mult, op1=ALU.add),
                     R=[k0, "vecs"], W=["ca1"])
                P.op("dve", lambda e: e.scalar_tensor_tensor(out=ca[1][:], in0=xp[:, 1:513], scalar=vecs[:, cw + 24 + ch:cw + 24 + ch + 1],
                                                              in1=ca[0][:], op0=ALU.mult, op1=ALU.add), R=[k0, "ca1", "vecs"], W=["ca1"])
                P.op("dve", lambda e: e.scalar_tensor_tensor(out=ca[0][:], in0=xp[:, 2:514], scalar=vecs[:, cw + 48 + ch:cw + 48 + ch + 1],
                                                             in1=ca[1][:], op0=ALU.mult, op1=ALU.add), R=[k0, "ca1", "vecs"], W=["ca1"])
                P.op("dve", lambda e: e.scalar_tensor_tensor(out=ca[1][:], in0=xp[:, 3:515], scalar=vecs[:, cw + 72 + ch:cw + 72 + ch + 1],
                                                              in1=ca[0][:], op0=ALU.mult, op1=ALU.add), R=[k0, "ca1", "vecs"], W=["ca1"])
                P.op("act", lambda e: e.activation(out=dstT[:, tb * 512:(tb + 1) * 512], in_=ca[1][:], func=AF.Silu),
                     R=["ca1", "xs_tm"], W=[(kind + "T", tb)])
            proj_fm(P, cx, wbf, wkey, hT, "hT", 4, [0, 1], evac_c)
            if kind in ("xs", "B"):
                for t4 in range(4):
                    for q in range(4):
                        tt = t4 * 4 + q
                        P.op("pe", lambda e, tt=tt, q=q, dstT=dstT: e.transpose(out=ps7b[:, q * 128:(q + 1) * 128], in_=dstT[:, tt * 128:(tt + 1) * 128],
                                                                               identity=cx.identb[:]), R=[(kind + "T", t4), "identb"], W=[("ps", 7)])
                    if kind == "xs":
                        P.op("dve", lambda e, t4=t4, j=j: e.tensor_copy(out=xs_tm[:, t4 * 4:(t4 + 1) * 4, j * 128:(j + 1) * 128],
                                                                        in_=v3(ps7b[:, 0:512], 4)), R=[("ps", 7)], W=["xs_tm"])
                    else:
                        P.op("dve", lambda e, t4=t4: e.tensor_copy(out=B_tm[:, t4 * 4:(t4 + 1) * 4, :], in_=v3(ps7b[:, 0:512], 4)),
                             R=[("ps", 7)], W=["B_tm"])
        for j in range(4):
            wbf, wkey = next_ws(OZ + g * 512 + j * 128)

            def evac_z(bank, tb, j=j):
                P.op("act", lambda e: e.activation(out=cvb[:, tb * 512:(tb + 1) * 512], in_=cx.ps[bank], func=AF.Silu),
                     R=[("ps", bank), "xs_tm"], W=[("zT", tb), ("xsT", tb)])
                for q in range(4):
                    P.op("pe", lambda e, q=q: e.transpose(out=ps7b[:, q * 128:(q + 1) * 128], in_=cvb[:, tb * 512 + q * 128:tb * 512 + (q + 1) * 128],
                                                          identity=cx.identb[:]), R=[("zT", tb), "identb"], W=[("ps", 7)])
                P.op("dve", lambda e: e.tensor_copy(out=z_tm[:, tb * 4:(tb + 1) * 4, j * 128:(j + 1) * 128], in_=v3(ps7b[:, 0:512], 4)),
                     R=[("ps", 7)], W=["z_tm"])
            proj_fm(P, cx, wbf, wkey, hT, "hT", 4, [0, 1], evac_z)
        g8 = slice(g * 8, (g + 1) * 8)
        BTk = [("BT", t) for t in range(4)]
        CTk = [("CT", t) for t in range(4)]
        def ssd_front(tt, g=g, g8=g8):
            tsl = slice(tt * 128, (tt + 1) * 128)
            sl = tt % 2
            P.dma("sp", rowsb[:], acs_d[g * 8:(g + 1) * 8, tt, :].partition_broadcast(128), R=["acs_d"], W=["rowsb"], sk="rowsb")
            P.op("pe", lambda e: e.matmul(cx.ps[0][:, 0:128], lhsT=BT[:, tsl], rhs=CT[:, tsl], start=True, stop=True),
                 R=BTk + CTk, W=[("ps", 0)])
            P.op("act", lambda e: e.copy(out=cbs[:], in_=cx.ps[0][:, 0:128]), R=[("ps", 0)], W=["cbs"])
            P.op("dve", lambda e: e.tensor_tensor(out=rowsb[:], in0=rowsb[:], in1=bc_last(acs[:, tt, g8], 128), op=ALU.subtract),
                 R=["rowsb", "acs"], W=["rowsb"])
            P.op("dve", lambda e: e.tensor_tensor(out=rowsb[:], in0=rowsb[:], in1=bc_mid(negm[:], 8), op=ALU.add), R=["rowsb", "negm"], W=["rowsb"])
            P.op("act", lambda e: e.activation(out=dec[:], in_=rowsb[:], func=AF.Exp), R=["rowsb"], W=["dec"])
            P.op("dve", lambda e: e.tensor_tensor(out=LT[sl][:], in0=dec[:], in1=bc_mid(cbs[:], 8), op=ALU.mult), R=["dec", "cbs"], W=[("LT", sl)])
            P.op("dve", lambda e: e.tensor_tensor(out=v3(xdt[sl][:], 8), in0=v3(xs_tm[:, tt, :], 8),
                                                  in1=bc_last(dt[:, tt, g8], 64), op=ALU.mult), R=["xs_tm", "dt"], W=[("xdt", sl)])
            P.op("dve", lambda e: e.tensor_tensor(out=v3(xw[sl][:], 8), in0=v3(xs_tm[:, tt, :], 8),
                                                  in1=bc_last(wsd[:, tt, g8], 64), op=ALU.mult), R=["xs_tm", "wsd"], W=[("xw", sl)])

        def ssd_back(tt, g=g, g8=g8):
            tsl = slice(tt * 128, (tt + 1) * 128)
            sl = tt % 2
            for hh in range(8):
                P.op("pe", lambda e, hh=hh: e.matmul(cx.ps[3][:, hh * 64:(hh + 1) * 64], lhsT=LT[sl][:, hh, :], rhs=xdt[sl][:, hh * 64:(hh + 1) * 64],
                                                     start=True, stop=True), R=[("LT", sl), ("xdt", sl)], W=[("ps", 3)])
            if tt > 0:
                P.op("pe", lambda e: e.matmul(cx.ps[4], lhsT=CT[:, tsl], rhs=prevb[:], start=True, stop=True),
                     R=CTk + ["prevb"], W=[("ps", 4)])
            if tt < 15:
                P.op("pe", lambda e: e.matmul(cx.ps[5], lhsT=B_tm[:, tt, :], rhs=xw[sl][:], start=True, stop=True),
                     R=["B_tm", ("xw", sl)], W=[("ps", 5)])
            if tt > 0:
                P.op("dve", lambda e: e.tensor_tensor(out=v3(y1[:], 8), in0=v3(cx.ps[4], 8),
                                                      in1=bc_last(eacs[:, tt, g8], 64), op=ALU.mult), R=[("ps", 4), "eacs"], W=["y1"])
                P.op("dve", lambda e: e.tensor_tensor(out=y1[:], in0=y1[:], in1=cx.ps[3], op=ALU.add), R=["y1", ("ps", 3)], W=["y1"])
            else:
                P.op("dve", lambda e: e.tensor_copy(out=y1[:], in_=cx.ps[3]), R=[("ps", 3)], W=["y1"])
            P.op("dve", lambda e: e.tensor_tensor(out=v3(y2[:], 8), in0=v3(xs_tm[:, tt, :], 8),
                                                  in1=bc_last(rb[:, 64 + g * 8:64 + (g + 1) * 8], 64), op=ALU.mult), R=["xs_tm", "rb"], W=["y2"])
            P.op("dve", lambda e: e.tensor_tensor(out=y2[:], in0=y2[:], in1=y1[:], op=ALU.add), R=["y2", "y1"], W=["y2"])
            P.op("dve", lambda e: e.tensor_tensor(out=yg[:], in0=y2[:], in1=z_tm[:, tt, :], op=ALU.mult), R=["y2", "z_tm"], W=["yg"])
            for q in range(4):
                P.op("pe", lambda e, q=q: e.transpose(out=ps7b[:, q * 128:(q + 1) * 128], in_=yg[:, q * 128:(q + 1) * 128], identity=cx.identb[:]),
                     R=["yg", "identb"], W=[("ps", 7)])
            P.op("act", lambda e: e.copy(out=yaT[sl][:], in_=v3(ps7b[:, 0:512], 4)), R=[("ps", 7)], W=[("yaT", sl)])
            P.dma("act", yab[g * 4:(g + 1) * 4, :, tsl].rearrange("j p t -> p j t"), yaT[sl][:], R=[("yaT", sl)], W=["yab"], sk=("yaT", sl))
            if tt < 15:
                if tt == 0:
                    P.op("dve", lambda e: e.tensor_copy(out=carry[:], in_=cx.ps[5]), R=[("ps", 5)], W=["carry"])
                else:
                    P.op("dve", lambda e: e.tensor_tensor(out=v3(carry[:], 8), in0=v3(carry[:], 8),
                                                          in1=bc_last(cdb[:, tt, g8], 64), op=ALU.mult), R=["carry", "cdb"], W=["carry"])
                    P.op("dve", lambda e: e.tensor_tensor(out=carry[:], in0=carry[:], in1=cx.ps[5], op=ALU.add), R=["carry", ("ps", 5)], W=["carry"])
                P.op("act", lambda e: e.copy(out=prevb[:], in_=carry[:]), R=["carry"], W=["prevb"])

        ssd_front(0)
        for tt in range(16):
            if tt + 1 < 16:
                ssd_front(tt + 1)
            ssd_back(tt)
        P.barrier()
    P.release()


def hybrid_sublayer(P, cx, xT, yT, vecs, rows_d, pos_d, win_d, wout_d, yab, triu, acs_d, psw_d, ms_d, abc):
    Acol, Bcol, Ccol = abc[:, 0:16], abc[:, 16:32], abc[:, 32:48]
    P.mark()
    hT = P.sb([128, 16, 2048], BF16, "hT")
    for half in range(2):
        phase_prenorm(P, cx, xT, half * 1024, 1024, hT[:, :, half * 1024:(half + 1) * 1024], Acol, Bcol)
    import os
    HS = int(os.environ.get("HS", "9"))
    if HS >= 2:
        hybrid_attention(P, cx, hT, vecs, pos_d, win_d, yab, psw_d, ms_d, HS)
    if HS >= 5:
        hybrid_ssd(P, cx, hT, vecs, rows_d, win_d, yab, triu, acs_d, HS)
    P.release()
    if HS < 9:
        return
    for half in range(2):
        t0 = half * 1024
        P.mark()
        cat = P.sb([128, 32, 1024], BF16, "cat")
        phase_prenorm(P, cx, yab, t0, 1024, cat[:, 0:16, :], vecs[:, VOFF["hyb_norm_g"]:VOFF["hyb_norm_g"] + 16],
                      vecs[:, VOFF["zero"]:VOFF["zero"] + 16], hkey="cat", sdt=BF16, xkey="yab")
        for hh in range(16):
            P.dma("sp", cat[:, 16 + hh, :], yab[16 + hh, :, t0:t0 + 1024], W=[("cat", 16 + hh)], sk=("cat", hh % 4))
        P.barrier()
        down_proj(P, cx, wout_d, 32, cat, yT, t0, 1024, "cat")
        phase_post(P, cx, xT, yT, t0, 1024, Ccol)
        P.release()

from concourse.bass_utils import run_bass_kernel_spmd

ROPE_THETA = 10000.0


def build_program(dbg=False, stop=99):
    nc = bass.Bass("TRN2", target_bir_lowering=False)
    P = Prog(nc)
    cx = Ctx()
    dt_in = lambda name, shape, dtype=F32: nc.dram_tensor(name, list(shape), dtype, kind="ExternalInput").ap()
    x_d = dt_in("x", [S, D])
    vec_d = dt_in("vecs", [128, NVEC])
    rows_d = dt_in("rows", [1, NROW])
    pos_d = dt_in("pos", [1, S], I32)
    wmod_d = dt_in("w_mod", [2, 144, 128, 16, 128])
    wg_d = dt_in("ffn_w_gate", [2, 2, D, FF])
    wu_d = dt_in("ffn_w_up", [2, 2, D, FF])
    wd_d = dt_in("ffn_w_down", [2, 2, FF, D])
    hin_d = dt_in("hyb_w_in", [D, 11296])
    hout_d = dt_in("hyb_w_out", [4096, D])
    sin_d = dt_in("sgu_w_in", [D, 8192])
    sout_d = dt_in("sgu_w_out", [4096, D])
    wsT_d = dt_in("wsT", [128, 8, 128])
    ms_d = dt_in("ms", [2, 128, 2048])
    psw_d = dt_in("psw", [128, 128])
    triu_d = dt_in("triu", [128, 128])
    out_d = nc.dram_tensor("out", [S, D], F32, kind="ExternalOutput").ap()
    xT = nc.dram_tensor("xT", [16, 128, S], F32).ap()
    yT = nc.dram_tensor("yT", [16, 128, S], F32).ap()
    yab = nc.dram_tensor("yab", [32, 128, S], BF16).ap()
    acs_d = nc.dram_tensor("acs_d", [32, 16, 128], F32).ap()
    dbgs = []
    setup_common(P, nc, cx)
    vecs = P.sb([128, NVEC], F32, "vecs")
    triu = P.sb([128, 128], F32, "triu")
    mods = P.sb([128, 288], F32, "mods")
    abc = P.sb([128, 48], F32, "abc")
    P.dma("sp", vecs[:], vec_d, W=["vecs"], sk="vecs")
    P.dma("sp", triu[:], triu_d, W=["triu"], sk="triu")
    P.barrier()
    phase_x_in(P, cx, x_d, xT)
    phase_mod(P, cx, vecs, wmod_d, mods)
    step = 0

    def snap():
        nonlocal step
        if dbg:
            d = nc.dram_tensor("dbg%d" % step, [16, 128, S], F32, kind="ExternalOutput").ap()
            P.dma("sp", d, xT, sk="dbg")
            P.barrier()
        step += 1
        return step >= stop

    done = False
    for l in range(2):
        if done:
            break
        sub_vectors(P, cx, vecs, mods, abc, l, 0, 0.5)
        ffn_sublayer(P, cx, xT, yT, wg_d[l, 0], wu_d[l, 0], wd_d[l, 0], abc[:, 0:16], abc[:, 16:32], abc[:, 32:48])
        if snap():
            break
        sub_vectors(P, cx, vecs, mods, abc, l, 1, 1.0)
        if l == 0:
            hybrid_sublayer(P, cx, xT, yT, vecs, rows_d, pos_d, hin_d, hout_d, yab, triu, acs_d, psw_d, ms_d, abc)
        else:
            sgu_sublayer(P, cx, xT, yT, vecs, rows_d, sin_d, wsT_d, sout_d, triu, abc)
        if snap():
            break
        sub_vectors(P, cx, vecs, mods, abc, l, 2, 0.5)
        ffn_sublayer(P, cx, xT, yT, wg_d[l, 1], wu_d[l, 1], wd_d[l, 1], abc[:, 0:16], abc[:, 16:32], abc[:, 32:48])
        if snap():
            break
    phase_x_out(P, cx, xT, out_d)
    P.finalize()
    return nc


def _lay(v):
    return np.ascontiguousarray(np.asarray(v, np.float32).reshape(-1, 128).T)


def _attn_masks():
    k = np.arange(128)[:, None]
    i = np.arange(128)[None, :]
    diff = i - k
    ms = np.zeros((2, 128, 16, 128), np.float32)
    for w in range(2):
        for e in range(16):
            if e == 0:
                if w == 1:
                    continue
                m = (diff >= 0).astype(np.float32) + ((diff >= 0) & (diff <= 32)) + ((diff >= 0) & (diff <= 8))
            elif e % 4 == 0:
                m = ((diff >= w) & (diff <= w + 31)).astype(np.float32) + ((diff >= w) & (diff <= w + 7))
            else:
                m = ((diff >= w) & (diff <= w + 7)).astype(np.float32)
            ms[w, :, e, :] = m
    return np.ascontiguousarray(ms.reshape(2, 128, 2048))


def make_in_maps(inp, cores):
    f = lambda k: np.asarray(inp[k], np.float32)
    invf = (ROPE_THETA ** (-(np.arange(128) % 64).astype(np.float64) / 64.0)).astype(np.float32)[:, None]
    shared_vec = [np.concatenate([_lay(f("b_mod")[l]) for l in range(2)], axis=1),
                  np.concatenate([_lay(f("norm_pre")[l, s]) for l in range(2) for s in range(3)], axis=1),
                  np.concatenate([_lay(f("norm_post")[l, s]) for l in range(2) for s in range(3)], axis=1),
                  np.concatenate([_lay(f("hyb_conv_w")[0, j]) for j in range(4)], axis=1),
                  _lay(f("hyb_conv_b")[0]), _lay(f("hyb_norm_g")[0]), _lay(f("sgu_b_in")[0]), invf, np.zeros((128, 16), np.float32)]
    rows = np.concatenate([f("hyb_dt_bias")[0], f("hyb_a_log")[0], f("hyb_d_skip")[0], f("sgu_ln_g")[0], f("sgu_ln_b")[0],
                           f("sgu_b_spatial")[0].reshape(-1)])[None, :].astype(np.float32)
    psw = np.zeros((128, 128), np.float32)
    for m in range(64):
        psw[m + 64, m] = -1.0
        psw[m, m + 64] = 1.0
    wm = f("w_mod")
    wm_t = np.ascontiguousarray(wm.reshape(2, 16, 128, 144, 128).transpose(0, 3, 2, 1, 4))
    shared = dict(rows=np.ascontiguousarray(rows), w_mod=wm_t, ffn_w_gate=f("ffn_w_gate"), ffn_w_up=f("ffn_w_up"),
                  ffn_w_down=f("ffn_w_down"), hyb_w_in=f("hyb_w_in")[0], hyb_w_out=f("hyb_w_out")[0], sgu_w_in=f("sgu_w_in")[0],
                  sgu_w_out=f("sgu_w_out")[0], wsT=np.ascontiguousarray(f("sgu_w_spatial")[0].transpose(2, 0, 1)),
                  ms=_attn_masks(), psw=psw, triu=np.triu(np.ones((128, 128), np.float32)), ident=np.eye(128, dtype=np.float32))
    maps = []
    for b in cores:
        vecs = np.concatenate([_lay(f("c")[b])] + shared_vec, axis=1)
        assert vecs.shape == (128, NVEC), vecs.shape
        m = dict(shared)
        m["x"] = np.ascontiguousarray(f("x")[b])
        m["vecs"] = np.ascontiguousarray(vecs)
        m["pos"] = np.ascontiguousarray(np.asarray(inp["positions"])[b:b + 1].astype(np.int32))
        maps.append(m)
    return maps


_NC = None


def kernel(**inp):
    global _NC
    if _NC is None:
        _NC = build_program()
    maps = make_in_maps(inp, list(range(8)))
    res = run_bass_kernel_spmd(_NC, maps, core_ids=list(range(8)))
    return np.stack([np.asarray(r["out"], np.float32) for r in res.results], axis=0)
```
